# Optimizing a Trainium2 kernel written in Bass

```python
import jax
import jax.numpy as jnp
from jax import lax
import numpy as np

D_MODEL = 1024
BATCH = 4
SEQ = 4096
DEPTH = 2

GRID_W = 64
CTX_LEN = 256
HEAD_DIM = 64
NA_HEADS = 8
NA_WIN_R = 8
NA_WIN_C = 16
NA_QBLK_C = 16
NA_KBAND_C = 32
FN_GROUPS = 8
FN_GROUP_DIM = 64
WA_HEADS = 8
WA_KV_HEADS = 2
WA_WINDOW = 128
WA_BLOCK = 128
D_FF = 2816
N_BRANCH = 3
N_MOD = 9
ROPE_BASE = 10000.0
EPS = 1e-6
NEG_INF = -1e30

NA_W = NA_HEADS * HEAD_DIM
FN_W = FN_GROUPS * FN_GROUP_DIM
WA_QW = WA_HEADS * HEAD_DIM
WA_KVW = WA_KV_HEADS * HEAD_DIM
BRANCH_W = NA_W
O_NA_K = NA_W
O_NA_V = 2 * NA_W
O_FN = 3 * NA_W
O_WA_Q = O_FN + FN_W
O_WA_K = O_WA_Q + WA_QW
O_WA_V = O_WA_K + WA_KVW
O_GATE = O_WA_V + WA_KVW
P_IN = O_GATE + N_BRANCH * D_MODEL

kernel_name = "hybrid_natten_fnet_swa_prefix_dit"


def rmsnorm(x, g):
    xf = x.astype(jnp.float32)
    y = xf * lax.rsqrt(jnp.mean(xf * xf, axis=-1, keepdims=True) + EPS)
    return (y * g.astype(jnp.float32)).astype(x.dtype)


def modulate(x, shift, scale):
    return x * (1 + scale) + shift


def swiglu(u, w13, w2):
    a, b = jnp.split(u @ w13, 2, axis=-1)
    return (jax.nn.silu(a) * b) @ w2


def ffn_half_step(h, g, shift, scale, gate, w13, w2):
    return h + 0.5 * gate * swiglu(modulate(rmsnorm(h, g), shift, scale), w13, w2)


def heads(t, n):
    return t.reshape(*t.shape[:-1], n, HEAD_DIM)


def axial_rope(n_tok):
    t = jnp.arange(n_tok)
    row = (t // GRID_W).astype(jnp.float32)
    col = (t % GRID_W).astype(jnp.float32)
    n_freq = HEAD_DIM // 4
    inv = ROPE_BASE ** (-jnp.arange(n_freq, dtype=jnp.float32) / n_freq)
    ang = jnp.concatenate([row[:, None] * inv, col[:, None] * inv], axis=-1)
    return jnp.cos(ang), jnp.sin(ang)


def apply_rope(t, cos, sin):
    tf = t.astype(jnp.float32)
    t1, t2 = jnp.split(tf, 2, axis=-1)
    c = cos[:, None, :]
    s = sin[:, None, :]
    return jnp.concatenate([t1 * c - t2 * s, t1 * s + t2 * c], axis=-1).astype(t.dtype)


def split_projection(p):
    return jnp.split(p, [O_NA_K, O_NA_V, O_FN, O_WA_Q, O_WA_K, O_WA_V, O_GATE], axis=-1)


def neighbourhood_attention(q, k, v, kc, vc, bias_tab, rows):
    B, S, H, Dh = q.shape
    wr = min(NA_WIN_R, rows)
    n_cb = GRID_W // NA_QBLK_C
    r = jnp.arange(rows)
    row_start = jnp.clip(r - wr // 2, 0, rows - wr)
    key_rows = row_start[:, None] + jnp.arange(wr)
    cb = jnp.arange(n_cb)
    band_start = jnp.clip(cb * NA_QBLK_C - NA_WIN_C // 2, 0, GRID_W - NA_KBAND_C)
    key_cols = band_start[:, None] + jnp.arange(NA_KBAND_C)
    q_cols = cb[:, None] * NA_QBLK_C + jnp.arange(NA_QBLK_C)
    win_start = jnp.clip(q_cols - NA_WIN_C // 2, 0, GRID_W - NA_WIN_C)
    kcb = key_cols[:, None, :]
    in_win = (kcb >= win_start[..., None]) & (kcb < win_start[..., None] + NA_WIN_C)
    dr_idx = key_rows - r[:, None] + NA_WIN_R - 1
    dc_idx = jnp.clip(kcb - q_cols[..., None], -(NA_WIN_C - 1), NA_WIN_C - 1) + NA_WIN_C - 1
    bias = bias_tab[:, dr_idx[:, None, None, :, None], dc_idx[None, :, :, None, :]]
    bias = bias.astype(jnp.float32)

    kg = k.reshape(B, rows, GRID_W, H, Dh)
    vg = v.reshape(B, rows, GRID_W, H, Dh)
    ridx = key_rows[:, None, :, None]
    cidx = key_cols[None, :, None, :]
    kb = kg[:, ridx, cidx]
    vb = vg[:, ridx, cidx]
    qg = q.reshape(B, rows, n_cb, NA_QBLK_C, H, Dh)
    scale = Dh ** -0.5
    s_loc = jnp.einsum('brcqhd,brcwkhd->bhrcqwk', qg, kb).astype(jnp.float32) * scale + bias[None]
    s_loc = jnp.where(in_win[:, :, None, :], s_loc, NEG_INF)
    s_loc = s_loc.reshape(*s_loc.shape[:5], wr * NA_KBAND_C)
    s_ctx = jnp.einsum('brcqhd,blhd->bhrcql', qg, kc).astype(jnp.float32) * scale
    n_loc = wr * NA_KBAND_C
    p = jax.nn.softmax(jnp.concatenate([s_loc, s_ctx], axis=-1), axis=-1).astype(v.dtype)
    p_loc = p[..., :n_loc].reshape(*p.shape[:5], wr, NA_KBAND_C)
    p_ctx = p[..., n_loc:]
    o = jnp.einsum('bhrcqwk,brcwkhd->brcqhd', p_loc, vb) + jnp.einsum('bhrcql,blhd->brcqhd', p_ctx, vc)
    return o.reshape(B, S, H * Dh)


def window_gqa_attention(q, k, v, kc, vc, sink):
    B, S, Hq, Dh = q.shape
    Hkv = k.shape[2]
    G = Hq // Hkv
    nb = S // WA_BLOCK
    qb = q.reshape(B, nb, WA_BLOCK, Hkv, G, Dh)
    pad = ((0, 0), (WA_BLOCK, WA_BLOCK), (0, 0), (0, 0))
    kp = jnp.pad(k, pad)
    vp = jnp.pad(v, pad)
    idx = jnp.arange(nb)[:, None] * WA_BLOCK + jnp.arange(3 * WA_BLOCK)
    kb = kp[:, idx]
    vb = vp[:, idx]
    qpos = jnp.arange(S).reshape(nb, WA_BLOCK)
    kpos = (idx - WA_BLOCK)[:, None, :]
    valid = (jnp.abs(kpos - qpos[..., None]) <= WA_WINDOW) & (kpos >= 0) & (kpos < S)
    scale = Dh ** -0.5
    s_loc = jnp.einsum('bnqkgd,bnjkd->bkgnqj', qb, kb).astype(jnp.float32) * scale
    s_loc = jnp.where(valid, s_loc, NEG_INF)
    s_ctx = jnp.einsum('bnqkgd,blkd->bkgnql', qb, kc).astype(jnp.float32) * scale
    s_sink = jnp.broadcast_to(sink.astype(jnp.float32).reshape(1, Hkv, G, 1, 1, 1), s_loc.shape[:-1] + (1,))
    p = jax.nn.softmax(jnp.concatenate([s_loc, s_ctx, s_sink], axis=-1), axis=-1).astype(v.dtype)
    n_loc = 3 * WA_BLOCK
    n_ctx = kc.shape[1]
    o = (jnp.einsum('bkgnqj,bnjkd->bnqkgd', p[..., :n_loc], vb)
         + jnp.einsum('bkgnql,blkd->bnqkgd', p[..., n_loc:n_loc + n_ctx], vc))
    return o.reshape(B, S, Hq * Dh)


def context_attention(q, k, v, sink=None):
    B, L, Hq, Dh = q.shape
    Hkv = k.shape[2]
    G = Hq // Hkv
    qg = q.reshape(B, L, Hkv, G, Dh)
    s = jnp.einsum('bqkgd,bjkd->bkgqj', qg, k).astype(jnp.float32) * (Dh ** -0.5)
    if sink is None:
        p = jax.nn.softmax(s, axis=-1)
    else:
        s_sink = jnp.broadcast_to(sink.astype(jnp.float32).reshape(1, Hkv, G, 1, 1), s.shape[:-1] + (1,))
        p = jax.nn.softmax(jnp.concatenate([s, s_sink], axis=-1), axis=-1)[..., :L]
    o = jnp.einsum('bkgqj,bjkd->bqkgd', p.astype(v.dtype), v)
    return o.reshape(B, L, Hq * Dh)


def fourier_mix(u):
    B, N, _ = u.shape
    ug = u.reshape(B, N, FN_GROUPS, FN_GROUP_DIM).astype(jnp.float32)
    f = jnp.fft.fft2(ug, axes=(1, 3), norm='ortho').real
    return f.reshape(B, N, FN_W).astype(u.dtype)


def merge_branches(a, f, w, gate_logits, w_br, w_out):
    outs = jnp.stack([a, f, w], axis=-2)
    y = jnp.einsum('bnic,icd->bnid', outs, w_br)
    g = jax.nn.sigmoid(gate_logits.reshape(*gate_logits.shape[:-1], N_BRANCH, D_MODEL))
    return jnp.sum(g * y, axis=-2) @ w_out


def setup_inputs(seed: int = 0) -> dict:
    key = jax.random.key(seed)
    ks = jax.random.split(key, 19)
    D = D_MODEL

    def nrm(k, shape, scale):
        return jax.random.normal(k, shape, jnp.float32) * scale

    return {
        "x": nrm(ks[0], (BATCH, SEQ, D), 1.0),
        "c": nrm(ks[1], (BATCH, D), 1.0),
        "ctx": nrm(ks[2], (BATCH, CTX_LEN, D), 1.0),
        "c_ctx": nrm(ks[3], (D,), 1.0),
        "w_ada": nrm(ks[4], (DEPTH, D, N_MOD * D), D ** -0.5),
        "b_ada": nrm(ks[5], (DEPTH, N_MOD * D), 0.02),
        "g_ffn1": 1.0 + nrm(ks[6], (DEPTH, D), 0.02),
        "ffn1_w13": nrm(ks[7], (DEPTH, D, 2 * D_FF), D ** -0.5),
        "ffn1_w2": nrm(ks[8], (DEPTH, D_FF, D), D_FF ** -0.5),
        "g_mix": 1.0 + nrm(ks[9], (DEPTH, D), 0.02),
        "w_in": nrm(ks[10], (DEPTH, D, P_IN), D ** -0.5),
        "na_bias": nrm(ks[11], (DEPTH, NA_HEADS, 2 * NA_WIN_R - 1, 2 * NA_WIN_C - 1), 0.1),
        "wa_sink": nrm(ks[12], (DEPTH, WA_HEADS), 1.0),
        "w_br": nrm(ks[13], (DEPTH, N_BRANCH, BRANCH_W, D), BRANCH_W ** -0.5),
        "w_out": nrm(ks[14], (DEPTH, D, D), D ** -0.5),
        "g_ffn2": 1.0 + nrm(ks[15], (DEPTH, D), 0.02),
        "ffn2_w13": nrm(ks[16], (DEPTH, D, 2 * D_FF), D ** -0.5),
        "ffn2_w2": nrm(ks[17], (DEPTH, D_FF, D), D_FF ** -0.5),
        "g_final": 1.0 + nrm(ks[18], (D,), 0.02),
    }


def reference(x, c, ctx, c_ctx, w_ada, b_ada, g_ffn1, ffn1_w13, ffn1_w2, g_mix, w_in, na_bias, wa_sink,
              w_br, w_out, g_ffn2, ffn2_w13, ffn2_w2, g_final):
    S = x.shape[1]
    rows = S // GRID_W
    cos, sin = axial_rope(S)
    h, hc = x, ctx
    for l in range(DEPTH):
        last = l == DEPTH - 1
        mod = (jax.nn.silu(c) @ w_ada[l] + b_ada[l])[:, None, :]
        mod_c = jax.nn.silu(c_ctx) @ w_ada[l] + b_ada[l]
        sh1, sc1, gt1, sh2, sc2, gt2, sh3, sc3, gt3 = jnp.split(mod, N_MOD, axis=-1)
        csh1, csc1, cgt1, csh2, csc2, cgt2, csh3, csc3, cgt3 = jnp.split(mod_c, N_MOD, axis=-1)

        h = ffn_half_step(h, g_ffn1[l], sh1, sc1, gt1, ffn1_w13[l], ffn1_w2[l])
        hc = ffn_half_step(hc, g_ffn1[l], csh1, csc1, cgt1, ffn1_w13[l], ffn1_w2[l])

        u = modulate(rmsnorm(h, g_mix[l]), sh2, sc2)
        uc = modulate(rmsnorm(hc, g_mix[l]), csh2, csc2)
        nq, nk, nv, fu, wq, wk, wv, gl = split_projection(u @ w_in[l])
        if last:
            cnk, cnv = jnp.split(uc @ w_in[l][:, O_NA_K:O_FN], 2, axis=-1)
            cwk, cwv = jnp.split(uc @ w_in[l][:, O_WA_K:O_GATE], 2, axis=-1)
        else:
            cnq, cnk, cnv, cfu, cwq, cwk, cwv, cgl = split_projection(uc @ w_in[l])
        nk_c, nv_c = heads(cnk, NA_HEADS), heads(cnv, NA_HEADS)
        wk_c, wv_c = heads(cwk, WA_KV_HEADS), heads(cwv, WA_KV_HEADS)

        a = neighbourhood_attention(heads(nq, NA_HEADS), heads(nk, NA_HEADS), heads(nv, NA_HEADS),
                                    nk_c, nv_c, na_bias[l], rows)
        f = fourier_mix(fu)
        w = window_gqa_attention(apply_rope(heads(wq, WA_HEADS), cos, sin),
                                 apply_rope(heads(wk, WA_KV_HEADS), cos, sin),
                                 heads(wv, WA_KV_HEADS), wk_c, wv_c, wa_sink[l])
        h = h + gt2 * merge_branches(a, f, w, gl, w_br[l], w_out[l])

        if not last:
            ca = context_attention(heads(cnq, NA_HEADS), nk_c, nv_c)
            cf = fourier_mix(cfu)
            cw = context_attention(heads(cwq, WA_HEADS), wk_c, wv_c, wa_sink[l])
            hc = hc + cgt2 * merge_branches(ca, cf, cw, cgl, w_br[l], w_out[l])
            hc = ffn_half_step(hc, g_ffn2[l], csh3, csc3, cgt3, ffn2_w13[l], ffn2_w2[l])

        h = ffn_half_step(h, g_ffn2[l], sh3, sc3, gt3, ffn2_w13[l], ffn2_w2[l])
    return rmsnorm(h, g_final)
```

```python
import contextlib
import numpy as np
import ml_dtypes
import concourse.bass as bass
import concourse.mybir as mybir
from concourse.bass_utils import run_bass_kernel_spmd

F32 = mybir.dt.float32
BF16 = mybir.dt.bfloat16
AF = mybir.ActivationFunctionType
ALU = mybir.AluOpType

D = 1024
T = 4096
LC = 256
NT = T + LC
DFF = 2816
NB = 4
DEPTH = 2
EPS = 1e-6
P_IN = 5888
TILES = [(0, 256, 1)] + [(LC + i * 512, 512, 0) for i in range(8)]
FM_CHUNKS = list(range(0, 4)) + list(range(4, 8)) + list(range(12, 16)) + list(range(16, 20)) + [20] + list(range(22, 46))
R_NQ, R_NK, R_FU, R_WQ, R_WK, R_GL = 0, 512, 1024, 1536, 2048, 2176
PT_ROWS = 5248


class Buf:
    def __init__(self, name, ap=None):
        self.name = name
        self.ap = ap
        self.w = {}
        self.r = {}
        self.dsem = None
        self.dcnt = 0

    def __getitem__(self, idx):
        return self.ap[idx]


class Eng:
    def __init__(self, K, name, eng, sem, is_pe=False):
        self.K = K
        self.name = name
        self.e = eng
        self.sem = sem
        self.cnt = 0
        self.waited = {}
        self.is_pe = is_pe
        self.pending = False

    def _wait(self, sem, val, eng):
        if eng is self and self.is_pe:
            return
        if eng is not None and val > eng.cnt:
            raise RuntimeError(f"pending token of {eng.name} awaited by {self.name}")
        k = id(sem)
        if self.waited.get(k, 0) >= val:
            return
        self.e.wait_ge(sem, val)
        self.waited[k] = val

    def deps(self, reads, writes):
        for b in reads:
            for (s, v, e) in list(b.w.values()):
                self._wait(s, v, e)
        for b in writes:
            for (s, v, e) in list(b.w.values()):
                self._wait(s, v, e)
            for (s, v, e) in list(b.r.values()):
                self._wait(s, v, e)

    @staticmethod
    def _reg(d, sem, val, eng):
        k = id(sem)
        if k not in d or d[k][1] < val:
            d[k] = (sem, val, eng)

    def op(self, ins_fn, reads=(), writes=(), signal=True):
        self.deps(reads, writes)
        ins = ins_fn()
        if signal:
            self.cnt += 1
            ins.then_inc(self.sem, 1)
            val = self.cnt
            self.pending = False
        else:
            val = self.cnt + 1
            self.pending = True
        for b in reads:
            self._reg(b.r, self.sem, val, self)
        for b in writes:
            self._reg(b.w, self.sem, val, self)
        return ins

    def dma(self, out_ap, in_ap, reads=(), writes=(), sbuf=None):
        self.deps(reads, writes)
        if sbuf.dsem is None:
            sbuf.dsem, sbuf.dcnt = self.K.new_sem("d_" + sbuf.name)
        ins = self.e.dma_start(out=out_ap, in_=in_ap)
        sbuf.dcnt += 16
        ins.then_inc(sbuf.dsem, 16)
        for b in reads:
            self._reg(b.r, sbuf.dsem, sbuf.dcnt, None)
        for b in writes:
            self._reg(b.w, sbuf.dsem, sbuf.dcnt, None)
        self.K.live_dma[id(sbuf)] = sbuf
        return ins


class K:
    def __init__(self, nc, es):
        self.nc = nc
        self.es = es
        self.nsem = 0
        self.free_sems = []
        self.live_dma = {}
        self.PE = Eng(self, "pe", nc.tensor, self._sem("s_pe"), is_pe=True)
        self.ACT = Eng(self, "act", nc.scalar, self._sem("s_act"))
        self.DVE = Eng(self, "dve", nc.vector, self._sem("s_dve"))
        self.POOL = Eng(self, "pool", nc.gpsimd, self._sem("s_pool"))
        self.SP = Eng(self, "sp", nc.sync, self._sem("s_sp"))
        self.engs = [self.PE, self.ACT, self.DVE, self.POOL, self.SP]
        self.dram = {}
        self.phase_es = None
        self.phase_bufs = []
        self.uid = 0

    def _sem(self, name):
        self.nsem += 1
        return self.es.enter_context(self.nc.semaphore(name))

    def new_sem(self, name):
        if self.free_sems:
            return self.free_sems.pop()
        self.uid += 1
        return self._sem(f"{name}_{self.uid}"), 0

    def sb(self, name, shape, dt, persistent=False):
        self.uid += 1
        es = self.es if persistent else self.phase_es
        t = es.enter_context(self.nc.sbuf_tensor(f"{name}_{self.uid}", list(shape), dt))
        b = Buf(name, t)
        if not persistent:
            self.phase_bufs.append(b)
        return b

    def dbuf(self, name, i=0):
        k = (name, i)
        if k not in self.dram:
            self.dram[k] = Buf(f"{name}_{i}")
        return self.dram[k]

    def barrier(self):
        assert not self.PE.pending, "PE group left open"
        toks = [(e.sem, e.cnt, e) for e in self.engs if e.cnt > 0]
        dm = [(b.dsem, b.dcnt) for b in self.live_dma.values() if b.dcnt > 0]
        for e in self.engs:
            for (s, v, src) in toks:
                if src is e:
                    if not e.is_pe and e.cnt > 0:
                        e._wait(s, v, None)
                else:
                    e._wait(s, v, None)
            for (s, v) in dm:
                e._wait(s, v, None)

    @contextlib.contextmanager
    def phase(self):
        self.phase_es = contextlib.ExitStack()
        self.phase_bufs = []
        with self.phase_es:
            yield
            self.barrier()
        for b in self.phase_bufs:
            if b.dsem is not None:
                self.live_dma.pop(id(b), None)
                self.free_sems.append((b.dsem, b.dcnt))
        self.phase_es = None


def build_program(dbg=(), stop_after=None):
    nc = bass.Bass("TRN2", target_bir_lowering=False)
    es = contextlib.ExitStack()

    def din(name, shape, dt=F32):
        return nc.dram_tensor(name, list(shape), dt, kind="ExternalInput").ap()

    def dscr(name, shape, dt):
        kind = "ExternalOutput" if name in dbg else "Internal"
        return nc.dram_tensor(name, list(shape), dt, kind=kind).ap()

    h0T = din("h0T", [D, NT])
    ccT = din("ccT", [128, 8, 2])
    w_ada = din("w_ada", [DEPTH, D, 9 * D])
    b_adaT = din("b_adaT", [DEPTH, 128, 72, 2])
    gvec = din("gvec", [128, 7, 8, 2])
    ffn_w13 = [din("ffn1_w13", [DEPTH, D, 2 * DFF]), din("ffn2_w13", [DEPTH, D, 2 * DFF])]
    ffn_w2 = [din("ffn1_w2", [DEPTH, DFF, D]), din("ffn2_w2", [DEPTH, DFF, D])]
    w_in = din("w_in", [DEPTH, D, P_IN])
    w_br = din("w_br", [DEPTH, 3, 512, D])
    w_out = din("w_out", [DEPTH, D, D])
    na_biasG = din("na_biasG", [DEPTH, 128, 8, 8, 256])
    na_mask = din("na_mask", [128, 8, 256])
    sinkG = din("sinkG", [DEPTH, 128, 2, 512])
    ropeC = din("ropeC", [64, T])
    ropeS = din("ropeS", [64, T])
    cs64 = din("cs64", [128, 256])
    dftC = din("dftC", [16, 128, 32, 256], BF16)
    dftS = din("dftS", [16, 128, 32, 256], BF16)
    dftC2 = din("dftC2", [128, 2, 256], BF16)
    dftS2 = din("dftS2", [128, 2, 256], BF16)
    trimask = din("trimask", [128, 2, 512])
    outT = nc.dram_tensor("outT", [D, T], F32, kind="ExternalOutput").ap()

    hT = dscr("hT", [D, NT], F32)
    uT = dscr("uT", [D, NT], BF16)
    gT = dscr("gT", [DFF, NT], BF16)
    pT = dscr("pT", [PT_ROWS, NT], BF16)
    Vtok = dscr("Vtok", [NT, 640], BF16)
    brT = dscr("brT", [3, 512, NT], BF16)

    with es:
        k = K(nc, es)
        PE, ACT, DVE, POOL, SP = k.PE, k.ACT, k.DVE, k.POOL, k.SP
        PS = []
        for i in range(8):
            t = es.enter_context(nc.psum_tensor(f"ps{i}", [128, 512], F32))
            PS.append(Buf(f"ps{i}", t))

        def mm(out, lhsT, rhs, start, stop, reads, writes):
            PE.op(lambda: nc.tensor.matmul(out, lhsT, rhs, start=start, stop=stop),
                  reads=reads, writes=writes, signal=stop)

        ones = k.sb("ones", [128, 128], BF16, persistent=True)
        epsT = k.sb("eps", [128, 1], F32, persistent=True)
        gv = k.sb("gv", [128, 7, 8, 2], F32, persistent=True)
        modT = [k.sb(f"mod{l}", [128, 72, 2], F32, persistent=True) for l in range(DEPTH)]
        Gs = [[k.sb(f"G{l}{s}", [128, 8, 2], F32, persistent=True) for s in range(3)] for l in range(DEPTH)]
        Gt = [[k.sb(f"Gt{l}{s}", [128, 8, 2], F32, persistent=True) for s in range(3)] for l in range(DEPTH)]
        DVE.op(lambda: nc.vector.memset(ones[:], 1.0), writes=[ones])
        DVE.op(lambda: nc.vector.memset(epsT[:], EPS), writes=[epsT])
        SP.dma(gv[:], gvec, writes=[gv], sbuf=gv)

        with k.phase():
            cc = k.sb("cc", [128, 8, 2], F32)
            scb = k.sb("scb", [128, 8, 2], BF16)
            bT = [k.sb(f"bT{l}", [128, 72, 2], F32) for l in range(DEPTH)]
            wsl = [k.sb(f"wada{i}", [128, 8, 1024], BF16) for i in range(2)]
            tmp1 = k.sb("tmp1", [128, 8, 2], F32)
            SP.dma(cc[:], ccT, writes=[cc], sbuf=cc)
            for l in range(DEPTH):
                SP.dma(bT[l][:], b_adaT[l], writes=[bT[l]], sbuf=bT[l])
            ACT.op(lambda: nc.scalar.activation(out=scb[:], in_=cc[:], func=AF.Silu), reads=[cc], writes=[scb])
            it = 0
            for l in range(DEPTH):
                wr = w_ada[l].rearrange("(k p) n -> p k n", p=128)
                ps = PS[l]
                for s9 in range(9):
                    wb = wsl[it % 2]
                    it += 1
                    POOL.dma(wb[:], wr[:, :, s9 * 1024:(s9 + 1) * 1024], writes=[wb], sbuf=wb)
                    for jj in range(8):
                        j = s9 * 8 + jj
                        for kk in range(8):
                            mm(ps[:, 2 * j:2 * j + 2], wb[:, kk, jj * 128:(jj + 1) * 128], scb[:, kk, :],
                               kk == 0, kk == 7, [wb, scb], [ps])
                DVE.op(lambda: nc.vector.tensor_tensor(out=modT[l][:].rearrange("p a b -> p (a b)"), in0=ps[:, 0:144],
                                                       in1=bT[l][:].rearrange("p a b -> p (a b)"), op=ALU.add),
                       reads=[ps, bT[l]], writes=[modT[l]])
                for s in range(3):
                    sc = modT[l][:, (3 * s + 1) * 8:(3 * s + 2) * 8, :]
                    DVE.op(lambda: nc.vector.tensor_scalar(out=tmp1[:], in0=sc, scalar1=1.0, scalar2=None, op0=ALU.add),
                           reads=[modT[l]], writes=[tmp1])
                    DVE.op(lambda: nc.vector.tensor_tensor(out=Gs[l][s][:], in0=tmp1[:], in1=gv[:, l * 3 + s, :, :], op=ALU.mult),
                           reads=[tmp1, gv], writes=[Gs[l][s]])
                    gt = modT[l][:, (3 * s + 2) * 8:(3 * s + 3) * 8, :]
                    fac = 1.0 if s == 1 else 0.5
                    DVE.op(lambda: nc.vector.tensor_scalar(out=Gt[l][s][:], in0=gt, scalar1=fac, scalar2=None, op0=ALU.mult),
                           reads=[modT[l]], writes=[Gt[l][s]])

        def shiftv(l, s):
            return modT[l][:, (3 * s) * 8:(3 * s + 1) * 8, :]

        def norm_phase(hsrc, hsrc_name, Gbuf, Gap, Shbuf, Shap, tiles, dst, dst_name, final=False):
            hr = hsrc.rearrange("(k p) t -> p k t", p=128)
            dr = dst.rearrange("(k p) t -> p k t", p=128)
            with k.phase():
                hin = [k.sb(f"hin{i}", [128, 8, 512], F32) for i in range(2)]
                sq = [k.sb(f"sq{i}", [128, 8, 512], BF16) for i in range(2)]
                tmp = [k.sb(f"tmp{i}", [128, 8, 512], F32) for i in range(2)]
                uo = None if final else [k.sb(f"uo{i}", [128, 8, 512], BF16) for i in range(2)]
                sd = [k.sb(f"sd{i}", [128, 512], F32) for i in range(2)]
                rs = [k.sb(f"rs{i}", [128, 512], F32) for i in range(2)]
                for ti, (c0, W, cls) in enumerate(tiles):
                    b = ti % 2
                    ps = PS[ti % 2]
                    tix = c0
                    SP.dma(hin[b][:, :, :W], hr[:, :, c0:c0 + W], reads=[k.dbuf(hsrc_name, tix)], writes=[hin[b]], sbuf=hin[b])
                    ACT.op(lambda: nc.scalar.activation(out=sq[b][:, :, :W], in_=hin[b][:, :, :W], func=AF.Square),
                           reads=[hin[b]], writes=[sq[b]])
                    for kk in range(8):
                        mm(ps[:, :W], ones[:], sq[b][:, kk, :W], kk == 0, kk == 7, [ones, sq[b]], [ps])
                    ACT.op(lambda: nc.scalar.activation(out=sd[b][:, :W], in_=ps[:, :W], func=AF.Sqrt, bias=epsT[:, 0:1], scale=1.0 / D),
                           reads=[ps, epsT], writes=[sd[b]])
                    DVE.op(lambda: nc.vector.reciprocal(out=rs[b][:, :W], in_=sd[b][:, :W]), reads=[sd[b]], writes=[rs[b]])
                    for kk in range(8):
                        DVE.op(lambda: nc.vector.scalar_tensor_tensor(out=tmp[b][:, kk, :W], in0=hin[b][:, kk, :W],
                                                                      scalar=Gap(kk, cls), in1=rs[b][:, :W],
                                                                      op0=ALU.mult, op1=ALU.mult),
                               reads=[hin[b], rs[b], Gbuf], writes=[tmp[b]])
                    if final:
                        SP.dma(dr[:, :, c0 - LC:c0 - LC + W], tmp[b][:, :, :W], reads=[tmp[b]], writes=[k.dbuf(dst_name, tix)], sbuf=tmp[b])
                    else:
                        for kk in range(8):
                            ACT.op(lambda: nc.scalar.activation(out=uo[b][:, kk, :W], in_=tmp[b][:, kk, :W], func=AF.Identity,
                                                                bias=Shap(kk, cls), scale=1.0),
                                   reads=[tmp[b], Shbuf], writes=[uo[b]])
                        SP.dma(dr[:, :, c0:c0 + W], uo[b][:, :, :W], reads=[uo[b]], writes=[k.dbuf(dst_name, tix)], sbuf=uo[b])

        def w13_phase(w13, tiles):
            wr = w13.rearrange("(k p) n -> p k n", p=128)
            ur = uT.rearrange("(k p) t -> p k t", p=128)
            gr = gT.rearrange("(j p) t -> p j t", p=128)
            with k.phase():
                wa = [k.sb(f"wa{i}", [128, 8, 1408], BF16) for i in range(2)]
                wb = [k.sb(f"wb{i}", [128, 8, 1408], BF16) for i in range(2)]
                uin = [k.sb(f"uin{i}", [128, 8, 512], BF16) for i in range(2)]
                sg = [k.sb(f"sg{i}", [128, 512], F32) for i in range(2)]
                go = [k.sb(f"go{i}", [128, 11, 512], BF16) for i in range(2)]
                for s in range(2):
                    POOL.dma(wa[s][:], wr[:, :, s * 1408:(s + 1) * 1408], writes=[wa[s]], sbuf=wa[s])
                    POOL.dma(wb[s][:], wr[:, :, DFF + s * 1408:DFF + (s + 1) * 1408], writes=[wb[s]], sbuf=wb[s])
                it = 0
                pi = 0
                for s in range(2):
                    for (c0, W, cls) in tiles:
                        b = it % 2
                        it += 1
                        SP.dma(uin[b][:, :, :W], ur[:, :, c0:c0 + W], reads=[k.dbuf("uT", c0)], writes=[uin[b]], sbuf=uin[b])
                        for j in range(11):
                            pa = PS[(2 * pi) % 8]
                            pb = PS[(2 * pi + 1) % 8]
                            sgb = sg[pi % 2]
                            pi += 1
                            for kk in range(8):
                                mm(pa[:, :W], wa[s][:, kk, j * 128:(j + 1) * 128], uin[b][:, kk, :W], kk == 0, kk == 7, [wa[s], uin[b]], [pa])
                            for kk in range(8):
                                mm(pb[:, :W], wb[s][:, kk, j * 128:(j + 1) * 128], uin[b][:, kk, :W], kk == 0, kk == 7, [wb[s], uin[b]], [pb])
                            ACT.op(lambda: nc.scalar.activation(out=sgb[:, :W], in_=pa[:, :W], func=AF.Silu), reads=[pa], writes=[sgb])
                            DVE.op(lambda: nc.vector.tensor_tensor(out=go[b][:, j, :W], in0=sgb[:, :W], in1=pb[:, :W], op=ALU.mult),
                                   reads=[sgb, pb], writes=[go[b]])
                        SP.dma(gr[:, s * 11:(s + 1) * 11, c0:c0 + W], go[b][:, :, :W], reads=[go[b]], writes=[k.dbuf("gT", (s, c0))], sbuf=go[b])

        def w2_phase(w2, hsrc, hsrc_name, gate, tiles):
            wr = w2.rearrange("(j p) n -> p j n", p=128)
            gr = gT.rearrange("(j p) t -> p j t", p=128)
            hr = hsrc.rearrange("(k p) t -> p k t", p=128)
            ho = hT.rearrange("(k p) t -> p k t", p=128)
            with k.phase():
                w = k.sb("w2", [128, 22, 1024], BF16)
                gin = [k.sb(f"gin{i}", [128, 22, 512], BF16) for i in range(2)]
                hin = [k.sb(f"hin{i}", [128, 8, 512], F32) for i in range(2)]
                POOL.dma(w[:, 0:11, :], wr[:, 0:11, :], writes=[w], sbuf=w)
                POOL.dma(w[:, 11:22, :], wr[:, 11:22, :], writes=[w], sbuf=w)
                pi = 0
                for ti, (c0, W, cls) in enumerate(tiles):
                    b = ti % 2
                    SP.dma(gin[b][:, :, :W], gr[:, :, c0:c0 + W], reads=[k.dbuf("gT", (0, c0)), k.dbuf("gT", (1, c0))], writes=[gin[b]], sbuf=gin[b])
                    SP.dma(hin[b][:, :, :W], hr[:, :, c0:c0 + W], reads=[k.dbuf(hsrc_name, c0)], writes=[hin[b]], sbuf=hin[b])
                    for n in range(8):
                        ps = PS[pi % 8]
                        pi += 1
                        for j in range(22):
                            mm(ps[:, :W], w[:, j, n * 128:(n + 1) * 128], gin[b][:, j, :W], j == 0, j == 21, [w, gin[b]], [ps])
                        DVE.op(lambda: nc.vector.scalar_tensor_tensor(out=hin[b][:, n, :W], in0=ps[:, :W], scalar=gate[:, n, cls:cls + 1],
                                                                      in1=hin[b][:, n, :W], op0=ALU.mult, op1=ALU.add),
                               reads=[ps, hin[b], gate], writes=[hin[b]])
                    SP.dma(ho[:, :, c0:c0 + W], hin[b][:, :, :W], reads=[hin[b]], writes=[k.dbuf("hT", c0)], sbuf=hin[b])

        def win_phase(l, tiles):
            wr = w_in[l].rearrange("(k p) n -> p k n", p=128)
            ur = uT.rearrange("(k p) t -> p k t", p=128)
            pr = pT.rearrange("(j p) t -> p j t", p=128)
            slabs = [FM_CHUNKS[i:i + 8] for i in range(0, len(FM_CHUNKS), 8)]
            with k.phase():
                ws = [k.sb(f"ws{i}", [128, 8, 1024], BF16) for i in range(2)]
                wv = k.sb("wv", [128, 8, 640], BF16)
                uin = [k.sb(f"uin{i}", [128, 8, 512], BF16) for i in range(2)]
                po = [k.sb(f"po{i}", [128, 8, 512], BF16) for i in range(2)]
                vo = [k.sb(f"vo{i}", [128, 640], BF16) for i in range(2)]
                POOL.dma(wv[:, :, 0:512], wr[:, :, 1024:1536], writes=[wv], sbuf=wv)
                POOL.dma(wv[:, :, 512:640], wr[:, :, 2688:2816], writes=[wv], sbuf=wv)

                def load_slab(si):
                    wsb = ws[si % 2]
                    ch = slabs[si]
                    runs = []
                    for idx, c in enumerate(ch):
                        if runs and runs[-1][1] + runs[-1][2] == c:
                            runs[-1][2] += 1
                        else:
                            runs.append([idx, c, 1])
                    for (idx, c, n) in runs:
                        POOL.dma(wsb[:, :, idx * 128:(idx + n) * 128], wr[:, :, c * 128:(c + n) * 128], writes=[wsb], sbuf=wsb)

                load_slab(0)
                it = 0
                pi = 0
                ei = 0
                row0 = 0
                for si, ch in enumerate(slabs):
                    if si + 1 < len(slabs):
                        load_slab(si + 1)
                    wsb = ws[si % 2]
                    nch = len(ch)
                    for (c0, W, cls) in tiles:
                        b = it % 2
                        it += 1
                        SP.dma(uin[b][:, :, :W], ur[:, :, c0:c0 + W], reads=[k.dbuf("uT", c0)], writes=[uin[b]], sbuf=uin[b])
                        for jj in range(nch):
                            ps = PS[pi % 6]
                            pi += 1
                            for kk in range(8):
                                mm(ps[:, :W], wsb[:, kk, jj * 128:(jj + 1) * 128], uin[b][:, kk, :W], kk == 0, kk == 7, [wsb, uin[b]], [ps])
                            if ei % 2 == 0:
                                ACT.op(lambda: nc.scalar.copy(out=po[b][:, jj, :W], in_=ps[:, :W]), reads=[ps], writes=[po[b]])
                            else:
                                DVE.op(lambda: nc.vector.tensor_copy(out=po[b][:, jj, :W], in_=ps[:, :W]), reads=[ps], writes=[po[b]])
                            ei += 1
                        SP.dma(pr[:, row0:row0 + nch, c0:c0 + W], po[b][:, 0:nch, :W], reads=[po[b]], writes=[k.dbuf("pT", (si, c0))], sbuf=po[b])
                        if si == len(slabs) - 1:
                            for tb in range(W // 128):
                                vb = vo[tb % 2]
                                p0, p1 = PS[6], PS[7]
                                for kk in range(8):
                                    mm(p0[:, 0:512], uin[b][:, kk, tb * 128:(tb + 1) * 128], wv[:, kk, 0:512], kk == 0, kk == 7, [wv, uin[b]], [p0])
                                for kk in range(8):
                                    mm(p1[:, 0:128], uin[b][:, kk, tb * 128:(tb + 1) * 128], wv[:, kk, 512:640], kk == 0, kk == 7, [wv, uin[b]], [p1])
                                ACT.op(lambda: nc.scalar.copy(out=vb[:, 0:512], in_=p0[:, 0:512]), reads=[p0], writes=[vb])
                                DVE.op(lambda: nc.vector.tensor_copy(out=vb[:, 512:640], in_=p1[:, 0:128]), reads=[p1], writes=[vb])
                                t0 = c0 + tb * 128
                                SP.dma(Vtok[t0:t0 + 128, :], vb[:], reads=[vb], writes=[k.dbuf("Vtok", 0)], sbuf=vb)
                    row0 += nch

        def na_phase(l, with_ctx_q):
            with k.phase():
                ebm = k.sb("ebm", [128, 8, 8, 256], BF16)
                msk = k.sb("msk", [128, 8, 256], F32)
                bl = [k.sb(f"bl{i}", [128, 8, 256], F32) for i in range(2)]
                SP.dma(msk[:], na_mask, writes=[msk], sbuf=msk)
                for cls in range(8):
                    b = bl[cls % 2]
                    SP.dma(b[:], na_biasG[l, :, cls, :, :], writes=[b], sbuf=b)
                    ACT.op(lambda: nc.scalar.activation(out=b[:], in_=b[:], func=AF.Exp), reads=[b], writes=[b])
                    for h in range(8):
                        DVE.op(lambda: nc.vector.tensor_tensor(out=ebm[:, cls, h, :], in0=b[:, h, :], in1=msk[:, cls, :], op=ALU.mult),
                               reads=[b, msk], writes=[ebm])
                KT = [k.sb(f"KT{i}", [64, NT], BF16) for i in range(2)]
                QT = [k.sb(f"QT{i}", [64, NT], BF16) for i in range(2)]
                VE = [k.sb(f"VE{i}", [128, 34, 128], BF16) for i in range(2)]
                VO = [k.sb(f"VO{i}", [128, 33, 128], BF16) for i in range(2)]
                AO = [k.sb(f"AO{i}", [64, NT], BF16) for i in range(2)]
                E = [k.sb(f"E{i}", [128, 512], BF16) for i in range(3)]
                rec = [k.sb(f"rec{i}", [128, 256], F32) for i in range(2)]
                for i in range(2):
                    DVE.op(lambda: nc.vector.memset(VE[i][:], 1.0), writes=[VE[i]])
                    DVE.op(lambda: nc.vector.memset(VO[i][:], 1.0), writes=[VO[i]])
                vr = Vtok.rearrange("(c p) f -> p c f", p=128)
                vro = Vtok[64:64 + 33 * 128, :].rearrange("(c p) f -> p c f", p=128)
                it = 0
                for h in range(8):
                    b = h % 2
                    SP.dma(KT[b][:], pT[R_NK + h * 64:R_NK + (h + 1) * 64, :], reads=[k.dbuf("pT", "all")], writes=[KT[b]], sbuf=KT[b])
                    SP.dma(QT[b][:], pT[R_NQ + h * 64:R_NQ + (h + 1) * 64, :], reads=[k.dbuf("pT", "all")], writes=[QT[b]], sbuf=QT[b])
                    SP.dma(VE[b][:, :, 0:64], vr[:, :, h * 64:(h + 1) * 64], reads=[k.dbuf("Vtok", 0)], writes=[VE[b]], sbuf=VE[b])
                    SP.dma(VO[b][:, :, 0:64], vro[:, :, h * 64:(h + 1) * 64], reads=[k.dbuf("Vtok", 0)], writes=[VO[b]], sbuf=VO[b])
                    for r in range(64):
                        rs_ = min(max(r - 4, 0), 56)
                        cls = r - rs_
                        tok0 = LC + rs_ * 64
                        ps_s = PS[it % 3]
                        ps_o = PS[3 + it % 3]
                        Eb = E[it % 3]
                        rc = rec[it % 2]
                        it += 1
                        q = QT[b][:, LC + r * 64:LC + (r + 1) * 64]
                        for c in range(4):
                            mm(ps_s[:, c * 64:(c + 1) * 64], KT[b][:, tok0 + c * 128:tok0 + (c + 1) * 128], q, True, True, [KT[b], QT[b]], [ps_s])
                        for c in range(2):
                            mm(ps_s[:, 256 + c * 64:256 + (c + 1) * 64], KT[b][:, c * 128:(c + 1) * 128], q, True, True, [KT[b], QT[b]], [ps_s])
                        ACT.op(lambda: nc.scalar.activation(out=Eb[:, 0:384], in_=ps_s[:, 0:384], func=AF.Exp, scale=0.125), reads=[ps_s], writes=[Eb])
                        DVE.op(lambda: nc.vector.tensor_tensor(out=Eb[:, 0:256], in0=Eb[:, 0:256], in1=ebm[:, cls, h, :], op=ALU.mult),
                               reads=[Eb, ebm], writes=[Eb])
                        for c in range(6):
                            if c < 4:
                                if rs_ % 2 == 0:
                                    vsrc, vb_ = VE[b][:, 2 + rs_ // 2 + c, :], VE[b]
                                else:
                                    vsrc, vb_ = VO[b][:, 2 + (rs_ - 1) // 2 + c, :], VO[b]
                            else:
                                vsrc, vb_ = VE[b][:, c - 4, :], VE[b]
                            mm(ps_o[:, 0:64], vsrc, Eb[:, c * 64:(c + 1) * 64], c == 0, c == 5, [vb_, Eb], [ps_o])
                        DVE.op(lambda: nc.vector.reciprocal(out=rc[64:128, 0:64], in_=ps_o[64:128, 0:64]), reads=[ps_o], writes=[rc])
                        DVE.op(lambda: nc.vector.tensor_tensor(out=AO[b][:, LC + r * 64:LC + (r + 1) * 64], in0=ps_o[0:64, 0:64], in1=rc[64:128, 0:64], op=ALU.mult),
                               reads=[ps_o, rc], writes=[AO[b]])
                    if with_ctx_q:
                        ps_s = PS[6]
                        ps_o = PS[7]
                        Eb = E[0]
                        rc = rec[0]
                        for c in range(2):
                            mm(ps_s[:, c * 256:(c + 1) * 256], KT[b][:, c * 128:(c + 1) * 128], QT[b][:, 0:256], True, True, [KT[b], QT[b]], [ps_s])
                        ACT.op(lambda: nc.scalar.activation(out=Eb[:, 0:512], in_=ps_s[:, 0:512], func=AF.Exp, scale=0.125), reads=[ps_s], writes=[Eb])
                        for c in range(2):
                            mm(ps_o[:, 0:256], VE[b][:, c, :], Eb[:, c * 256:(c + 1) * 256], c == 0, c == 1, [VE[b], Eb], [ps_o])
                        DVE.op(lambda: nc.vector.reciprocal(out=rc[64:128, 0:256], in_=ps_o[64:128, 0:256]), reads=[ps_o], writes=[rc])
                        DVE.op(lambda: nc.vector.tensor_tensor(out=AO[b][:, 0:256], in0=ps_o[0:64, 0:256], in1=rc[64:128, 0:256], op=ALU.mult),
                               reads=[ps_o, rc], writes=[AO[b]])
                        SP.dma(brT[0, h * 64:(h + 1) * 64, :], AO[b][:], reads=[AO[b]], writes=[k.dbuf("brT", 0)], sbuf=AO[b])
                    else:
                        SP.dma(brT[0, h * 64:(h + 1) * 64, LC:NT], AO[b][:, LC:NT], reads=[AO[b]], writes=[k.dbuf("brT", 0)], sbuf=AO[b])

        def wa_phase(l, with_ctx_q):
            with k.phase():
                cosT = k.sb("cosT", [64, T], F32)
                sinT = k.sb("sinT", [64, T], F32)
                tri = k.sb("tri", [128, 2, 512], F32)
                trib = k.sb("trib", [128, 2, 512], BF16)
                es_ = k.sb("esink", [128, 2, 512], F32)
                SP.dma(cosT[:], ropeC, writes=[cosT], sbuf=cosT)
                SP.dma(sinT[:], ropeS, writes=[sinT], sbuf=sinT)
                SP.dma(tri[:], trimask, writes=[tri], sbuf=tri)
                SP.dma(es_[:], sinkG[l], writes=[es_], sbuf=es_)
                DVE.op(lambda: nc.vector.tensor_copy(out=trib[:], in_=tri[:]), reads=[tri], writes=[trib])
                ACT.op(lambda: nc.scalar.activation(out=es_[:], in_=es_[:], func=AF.Exp), reads=[es_], writes=[es_])
                A = [k.sb(f"A{i}", [64, NT], BF16) for i in range(1)]
                Bm = [k.sb(f"B{i}", [64, NT], BF16) for i in range(1)]
                t1 = [k.sb(f"t1{i}", [64, T], F32) for i in range(1)]
                KT = k.sb("KTw", [64, NT], BF16)
                QT = k.sb("QTw", [64, 4, NT], BF16)
                VE = k.sb("VEw", [128, 34, 128], BF16)
                WO = k.sb("WO", [64, 4, NT], BF16)
                E = [k.sb(f"Ew{i}", [128, 512], BF16) for i in range(6)]
                den = [k.sb(f"den{i}", [128, 512], F32) for i in range(2)]
                DVE.op(lambda: nc.vector.memset(VE[:], 1.0), writes=[VE])
                vr = Vtok.rearrange("(c p) f -> p c f", p=128)
                li = 0

                def load_rope(row0, dst_ap, dstbuf):
                    nonlocal li
                    a = A[0]
                    bm = Bm[0]
                    tt = t1[0]
                    li += 1
                    SP.dma(a[:], pT[row0:row0 + 64, :], reads=[k.dbuf("pT", "all")], writes=[a], sbuf=a)
                    SP.dma(bm[0:32, :], pT[row0 + 32:row0 + 64, :], reads=[k.dbuf("pT", "all")], writes=[bm], sbuf=bm)
                    SP.dma(bm[32:64, :], pT[row0:row0 + 32, :], reads=[k.dbuf("pT", "all")], writes=[bm], sbuf=bm)
                    POOL.op(lambda: nc.gpsimd.tensor_copy(out=dst_ap[:, 0:LC], in_=a[:, 0:LC]), reads=[a], writes=[dstbuf])
                    DVE.op(lambda: nc.vector.tensor_tensor(out=tt[:], in0=a[:, LC:NT], in1=cosT[:], op=ALU.mult), reads=[a, cosT], writes=[tt])
                    POOL.op(lambda: nc.gpsimd.tensor_tensor(out=bm[:, LC:NT], in0=bm[:, LC:NT], in1=sinT[:], op=ALU.mult), reads=[bm, sinT], writes=[bm])
                    DVE.op(lambda: nc.vector.tensor_tensor(out=dst_ap[:, LC:NT], in0=tt[:], in1=bm[:, LC:NT], op=ALU.add), reads=[tt, bm], writes=[dstbuf])

                it = 0
                for kv in range(2):
                    load_rope(R_WK + kv * 64, KT[:], KT)
                    for g in range(4):
                        load_rope(R_WQ + (kv * 4 + g) * 64, QT[:, g, :], QT)
                    SP.dma(VE[:, :, 0:64], vr[:, :, 512 + kv * 64:512 + (kv + 1) * 64], reads=[k.dbuf("Vtok", 0)], writes=[VE], sbuf=VE)
                    for n in range(32):
                        chunks = []
                        if n > 0:
                            chunks.append((LC + (n - 1) * 128, 0))
                        chunks.append((LC + n * 128, None))
                        if n < 31:
                            chunks.append((LC + (n + 1) * 128, 1))
                        chunks.append((0, None))
                        chunks.append((128, None))
                        ps_o = PS[6 + it % 2]
                        dn = den[it % 2]
                        it += 1
                        q = QT[:, :, LC + n * 128:LC + (n + 1) * 128]
                        Es = []
                        for ci, (kt0, mk) in enumerate(chunks):
                            ps_s = PS[(it * 5 + ci) % 6]
                            Eb = E[(it * 5 + ci) % 6]
                            mm(ps_s[:, :].rearrange("p (g q) -> p g q", g=4), KT[:, kt0:kt0 + 128], q, True, True, [KT, QT], [ps_s])
                            ACT.op(lambda: nc.scalar.activation(out=Eb[:], in_=ps_s[:], func=AF.Exp, scale=0.125), reads=[ps_s], writes=[Eb])
                            if mk is not None:
                                POOL.op(lambda: nc.gpsimd.tensor_tensor(out=Eb[:], in0=Eb[:], in1=trib[:, mk, :], op=ALU.mult), reads=[Eb, trib], writes=[Eb])
                            Es.append((Eb, kt0))
                        for ci, (Eb, kt0) in enumerate(Es):
                            mm(ps_o[:, :], VE[:, kt0 // 128, :], Eb[:], ci == 0, ci == len(Es) - 1, [VE, Eb], [ps_o])
                        DVE.op(lambda: nc.vector.tensor_tensor(out=dn[64:128, :], in0=ps_o[64:128, :], in1=es_[64:128, kv, :], op=ALU.add), reads=[ps_o, es_], writes=[dn])
                        DVE.op(lambda: nc.vector.reciprocal(out=dn[64:128, :], in_=dn[64:128, :]), reads=[dn], writes=[dn])
                        DVE.op(lambda: nc.vector.tensor_tensor(out=WO[:, :, LC + n * 128:LC + (n + 1) * 128],
                                                               in0=ps_o[0:64, :].rearrange("p (g q) -> p g q", g=4),
                                                               in1=dn[64:128, :].rearrange("p (g q) -> p g q", g=4), op=ALU.mult),
                               reads=[ps_o, dn], writes=[WO])
                    if with_ctx_q:
                        for half in range(2):
                            ps_o = PS[6 + it % 2]
                            dn = den[it % 2]
                            it += 1
                            q = QT[:, 2 * half:2 * half + 2, 0:LC]
                            Es = []
                            for c in range(2):
                                ps_s = PS[c]
                                Eb = E[c]
                                mm(ps_s[:, :].rearrange("p (g q) -> p g q", g=2), KT[:, c * 128:(c + 1) * 128], q, True, True, [KT, QT], [ps_s])
                                ACT.op(lambda: nc.scalar.activation(out=Eb[:], in_=ps_s[:], func=AF.Exp, scale=0.125), reads=[ps_s], writes=[Eb])
                                Es.append(Eb)
                            for c in range(2):
                                mm(ps_o[:, :], VE[:, c, :], Es[c][:], c == 0, c == 1, [VE, Es[c]], [ps_o])
                            for gg in range(2):
                                g = 2 * half + gg
                                DVE.op(lambda: nc.vector.tensor_scalar(out=dn[64:128, gg * 256:(gg + 1) * 256], in0=ps_o[64:128, gg * 256:(gg + 1) * 256],
                                                                       scalar1=es_[64:128, kv, g * 128:g * 128 + 1], scalar2=None, op0=ALU.add),
                                       reads=[ps_o, es_], writes=[dn])
                            DVE.op(lambda: nc.vector.reciprocal(out=dn[64:128, :], in_=dn[64:128, :]), reads=[dn], writes=[dn])
                            DVE.op(lambda: nc.vector.tensor_tensor(out=WO[:, 2 * half:2 * half + 2, 0:LC],
                                                                   in0=ps_o[0:64, :].rearrange("p (g q) -> p g q", g=2),
                                                                   in1=dn[64:128, :].rearrange("p (g q) -> p g q", g=2), op=ALU.mult),
                                   reads=[ps_o, dn], writes=[WO])
                    c_lo = 0 if with_ctx_q else LC
                    dst = brT[2, kv * 256:(kv + 1) * 256, c_lo:NT].rearrange("(g p) t -> p g t", p=64)
                    SP.dma(dst, WO[:, :, c_lo:NT], reads=[WO], writes=[k.dbuf("brT", 2)], sbuf=WO)

        def fourier_phase(with_ctx):
            fr = pT[R_FU:R_FU + 512, :].rearrange("(j p) t -> p j t", p=128)
            fo = brT[1].rearrange("(j p) t -> p j t", p=128)
            with k.phase():
                cs32 = k.sb("cs32", [128, 256], F32)
                csb = k.sb("csb", [128, 256], BF16)
                fu = k.sb("fu", [128, 4, NT], BF16)
                ucs = k.sb("ucs", [128, 34, 2, 512], BF16)
                tb = [[k.sb(f"tb{i}{j}", [128, 32, 256], BF16) for j in range(2)] for i in range(2)]
                tb2 = [k.sb(f"tb2{j}", [128, 2, 256], BF16) for j in range(2)]
                fo_sb = [k.sb(f"fo{i}", [128, 4, 256], BF16) for i in range(2)]
                SP.dma(cs32[:], cs64, writes=[cs32], sbuf=cs32)
                DVE.op(lambda: nc.vector.tensor_copy(out=csb[:], in_=cs32[:]), reads=[cs32], writes=[csb])
                SP.dma(fu[:], fr, reads=[k.dbuf("pT", "all")], writes=[fu], sbuf=fu)
                ei = 0
                for c in range(34):
                    if c < 2 and not with_ctx:
                        continue
                    for half in range(2):
                        ps = PS[(2 * c + half) % 4]
                        for jj in range(2):
                            j = 2 * half + jj
                            mm(ps[:, jj * 256:(jj + 1) * 256], fu[:, j, c * 128:(c + 1) * 128], csb[:], True, True, [fu, csb], [ps])
                        src = ps[:, :].rearrange("p (jj s f) -> p s jj f", jj=2, s=2)
                        dstv = ucs[:, c, :, half * 256:(half + 1) * 256].rearrange("p s (jj f) -> p s jj f", jj=2)
                        if ei % 2 == 0:
                            ACT.op(lambda: nc.scalar.copy(out=dstv, in_=src), reads=[ps], writes=[ucs])
                        else:
                            DVE.op(lambda: nc.vector.tensor_copy(out=dstv, in_=src), reads=[ps], writes=[ucs])
                        ei += 1
                pi = 0
                for kb in range(16):
                    tC, tS = tb[kb % 2]
                    SP.dma(tC[:], dftC[kb], writes=[tC], sbuf=tC)
                    SP.dma(tS[:], dftS[kb], writes=[tS], sbuf=tS)
                    fb = fo_sb[kb % 2]
                    for j in range(4):
                        ps = PS[4 + pi % 4]
                        pi += 1
                        for c in range(32):
                            mm(ps[:, 0:256], ucs[:, 2 + c, 0, j * 128:(j + 1) * 128], tC[:, c, :], c == 0, False, [ucs, tC], [ps])
                            mm(ps[:, 0:256], ucs[:, 2 + c, 1, j * 128:(j + 1) * 128], tS[:, c, :], False, c == 31, [ucs, tS], [ps])
                        if j % 2 == 0:
                            ACT.op(lambda: nc.scalar.copy(out=fb[:, j, :], in_=ps[:, 0:256]), reads=[ps], writes=[fb])
                        else:
                            DVE.op(lambda: nc.vector.tensor_copy(out=fb[:, j, :], in_=ps[:, 0:256]), reads=[ps], writes=[fb])
                    SP.dma(fo[:, :, LC + kb * 256:LC + (kb + 1) * 256], fb[:], reads=[fb], writes=[k.dbuf("brT", 1)], sbuf=fb)
                if with_ctx:
                    tC, tS = tb2
                    SP.dma(tC[:], dftC2, writes=[tC], sbuf=tC)
                    SP.dma(tS[:], dftS2, writes=[tS], sbuf=tS)
                    fb = fo_sb[0]
                    for j in range(4):
                        ps = PS[4 + pi % 4]
                        pi += 1
                        for c in range(2):
                            mm(ps[:, 0:256], ucs[:, c, 0, j * 128:(j + 1) * 128], tC[:, c, :], c == 0, False, [ucs, tC], [ps])
                            mm(ps[:, 0:256], ucs[:, c, 1, j * 128:(j + 1) * 128], tS[:, c, :], False, c == 1, [ucs, tS], [ps])
                        DVE.op(lambda: nc.vector.tensor_copy(out=fb[:, j, :], in_=ps[:, 0:256]), reads=[ps], writes=[fb])
                    SP.dma(fo[:, :, 0:LC], fb[:], reads=[fb], writes=[k.dbuf("brT", 1)], sbuf=fb)

        def merge_phase(l, tiles):
            br = brT.rearrange("i (c p) t -> p i c t", p=128)
            gr = pT[R_GL:R_GL + 3072, :].rearrange("(j p) t -> p j t", p=128)
            hr = hT.rearrange("(k p) t -> p k t", p=128)
            wbr_r = w_br[l].rearrange("i (c p) n -> p i c n", p=128)
            wo_r = w_out[l].rearrange("(k p) n -> p k n", p=128)
            gate = Gt[l][1]
            with k.phase():
                wbr = k.sb("wbr", [128, 3, 4, 1024], BF16)
                wo = k.sb("wo", [128, 8, 1024], BF16)
                bin_ = [k.sb(f"bin{i}", [128, 3, 4, 512], BF16) for i in range(2)]
                gin = [k.sb(f"gin{i}", [128, 24, 512], BF16) for i in range(2)]
                hin = [k.sb(f"hin{i}", [128, 8, 512], F32) for i in range(2)]
                sg = [k.sb(f"sg{i}", [128, 512], F32) for i in range(3)]
                m32 = [k.sb(f"m32{i}", [128, 512], F32) for i in range(2)]
                tt = [k.sb(f"tt{i}", [128, 512], F32) for i in range(2)]
                mT = [k.sb(f"mT{i}", [128, 8, 512], BF16) for i in range(2)]
                for i in range(3):
                    POOL.dma(wbr[:, i, :, :], wbr_r[:, i, :, :], writes=[wbr], sbuf=wbr)
                POOL.dma(wo[:], wo_r, writes=[wo], sbuf=wo)
                pi = 0
                si = 0
                for ti, (c0, W, cls) in enumerate(tiles):
                    b = ti % 2
                    for i in range(3):
                        SP.dma(bin_[b][:, i, :, :W], br[:, i, :, c0:c0 + W], reads=[k.dbuf("brT", i)], writes=[bin_[b]], sbuf=bin_[b])
                    SP.dma(gin[b][:, :, :W], gr[:, :, c0:c0 + W], reads=[k.dbuf("pT", "all")], writes=[gin[b]], sbuf=gin[b])
                    SP.dma(hin[b][:, :, :W], hr[:, :, c0:c0 + W], reads=[k.dbuf("hT", c0)], writes=[hin[b]], sbuf=hin[b])
                    for n in range(8):
                        mb = m32[n % 2]
                        for i in range(3):
                            ps = PS[pi % 5]
                            pi += 1
                            sgb = sg[si % 3]
                            tb_ = tt[si % 2]
                            si += 1
                            for c in range(4):
                                mm(ps[:, :W], wbr[:, i, c, n * 128:(n + 1) * 128], bin_[b][:, i, c, :W], c == 0, c == 3, [wbr, bin_[b]], [ps])
                            ACT.op(lambda: nc.scalar.activation(out=sgb[:, :W], in_=gin[b][:, i * 8 + n, :W], func=AF.Sigmoid), reads=[gin[b]], writes=[sgb])
                            if i == 0:
                                DVE.op(lambda: nc.vector.tensor_tensor(out=mb[:, :W], in0=sgb[:, :W], in1=ps[:, :W], op=ALU.mult), reads=[sgb, ps], writes=[mb])
                            else:
                                DVE.op(lambda: nc.vector.tensor_tensor(out=tb_[:, :W], in0=sgb[:, :W], in1=ps[:, :W], op=ALU.mult), reads=[sgb, ps], writes=[tb_])
                                if i == 1:
                                    POOL.op(lambda: nc.gpsimd.tensor_tensor(out=mb[:, :W], in0=mb[:, :W], in1=tb_[:, :W], op=ALU.add), reads=[mb, tb_], writes=[mb])
                                else:
                                    POOL.op(lambda: nc.gpsimd.tensor_tensor(out=mT[b][:, n, :W], in0=mb[:, :W], in1=tb_[:, :W], op=ALU.add), reads=[mb, tb_], writes=[mT[b]])
                    for n2 in range(8):
                        ps = PS[5 + n2 % 3]
                        for kk in range(8):
                            mm(ps[:, :W], wo[:, kk, n2 * 128:(n2 + 1) * 128], mT[b][:, kk, :W], kk == 0, kk == 7, [wo, mT[b]], [ps])
                        DVE.op(lambda: nc.vector.scalar_tensor_tensor(out=hin[b][:, n2, :W], in0=ps[:, :W], scalar=gate[:, n2, cls:cls + 1],
                                                                      in1=hin[b][:, n2, :W], op0=ALU.mult, op1=ALU.add),
                               reads=[ps, hin[b], gate], writes=[hin[b]])
                    SP.dma(hr[:, :, c0:c0 + W], hin[b][:, :, :W], reads=[hin[b]], writes=[k.dbuf("hT", c0)], sbuf=hin[b])

        def alias_all(name, keys):
            allb = k.dbuf(name, "all")
            for kk_ in keys:
                fb = k.dram.get((name, kk_))
                if fb is not None:
                    for (s, v, e) in fb.w.values():
                        Eng._reg(allb.w, s, v, e)

        steps = []

        def run():
            for l in range(DEPTH):
                last = l == DEPTH - 1
                tiles_all = TILES
                tiles_lat = TILES[1:]
                hsrc, hname = (h0T, "h0T") if l == 0 else (hT, "hT")
                norm_phase(hsrc, hname, Gs[l][0], lambda kk, cls: Gs[l][0][:, kk, cls:cls + 1], modT[l],
                           lambda kk, cls: shiftv(l, 0)[:, kk, cls:cls + 1], tiles_all, uT, "uT")
                yield f"norm1_{l}"
                w13_phase(ffn_w13[0][l], tiles_all)
                yield f"w13a_{l}"
                w2_phase(ffn_w2[0][l], hsrc, hname, Gt[l][0], tiles_all)
                yield f"ffn1_{l}"
                norm_phase(hT, "hT", Gs[l][1], lambda kk, cls: Gs[l][1][:, kk, cls:cls + 1], modT[l],
                           lambda kk, cls: shiftv(l, 1)[:, kk, cls:cls + 1], tiles_all, uT, "uT")
                win_phase(l, tiles_all)
                yield f"win_{l}"
                na_phase(l, not last)
                yield f"na_{l}"
                fourier_phase(not last)
                yield f"fn_{l}"
                wa_phase(l, not last)
                yield f"wa_{l}"
                tl = tiles_lat if last else tiles_all
                merge_phase(l, tl)
                yield f"merge_{l}"
                norm_phase(hT, "hT", Gs[l][2], lambda kk, cls: Gs[l][2][:, kk, cls:cls + 1], modT[l],
                           lambda kk, cls: shiftv(l, 2)[:, kk, cls:cls + 1], tl, uT, "uT")
                w13_phase(ffn_w13[1][l], tl)
                w2_phase(ffn_w2[1][l], hT, "hT", Gt[l][2], tl)
                yield f"ffn2_{l}"
            norm_phase(hT, "hT", gv, lambda kk, cls: gv[:, 6, kk, 0:1], None, None, TILES[1:], outT, "outT", final=True)
            yield "final"

        for name in run():
            if stop_after is not None and name == stop_after:
                break
        k.barrier()
    return nc


def _constants():
    bf = ml_dtypes.bfloat16
    t = np.arange(T)
    row = (t // 64).astype(np.float64)
    col = (t % 64).astype(np.float64)
    inv = 10000.0 ** (-np.arange(16, dtype=np.float64) / 16)
    ang = np.concatenate([row[:, None] * inv, col[:, None] * inv], axis=-1)
    c = np.cos(ang).T
    s = np.sin(ang).T
    ropeC = np.concatenate([c, c], axis=0).astype(np.float32)
    ropeS = np.concatenate([-s, s], axis=0).astype(np.float32)
    cc = np.arange(64)
    a = 2 * np.pi * np.outer(cc, cc) / 64
    C64, S64 = np.cos(a), np.sin(a)
    z = np.zeros((64, 64))
    Cb = np.block([[C64, z], [z, C64]])
    Sb = np.block([[S64, z], [z, S64]])
    cs64 = np.concatenate([Cb, Sb], axis=1).astype(np.float32)

    def pos_tables(N, norm):
        n = np.arange(N)
        m = (np.outer(n, n) % N).astype(np.float64)
        a = 2 * np.pi * m / N
        return (np.cos(a) * norm).astype(np.float32), (-np.sin(a) * norm).astype(np.float32)

    Cn, Sn = pos_tables(T, 1.0 / 512)
    def lay(M):
        return np.ascontiguousarray(M.reshape(32, 128, 16, 256).transpose(2, 1, 0, 3)).astype(bf)
    dftC, dftS = lay(Cn), lay(Sn)
    C2, S2 = pos_tables(LC, 1.0 / 128)
    def lay2(M):
        return np.ascontiguousarray(M.reshape(2, 128, 256).transpose(1, 0, 2)).astype(bf)
    dftC2, dftS2 = lay2(C2), lay2(S2)
    p = np.arange(128)[:, None]
    q = np.arange(128)[None, :]
    m0 = (q <= p).astype(np.float32)
    m1 = (p <= q).astype(np.float32)
    trimask = np.stack([np.tile(m0, (1, 4)), np.tile(m1, (1, 4))], axis=1).astype(np.float32)
    pk = np.arange(128)
    kc = pk % 64
    qc = np.arange(64)
    win_start = np.clip(qc - 8, 0, 48)
    valid = (kc[:, None] >= win_start[None, :]) & (kc[:, None] < win_start[None, :] + 16)
    na_mask = np.broadcast_to(valid[:, None, None, :], (128, 8, 4, 64)).reshape(128, 8, 256).astype(np.float32)
    na_mask = np.ascontiguousarray(na_mask)
    kr = (2 * np.arange(4)[None, :] + (pk // 64)[:, None])
    dr_idx = kr[:, None, :] - np.arange(8)[None, :, None] + 7
    dr_ok = (dr_idx >= 0) & (dr_idx <= 14)
    dr_idx = np.clip(dr_idx, 0, 14)
    dc_idx = np.clip(kc[:, None] - qc[None, :], -15, 15) + 15
    return dict(ropeC=ropeC, ropeS=ropeS, cs64=cs64, dftC=dftC, dftS=dftS, dftC2=dftC2, dftS2=dftS2,
                trimask=trimask, na_mask=na_mask), (dr_idx, dc_idx)


_CONST = None
_PROG = {}


def _prep_inputs(inp):
    global _CONST
    if _CONST is None:
        _CONST = _constants()
    const, (dr_idx, dc_idx) = _CONST
    f32 = np.float32
    x = np.asarray(inp["x"], f32)
    ctx = np.asarray(inp["ctx"], f32)
    c = np.asarray(inp["c"], f32)
    c_ctx = np.asarray(inp["c_ctx"], f32)

    def chunked(v):
        return np.ascontiguousarray(v.reshape(8, 128).T)

    b_ada = np.asarray(inp["b_ada"], f32)
    b_adaT = np.ascontiguousarray(np.repeat(b_ada.reshape(DEPTH, 72, 128).transpose(0, 2, 1)[..., None], 2, axis=-1))
    gl = [inp["g_ffn1"][0], inp["g_mix"][0], inp["g_ffn2"][0], inp["g_ffn1"][1], inp["g_mix"][1], inp["g_ffn2"][1], inp["g_final"]]
    gvec = np.stack([chunked(np.asarray(g, f32)) for g in gl], axis=1)
    gvec = np.ascontiguousarray(np.repeat(gvec[..., None], 2, axis=-1))
    nb = np.asarray(inp["na_bias"], f32)
    G = nb[:, :, dr_idx[:, :, :, None], dc_idx[:, None, None, :]]
    na_biasG = np.ascontiguousarray(G.transpose(0, 2, 3, 1, 4, 5).reshape(DEPTH, 128, 8, 8, 256))
    sk = np.asarray(inp["wa_sink"], f32)
    sinkG = np.ascontiguousarray(np.broadcast_to(sk.reshape(DEPTH, 1, 2, 4, 1), (DEPTH, 128, 2, 4, 128)).reshape(DEPTH, 128, 2, 512))
    shared = dict(const)
    shared.update(
        w_ada=np.asarray(inp["w_ada"], f32), b_adaT=b_adaT, gvec=gvec,
        ffn1_w13=np.asarray(inp["ffn1_w13"], f32), ffn2_w13=np.asarray(inp["ffn2_w13"], f32),
        ffn1_w2=np.asarray(inp["ffn1_w2"], f32), ffn2_w2=np.asarray(inp["ffn2_w2"], f32),
        w_in=np.asarray(inp["w_in"], f32), w_br=np.asarray(inp["w_br"], f32), w_out=np.asarray(inp["w_out"], f32),
        na_biasG=na_biasG, sinkG=sinkG,
    )
    maps = []
    for b in range(NB):
        m = dict(shared)
        m["h0T"] = np.ascontiguousarray(np.concatenate([ctx[b].T, x[b].T], axis=1))
        m["ccT"] = np.ascontiguousarray(np.stack([chunked(c[b]), chunked(c_ctx)], axis=-1))
        maps.append(m)
    return maps


def kernel(**inputs):
    maps = _prep_inputs(inputs)
    if "main" not in _PROG:
        _PROG["main"] = build_program()
    nc = _PROG["main"]
    res = run_bass_kernel_spmd(nc, maps, core_ids=list(range(NB)))
    out = np.stack([np.ascontiguousarray(r["outT"].T) for r in res.results], axis=0)
    return out.astype(np.float32)
```

```python
import contextlib
import numpy as np
import ml_dtypes
import concourse.bass as bass
import concourse.mybir as mybir
from concourse.bass_utils import run_bass_kernel_spmd

F32 = mybir.dt.float32
BF16 = mybir.dt.bfloat16
AF = mybir.ActivationFunctionType
ALU = mybir.AluOpType

D = 1024
T = 4096
LC = 256
NT = T + LC
TH = T // 2
NTL = LC + TH
NCORES = 8
DFF = 2816
NB = 4
DEPTH = 2
EPS = 1e-6
P_IN = 5888
TILES = [(0, 256, 1)] + [(LC + i * 512, 512, 0) for i in range(4)]
GTILES = [(0, 256, 1)] + [(LC + i * 512, 512, 0) for i in range(8)]
R_NQ, R_NK, R_FU, R_WQ, R_WK = 0, 256, 512, 768, 1024
PT_ROWS = 1088
FM_COLS = [0, 128, 256, 384, 768, 896, 1024, 1152]


class Buf:
    def __init__(self, name, ap=None):
        self.name = name
        self.ap = ap
        self.w = {}
        self.r = {}
        self.dsem = None
        self.dcnt = 0

    def __getitem__(self, idx):
        return self.ap[idx]


class Eng:
    def __init__(self, K, name, eng, sem, is_pe=False):
        self.K = K
        self.name = name
        self.e = eng
        self.sem = sem
        self.cnt = 0
        self.waited = {}
        self.is_pe = is_pe
        self.pending = False

    def _wait(self, sem, val, eng):
        if eng is self and self.is_pe:
            return
        if eng is not None and val > eng.cnt:
            raise RuntimeError(f"pending token of {eng.name} awaited by {self.name}")
        k = id(sem)
        if self.waited.get(k, 0) >= val:
            return
        self.e.wait_ge(sem, val)
        self.waited[k] = val

    def deps(self, reads, writes):
        for b in reads:
            for (s, v, e) in list(b.w.values()):
                self._wait(s, v, e)
        for b in writes:
            for (s, v, e) in list(b.w.values()):
                self._wait(s, v, e)
            for (s, v, e) in list(b.r.values()):
                self._wait(s, v, e)

    @staticmethod
    def _reg(d, sem, val, eng):
        k = id(sem)
        if k not in d or d[k][1] < val:
            d[k] = (sem, val, eng)

    def op(self, ins_fn, reads=(), writes=(), signal=True):
        self.deps(reads, writes)
        ins = ins_fn()
        if signal:
            self.cnt += 1
            ins.then_inc(self.sem, 1)
            val = self.cnt
            self.pending = False
        else:
            val = self.cnt + 1
            self.pending = True
        for b in reads:
            self._reg(b.r, self.sem, val, self)
        for b in writes:
            self._reg(b.w, self.sem, val, self)
        return ins

    def dma(self, out_ap, in_ap, reads=(), writes=(), sbuf=None):
        self.deps(reads, writes)
        if sbuf.dsem is None:
            sbuf.dsem, sbuf.dcnt = self.K.new_sem("d_" + sbuf.name)
        ins = self.e.dma_start(out=out_ap, in_=in_ap)
        sbuf.dcnt += 16
        ins.then_inc(sbuf.dsem, 16)
        for b in reads:
            self._reg(b.r, sbuf.dsem, sbuf.dcnt, None)
        for b in writes:
            self._reg(b.w, sbuf.dsem, sbuf.dcnt, None)
        self.K.live_dma[id(sbuf)] = sbuf
        return ins


class K:
    def __init__(self, nc, es):
        self.nc = nc
        self.es = es
        self.nsem = 0
        self.free_sems = []
        self.live_dma = {}
        self.PE = Eng(self, "pe", nc.tensor, self._sem("s_pe"), is_pe=True)
        self.ACT = Eng(self, "act", nc.scalar, self._sem("s_act"))
        self.DVE = Eng(self, "dve", nc.vector, self._sem("s_dve"))
        self.POOL = Eng(self, "pool", nc.gpsimd, self._sem("s_pool"))
        self.SP = Eng(self, "sp", nc.sync, self._sem("s_sp"))
        self.engs = [self.PE, self.ACT, self.DVE, self.POOL, self.SP]
        self.dram = {}
        self.phase_es = None
        self.phase_bufs = []
        self.uid = 0

    def _sem(self, name):
        self.nsem += 1
        return self.es.enter_context(self.nc.semaphore(name))

    def new_sem(self, name):
        if self.free_sems:
            return self.free_sems.pop()
        self.uid += 1
        return self._sem(f"{name}_{self.uid}"), 0

    def sb(self, name, shape, dt, persistent=False):
        self.uid += 1
        es = self.es if persistent else self.phase_es
        t = es.enter_context(self.nc.sbuf_tensor(f"{name}_{self.uid}", list(shape), dt))
        b = Buf(name, t)
        if not persistent:
            self.phase_bufs.append(b)
        return b

    def dbuf(self, name, i=0):
        k = (name, i)
        if k not in self.dram:
            self.dram[k] = Buf(f"{name}_{i}")
        return self.dram[k]

    def barrier(self):
        assert not self.PE.pending, "PE group left open"
        toks = [(e.sem, e.cnt, e) for e in self.engs if e.cnt > 0]
        dm = [(b.dsem, b.dcnt) for b in self.live_dma.values() if b.dcnt > 0]
        for e in self.engs:
            for (s, v, src) in toks:
                if src is e:
                    if not e.is_pe and e.cnt > 0:
                        e._wait(s, v, None)
                else:
                    e._wait(s, v, None)
            for (s, v) in dm:
                e._wait(s, v, None)

    @contextlib.contextmanager
    def phase(self):
        self.phase_es = contextlib.ExitStack()
        self.phase_bufs = []
        with self.phase_es:
            yield
            self.barrier()
        for b in self.phase_bufs:
            if b.dsem is not None:
                self.live_dma.pop(id(b), None)
                self.free_sems.append((b.dsem, b.dcnt))
        self.phase_es = None


def build_program(dbg=(), stop_after=None):
    nc = bass.Bass("TRN2", target_bir_lowering=False)
    es = contextlib.ExitStack()

    def din(name, shape, dt=F32):
        return nc.dram_tensor(name, list(shape), dt, kind="ExternalInput").ap()

    def dscr(name, shape, dt):
        kind = "ExternalOutput" if name in dbg else "Internal"
        return nc.dram_tensor(name, list(shape), dt, kind=kind).ap()

    h0T = din("h0T", [D, NTL])
    oh_in = din("oh", [128, 2])
    ccT = din("ccT", [128, 8, 2])
    w_ada = din("w_ada", [DEPTH, D, 9 * D])
    b_adaT = din("b_adaT", [DEPTH, 128, 72, 2])
    gvec = din("gvec", [128, 7, 8, 2])
    ffn_w13 = [din("ffn1_w13", [DEPTH, D, 2 * DFF]), din("ffn2_w13", [DEPTH, D, 2 * DFF])]
    ffn_w2 = [din("ffn1_w2", [DEPTH, DFF, D]), din("ffn2_w2", [DEPTH, DFF, D])]
    w_in_own = din("w_in_own", [DEPTH, D, 1408])
    w_in_gl = din("w_in_gl", [DEPTH, D, 3072])
    w_br = din("w_br", [DEPTH, 3, 512, D])
    w_out = din("w_out", [DEPTH, D, D])
    na_biasG = din("na_biasG", [DEPTH, 128, 8, 4, 256])
    na_mask = din("na_mask", [128, 8, 256])
    sinkG = din("sinkG", [DEPTH, 128, 512])
    ropeC = din("ropeC", [64, T])
    ropeS = din("ropeS", [64, T])
    cs64 = din("cs64", [128, 256])
    dftC = din("dftC", [16, 128, 32, 256], BF16)
    dftS = din("dftS", [16, 128, 32, 256], BF16)
    dftC2 = din("dftC2", [128, 2, 256], BF16)
    dftS2 = din("dftS2", [128, 2, 256], BF16)
    trimask = din("trimask", [128, 2, 512])
    outT = nc.dram_tensor("outT", [D, TH], F32, kind="ExternalOutput").ap()

    hT = dscr("hT", [D, NTL], F32)
    uT = dscr("uT", [D, NTL], BF16)
    gT = dscr("gT", [DFF, NTL], BF16)
    glT = dscr("glT", [3072, NTL], BF16)
    pT = dscr("pT", [PT_ROWS, NT], BF16)
    Vtok = dscr("Vtok", [NT, 320], BF16)
    brT = dscr("brT", [3, 256, NT], BF16)
    uTs = dscr("uTs", [D, TH], BF16)
    uTg = dscr("uTg", [2 * D, TH], BF16)
    brTg = dscr("brTg", [2 * 768, NT], BF16)

    with es:
        k = K(nc, es)
        PE, ACT, DVE, POOL, SP = k.PE, k.ACT, k.DVE, k.POOL, k.SP
        PS = []
        for i in range(8):
            t = es.enter_context(nc.psum_tensor(f"ps{i}", [128, 512], F32))
            PS.append(Buf(f"ps{i}", t))

        def mm(out, lhsT, rhs, start, stop, reads, writes):
            PE.op(lambda: nc.tensor.matmul(out, lhsT, rhs, start=start, stop=stop),
                  reads=reads, writes=writes, signal=stop)

        ones = k.sb("ones", [128, 128], BF16, persistent=True)
        epsT = k.sb("eps", [128, 1], F32, persistent=True)
        gv = k.sb("gv", [128, 7, 8, 2], F32, persistent=True)
        modT = [k.sb(f"mod{l}", [128, 72, 2], F32, persistent=True) for l in range(DEPTH)]
        Gs = [[k.sb(f"G{l}{s}", [128, 8, 2], F32, persistent=True) for s in range(3)] for l in range(DEPTH)]
        Gt = [[k.sb(f"Gt{l}{s}", [128, 8, 2], F32, persistent=True) for s in range(3)] for l in range(DEPTH)]
        ccdummy = k.sb("ccdummy", [128, 2], F32, persistent=True)
        DVE.op(lambda: nc.vector.memset(ones[:], 1.0), writes=[ones])
        DVE.op(lambda: nc.vector.memset(epsT[:], EPS), writes=[epsT])
        SP.dma(gv[:], gvec, writes=[gv], sbuf=gv)
        oh = k.sb("oh", [128, 2], F32, persistent=True)
        SP.dma(oh[:], oh_in, writes=[oh], sbuf=oh)
        cc_sem = k._sem("s_cc")
        cc_cnt = [0]

        def all_gather(pieces):
            with k.phase():
                k.barrier()
                for (src, dst) in pieces:
                    ins = nc.gpsimd.collective_compute("AllGather", ALU.bypass, replica_groups=[[0, 1], [2, 3], [4, 5], [6, 7]],
                                                       ins=[src], outs=[dst])
                    cc_cnt[0] += 1
                    ins.then_inc(cc_sem, 1)
                    nc.gpsimd.wait_ge(cc_sem, cc_cnt[0])
                POOL.op(lambda: nc.gpsimd.memset(ccdummy[:], 0.0), writes=[ccdummy])

        with k.phase():
            cc = k.sb("cc", [128, 8, 2], F32)
            scb = k.sb("scb", [128, 8, 2], BF16)
            bT = [k.sb(f"bT{l}", [128, 72, 2], F32) for l in range(DEPTH)]
            wsl = [k.sb(f"wada{i}", [128, 8, 1024], BF16) for i in range(2)]
            tmp1 = k.sb("tmp1", [128, 8, 2], F32)
            SP.dma(cc[:], ccT, writes=[cc], sbuf=cc)
            for l in range(DEPTH):
                SP.dma(bT[l][:], b_adaT[l], writes=[bT[l]], sbuf=bT[l])
            ACT.op(lambda: nc.scalar.activation(out=scb[:], in_=cc[:], func=AF.Silu), reads=[cc], writes=[scb])
            it = 0
            for l in range(DEPTH):
                wr = w_ada[l].rearrange("(k p) n -> p k n", p=128)
                ps = PS[l]
                for s9 in range(9):
                    wb = wsl[it % 2]
                    it += 1
                    POOL.dma(wb[:], wr[:, :, s9 * 1024:(s9 + 1) * 1024], writes=[wb], sbuf=wb)
                    for jj in range(8):
                        j = s9 * 8 + jj
                        for kk in range(8):
                            mm(ps[:, 2 * j:2 * j + 2], wb[:, kk, jj * 128:(jj + 1) * 128], scb[:, kk, :],
                               kk == 0, kk == 7, [wb, scb], [ps])
                DVE.op(lambda: nc.vector.tensor_tensor(out=modT[l][:].rearrange("p a b -> p (a b)"), in0=ps[:, 0:144],
                                                       in1=bT[l][:].rearrange("p a b -> p (a b)"), op=ALU.add),
                       reads=[ps, bT[l]], writes=[modT[l]])
                for s in range(3):
                    sc = modT[l][:, (3 * s + 1) * 8:(3 * s + 2) * 8, :]
                    DVE.op(lambda: nc.vector.tensor_scalar(out=tmp1[:], in0=sc, scalar1=1.0, scalar2=None, op0=ALU.add),
                           reads=[modT[l]], writes=[tmp1])
                    DVE.op(lambda: nc.vector.tensor_tensor(out=Gs[l][s][:], in0=tmp1[:], in1=gv[:, l * 3 + s, :, :], op=ALU.mult),
                           reads=[tmp1, gv], writes=[Gs[l][s]])
                    gt = modT[l][:, (3 * s + 2) * 8:(3 * s + 3) * 8, :]
                    fac = 1.0 if s == 1 else 0.5
                    DVE.op(lambda: nc.vector.tensor_scalar(out=Gt[l][s][:], in0=gt, scalar1=fac, scalar2=None, op0=ALU.mult),
                           reads=[modT[l]], writes=[Gt[l][s]])

        def shiftv(l, s):
            return modT[l][:, (3 * s) * 8:(3 * s + 1) * 8, :]

        def norm_phase(hsrc, hsrc_name, Gbuf, Gap, Shbuf, Shap, tiles, dst, dst_name, final=False, send=None):
            hr = hsrc.rearrange("(k p) t -> p k t", p=128)
            dr = dst.rearrange("(k p) t -> p k t", p=128)
            with k.phase():
                hin = [k.sb(f"hin{i}", [128, 8, 512], F32) for i in range(2)]
                sq = [k.sb(f"sq{i}", [128, 8, 512], BF16) for i in range(2)]
                tmp = [k.sb(f"tmp{i}", [128, 8, 512], F32) for i in range(2)]
                uo = None if final else [k.sb(f"uo{i}", [128, 8, 512], BF16) for i in range(2)]
                sd = [k.sb(f"sd{i}", [128, 512], F32) for i in range(2)]
                rs = [k.sb(f"rs{i}", [128, 512], F32) for i in range(2)]
                for ti, (c0, W, cls) in enumerate(tiles):
                    b = ti % 2
                    ps = PS[ti % 2]
                    tix = c0
                    SP.dma(hin[b][:, :, :W], hr[:, :, c0:c0 + W], reads=[k.dbuf(hsrc_name, tix)], writes=[hin[b]], sbuf=hin[b])
                    ACT.op(lambda: nc.scalar.activation(out=sq[b][:, :, :W], in_=hin[b][:, :, :W], func=AF.Square),
                           reads=[hin[b]], writes=[sq[b]])
                    for kk in range(8):
                        mm(ps[:, :W], ones[:], sq[b][:, kk, :W], kk == 0, kk == 7, [ones, sq[b]], [ps])
                    ACT.op(lambda: nc.scalar.activation(out=sd[b][:, :W], in_=ps[:, :W], func=AF.Sqrt, bias=epsT[:, 0:1], scale=1.0 / D),
                           reads=[ps, epsT], writes=[sd[b]])
                    DVE.op(lambda: nc.vector.reciprocal(out=rs[b][:, :W], in_=sd[b][:, :W]), reads=[sd[b]], writes=[rs[b]])
                    for kk in range(8):
                        DVE.op(lambda: nc.vector.scalar_tensor_tensor(out=tmp[b][:, kk, :W], in0=hin[b][:, kk, :W],
                                                                      scalar=Gap(kk, cls), in1=rs[b][:, :W],
                                                                      op0=ALU.mult, op1=ALU.mult),
                               reads=[hin[b], rs[b], Gbuf], writes=[tmp[b]])
                    if final:
                        SP.dma(dr[:, :, c0 - LC:c0 - LC + W], tmp[b][:, :, :W], reads=[tmp[b]], writes=[k.dbuf(dst_name, tix)], sbuf=tmp[b])
                    else:
                        for kk in range(8):
                            ACT.op(lambda: nc.scalar.activation(out=uo[b][:, kk, :W], in_=tmp[b][:, kk, :W], func=AF.Identity,
                                                                bias=Shap(kk, cls), scale=1.0),
                                   reads=[tmp[b], Shbuf], writes=[uo[b]])
                        SP.dma(dr[:, :, c0:c0 + W], uo[b][:, :, :W], reads=[uo[b]], writes=[k.dbuf(dst_name, tix)], sbuf=uo[b])
                        if send is not None and cls == 0:
                            sr = send.rearrange("(k p) t -> p k t", p=128)
                            SP.dma(sr[:, :, c0 - LC:c0 - LC + W], uo[b][:, :, :W], reads=[uo[b]], writes=[k.dbuf("uTs", tix)], sbuf=uo[b])

        def w13_phase(w13, tiles):
            wr = w13.rearrange("(k p) n -> p k n", p=128)
            ur = uT.rearrange("(k p) t -> p k t", p=128)
            gr = gT.rearrange("(j p) t -> p j t", p=128)
            with k.phase():
                wa = [k.sb(f"wa{i}", [128, 8, 1408], BF16) for i in range(2)]
                wb = [k.sb(f"wb{i}", [128, 8, 1408], BF16) for i in range(2)]
                uin = [k.sb(f"uin{i}", [128, 8, 512], BF16) for i in range(2)]
                sg = [k.sb(f"sg{i}", [128, 512], F32) for i in range(2)]
                go = [k.sb(f"go{i}", [128, 11, 512], BF16) for i in range(2)]
                for s in range(2):
                    POOL.dma(wa[s][:], wr[:, :, s * 1408:(s + 1) * 1408], writes=[wa[s]], sbuf=wa[s])
                    POOL.dma(wb[s][:], wr[:, :, DFF + s * 1408:DFF + (s + 1) * 1408], writes=[wb[s]], sbuf=wb[s])
                it = 0
                pi = 0
                for s in range(2):
                    for (c0, W, cls) in tiles:
                        b = it % 2
                        it += 1
                        SP.dma(uin[b][:, :, :W], ur[:, :, c0:c0 + W], reads=[k.dbuf("uT", c0)], writes=[uin[b]], sbuf=uin[b])
                        for j in range(11):
                            pa = PS[(2 * pi) % 8]
                            pb = PS[(2 * pi + 1) % 8]
                            sgb = sg[pi % 2]
                            pi += 1
                            for kk in range(8):
                                mm(pa[:, :W], wa[s][:, kk, j * 128:(j + 1) * 128], uin[b][:, kk, :W], kk == 0, kk == 7, [wa[s], uin[b]], [pa])
                            for kk in range(8):
                                mm(pb[:, :W], wb[s][:, kk, j * 128:(j + 1) * 128], uin[b][:, kk, :W], kk == 0, kk == 7, [wb[s], uin[b]], [pb])
                            ACT.op(lambda: nc.scalar.activation(out=sgb[:, :W], in_=pa[:, :W], func=AF.Silu), reads=[pa], writes=[sgb])
                            DVE.op(lambda: nc.vector.tensor_tensor(out=go[b][:, j, :W], in0=sgb[:, :W], in1=pb[:, :W], op=ALU.mult),
                                   reads=[sgb, pb], writes=[go[b]])
                        SP.dma(gr[:, s * 11:(s + 1) * 11, c0:c0 + W], go[b][:, :, :W], reads=[go[b]], writes=[k.dbuf("gT", (s, c0))], sbuf=go[b])

        def w2_phase(w2, hsrc, hsrc_name, gate, tiles):
            wr = w2.rearrange("(j p) n -> p j n", p=128)
            gr = gT.rearrange("(j p) t -> p j t", p=128)
            hr = hsrc.rearrange("(k p) t -> p k t", p=128)
            ho = hT.rearrange("(k p) t -> p k t", p=128)
            with k.phase():
                w = k.sb("w2", [128, 22, 1024], BF16)
                gin = [k.sb(f"gin{i}", [128, 22, 512], BF16) for i in range(2)]
                hin = [k.sb(f"hin{i}", [128, 8, 512], F32) for i in range(2)]
                POOL.dma(w[:, 0:11, :], wr[:, 0:11, :], writes=[w], sbuf=w)
                POOL.dma(w[:, 11:22, :], wr[:, 11:22, :], writes=[w], sbuf=w)
                pi = 0
                for ti, (c0, W, cls) in enumerate(tiles):
                    b = ti % 2
                    SP.dma(gin[b][:, :, :W], gr[:, :, c0:c0 + W], reads=[k.dbuf("gT", (0, c0)), k.dbuf("gT", (1, c0))], writes=[gin[b]], sbuf=gin[b])
                    SP.dma(hin[b][:, :, :W], hr[:, :, c0:c0 + W], reads=[k.dbuf(hsrc_name, c0)], writes=[hin[b]], sbuf=hin[b])
                    for n in range(8):
                        ps = PS[pi % 8]
                        pi += 1
                        for j in range(22):
                            mm(ps[:, :W], w[:, j, n * 128:(n + 1) * 128], gin[b][:, j, :W], j == 0, j == 21, [w, gin[b]], [ps])
                        DVE.op(lambda: nc.vector.scalar_tensor_tensor(out=hin[b][:, n, :W], in0=ps[:, :W], scalar=gate[:, n, cls:cls + 1],
                                                                      in1=hin[b][:, n, :W], op0=ALU.mult, op1=ALU.add),
                               reads=[ps, hin[b], gate], writes=[hin[b]])
                    SP.dma(ho[:, :, c0:c0 + W], hin[b][:, :, :W], reads=[hin[b]], writes=[k.dbuf("hT", c0)], sbuf=hin[b])

        def win_gl_phase(l, tiles):
            wr = w_in_gl[l].rearrange("(k p) n -> p k n", p=128)
            ur = uT.rearrange("(k p) t -> p k t", p=128)
            pr = glT.rearrange("(j p) t -> p j t", p=128)
            with k.phase():
                ws = [k.sb(f"ws{i}", [128, 8, 1024], BF16) for i in range(3)]
                uin = [k.sb(f"uin{i}", [128, 8, 512], BF16) for i in range(2)]
                po = [k.sb(f"po{i}", [128, 8, 512], BF16) for i in range(2)]
                for si in range(3):
                    POOL.dma(ws[si][:], wr[:, :, si * 1024:(si + 1) * 1024], writes=[ws[si]], sbuf=ws[si])
                it = 0
                pi = 0
                ei = 0
                for si in range(3):
                    wsb = ws[si]
                    for (c0, W, cls) in tiles:
                        b = it % 2
                        it += 1
                        SP.dma(uin[b][:, :, :W], ur[:, :, c0:c0 + W], reads=[k.dbuf("uT", c0)], writes=[uin[b]], sbuf=uin[b])
                        for jj in range(8):
                            ps = PS[pi % 8]
                            pi += 1
                            for kk in range(8):
                                mm(ps[:, :W], wsb[:, kk, jj * 128:(jj + 1) * 128], uin[b][:, kk, :W], kk == 0, kk == 7, [wsb, uin[b]], [ps])
                            if ei % 2 == 0:
                                ACT.op(lambda: nc.scalar.copy(out=po[b][:, jj, :W], in_=ps[:, :W]), reads=[ps], writes=[po[b]])
                            else:
                                DVE.op(lambda: nc.vector.tensor_copy(out=po[b][:, jj, :W], in_=ps[:, :W]), reads=[ps], writes=[po[b]])
                            ei += 1
                        SP.dma(pr[:, si * 8:(si + 1) * 8, c0:c0 + W], po[b][:, :, :W], reads=[po[b]], writes=[k.dbuf("glT", (si, c0))], sbuf=po[b])

        def win_own_phase(l):
            wr = w_in_own[l].rearrange("(k p) n -> p k n", p=128)
            ur = uT.rearrange("(k p) t -> p k t", p=128)
            ug = uTg.rearrange("(q r k p) t -> p q r k t", p=128, k=4, r=2)
            pr = pT[0:1024, :].rearrange("(j p) t -> p j t", p=128)
            with k.phase():
                wsb = k.sb("wown", [128, 8, 1408], BF16)
                uin = [k.sb(f"uin{i}", [128, 8, 512], BF16) for i in range(2)]
                po = [k.sb(f"po{i}", [128, 9, 512], BF16) for i in range(2)]
                vo = [k.sb(f"vo{i}", [128, 320], BF16) for i in range(2)]
                POOL.dma(wsb[:], wr, writes=[wsb], sbuf=wsb)
                pi = 0
                ei = 0
                for ti, (c0, W, cls) in enumerate(GTILES):
                    b = ti % 2
                    if cls == 1:
                        SP.dma(uin[b][:, :, :W], ur[:, :, 0:W], reads=[k.dbuf("uT", 0)], writes=[uin[b]], sbuf=uin[b])
                    else:
                        g = (c0 - LC) // 512
                        for q in range(2):
                            SP.dma(uin[b][:, 4 * q:4 * q + 4, :W], ug[:, q, g // 4, :, (g % 4) * 512:(g % 4) * 512 + W], reads=[k.dbuf("uTg", 0)], writes=[uin[b]], sbuf=uin[b])
                    for jj in range(9):
                        ps = PS[pi % 6]
                        pi += 1
                        if jj < 8:
                            cs, M = FM_COLS[jj], 128
                        else:
                            cs, M = 1280, 64
                        for kk in range(8):
                            mm(ps[0:M, :W], wsb[:, kk, cs:cs + M], uin[b][:, kk, :W], kk == 0, kk == 7, [wsb, uin[b]], [ps])
                        if ei % 2 == 0:
                            ACT.op(lambda: nc.scalar.copy(out=po[b][0:M, jj, :W], in_=ps[0:M, :W]), reads=[ps], writes=[po[b]])
                        else:
                            DVE.op(lambda: nc.vector.tensor_copy(out=po[b][0:M, jj, :W], in_=ps[0:M, :W]), reads=[ps], writes=[po[b]])
                        ei += 1
                    SP.dma(pr[:, :, c0:c0 + W], po[b][:, 0:8, :W], reads=[po[b]], writes=[k.dbuf("pT", c0)], sbuf=po[b])
                    SP.dma(pT[1024:1088, c0:c0 + W], po[b][0:64, 8, :W], reads=[po[b]], writes=[k.dbuf("pT", c0)], sbuf=po[b])
                    for tb in range(W // 128):
                        vb = vo[tb % 2]
                        p0 = PS[6 + tb % 2]
                        for kk in range(8):
                            mm(p0[:, 0:256], uin[b][:, kk, tb * 128:(tb + 1) * 128], wsb[:, kk, 512:768], kk == 0, kk == 7, [wsb, uin[b]], [p0])
                        for kk in range(8):
                            mm(p0[:, 256:320], uin[b][:, kk, tb * 128:(tb + 1) * 128], wsb[:, kk, 1344:1408], kk == 0, kk == 7, [wsb, uin[b]], [p0])
                        if tb % 2 == 0:
                            ACT.op(lambda: nc.scalar.copy(out=vb[:, 0:320], in_=p0[:, 0:320]), reads=[p0], writes=[vb])
                        else:
                            DVE.op(lambda: nc.vector.tensor_copy(out=vb[:, 0:320], in_=p0[:, 0:320]), reads=[p0], writes=[vb])
                        t0 = c0 + tb * 128
                        SP.dma(Vtok[t0:t0 + 128, :], vb[:], reads=[vb], writes=[k.dbuf("Vtok", 0)], sbuf=vb)

        def na_phase(l, with_ctx_q):
            with k.phase():
                ebm = k.sb("ebm", [128, 8, 4, 256], BF16)
                msk = k.sb("msk", [128, 8, 256], F32)
                bl = [k.sb(f"bl{i}", [128, 4, 256], F32) for i in range(2)]
                SP.dma(msk[:], na_mask, writes=[msk], sbuf=msk)
                for cls in range(8):
                    b = bl[cls % 2]
                    SP.dma(b[:], na_biasG[l, :, cls, :, :], writes=[b], sbuf=b)
                    ACT.op(lambda: nc.scalar.activation(out=b[:], in_=b[:], func=AF.Exp), reads=[b], writes=[b])
                    for h in range(4):
                        DVE.op(lambda: nc.vector.tensor_tensor(out=ebm[:, cls, h, :], in0=b[:, h, :], in1=msk[:, cls, :], op=ALU.mult),
                               reads=[b, msk], writes=[ebm])
                KT = [k.sb(f"KT{i}", [64, NT], BF16) for i in range(2)]
                QT = [k.sb(f"QT{i}", [64, NT], BF16) for i in range(2)]
                VE = [k.sb(f"VE{i}", [128, 34, 128], BF16) for i in range(2)]
                VO = [k.sb(f"VO{i}", [128, 33, 128], BF16) for i in range(2)]
                AO = [k.sb(f"AO{i}", [64, NT], BF16) for i in range(2)]
                E = [k.sb(f"E{i}", [128, 512], BF16) for i in range(3)]
                rec = [k.sb(f"rec{i}", [128, 256], F32) for i in range(2)]
                for i in range(2):
                    DVE.op(lambda: nc.vector.memset(VE[i][:], 1.0), writes=[VE[i]])
                    DVE.op(lambda: nc.vector.memset(VO[i][:], 1.0), writes=[VO[i]])
                vr = Vtok.rearrange("(c p) f -> p c f", p=128)
                vro = Vtok[64:64 + 33 * 128, :].rearrange("(c p) f -> p c f", p=128)
                it = 0
                for h in range(4):
                    b = h % 2
                    SP.dma(KT[b][:], pT[R_NK + h * 64:R_NK + (h + 1) * 64, :], reads=[k.dbuf("pT", "all")], writes=[KT[b]], sbuf=KT[b])
                    SP.dma(QT[b][:], pT[R_NQ + h * 64:R_NQ + (h + 1) * 64, :], reads=[k.dbuf("pT", "all")], writes=[QT[b]], sbuf=QT[b])
                    for (ca, cb) in ((0, 9), (9, 18), (18, 27), (27, 34)):
                        SP.dma(VE[b][:, ca:cb, 0:64], vr[:, ca:cb, h * 64:(h + 1) * 64], reads=[k.dbuf("Vtok", 0)], writes=[VE[b]], sbuf=VE[b])
                    for (ca, cb) in ((0, 9), (9, 18), (18, 27), (27, 33)):
                        SP.dma(VO[b][:, ca:cb, 0:64], vro[:, ca:cb, h * 64:(h + 1) * 64], reads=[k.dbuf("Vtok", 0)], writes=[VO[b]], sbuf=VO[b])
                    for r in range(64):
                        rs_ = min(max(r - 4, 0), 56)
                        cls = r - rs_
                        tok0 = LC + rs_ * 64
                        ps_s = PS[it % 3]
                        ps_o = PS[3 + it % 3]
                        Eb = E[it % 3]
                        rc = rec[it % 2]
                        it += 1
                        q = QT[b][:, LC + r * 64:LC + (r + 1) * 64]
                        for c in range(4):
                            mm(ps_s[:, c * 64:(c + 1) * 64], KT[b][:, tok0 + c * 128:tok0 + (c + 1) * 128], q, True, True, [KT[b], QT[b]], [ps_s])
                        for c in range(2):
                            mm(ps_s[:, 256 + c * 64:256 + (c + 1) * 64], KT[b][:, c * 128:(c + 1) * 128], q, True, True, [KT[b], QT[b]], [ps_s])
                        ACT.op(lambda: nc.scalar.activation(out=Eb[:, 0:384], in_=ps_s[:, 0:384], func=AF.Exp, scale=0.125), reads=[ps_s], writes=[Eb])
                        DVE.op(lambda: nc.vector.tensor_tensor(out=Eb[:, 0:256], in0=Eb[:, 0:256], in1=ebm[:, cls, h, :], op=ALU.mult),
                               reads=[Eb, ebm], writes=[Eb])
                        for c in range(6):
                            if c < 4:
                                if rs_ % 2 == 0:
                                    vsrc, vb_ = VE[b][:, 2 + rs_ // 2 + c, :], VE[b]
                                else:
                                    vsrc, vb_ = VO[b][:, 2 + (rs_ - 1) // 2 + c, :], VO[b]
                            else:
                                vsrc, vb_ = VE[b][:, c - 4, :], VE[b]
                            mm(ps_o[:, 0:64], vsrc, Eb[:, c * 64:(c + 1) * 64], c == 0, c == 5, [vb_, Eb], [ps_o])
                        DVE.op(lambda: nc.vector.reciprocal(out=rc[64:128, 0:64], in_=ps_o[64:128, 0:64]), reads=[ps_o], writes=[rc])
                        DVE.op(lambda: nc.vector.tensor_tensor(out=AO[b][:, LC + r * 64:LC + (r + 1) * 64], in0=ps_o[0:64, 0:64], in1=rc[64:128, 0:64], op=ALU.mult),
                               reads=[ps_o, rc], writes=[AO[b]])
                    if with_ctx_q:
                        ps_s = PS[6]
                        ps_o = PS[7]
                        Eb = E[0]
                        rc = rec[0]
                        for c in range(2):
                            mm(ps_s[:, c * 256:(c + 1) * 256], KT[b][:, c * 128:(c + 1) * 128], QT[b][:, 0:256], True, True, [KT[b], QT[b]], [ps_s])
                        ACT.op(lambda: nc.scalar.activation(out=Eb[:, 0:512], in_=ps_s[:, 0:512], func=AF.Exp, scale=0.125), reads=[ps_s], writes=[Eb])
                        for c in range(2):
                            mm(ps_o[:, 0:256], VE[b][:, c, :], Eb[:, c * 256:(c + 1) * 256], c == 0, c == 1, [VE[b], Eb], [ps_o])
                        DVE.op(lambda: nc.vector.reciprocal(out=rc[64:128, 0:256], in_=ps_o[64:128, 0:256]), reads=[ps_o], writes=[rc])
                        DVE.op(lambda: nc.vector.tensor_tensor(out=AO[b][:, 0:256], in0=ps_o[0:64, 0:256], in1=rc[64:128, 0:256], op=ALU.mult),
                               reads=[ps_o, rc], writes=[AO[b]])
                        SP.dma(brT[0, h * 64:(h + 1) * 64, :], AO[b][:], reads=[AO[b]], writes=[k.dbuf("brT", 0)], sbuf=AO[b])
                    else:
                        SP.dma(brT[0, h * 64:(h + 1) * 64, LC:NT], AO[b][:, LC:NT], reads=[AO[b]], writes=[k.dbuf("brT", 0)], sbuf=AO[b])

        def wa_phase(l, with_ctx_q):
            with k.phase():
                cosT = k.sb("cosT", [64, T], F32)
                sinT = k.sb("sinT", [64, T], F32)
                tri = k.sb("tri", [128, 2, 512], F32)
                trib = k.sb("trib", [128, 2, 512], BF16)
                es_ = k.sb("esink", [128, 512], F32)
                SP.dma(cosT[:], ropeC, writes=[cosT], sbuf=cosT)
                SP.dma(sinT[:], ropeS, writes=[sinT], sbuf=sinT)
                SP.dma(tri[:], trimask, writes=[tri], sbuf=tri)
                SP.dma(es_[:], sinkG[l], writes=[es_], sbuf=es_)
                DVE.op(lambda: nc.vector.tensor_copy(out=trib[:], in_=tri[:]), reads=[tri], writes=[trib])
                ACT.op(lambda: nc.scalar.activation(out=es_[:], in_=es_[:], func=AF.Exp), reads=[es_], writes=[es_])
                A = [k.sb(f"A{i}", [64, NT], BF16) for i in range(1)]
                Bm = [k.sb(f"B{i}", [64, NT], BF16) for i in range(1)]
                t1 = [k.sb(f"t1{i}", [64, T], F32) for i in range(1)]
                KT = k.sb("KTw", [64, NT], BF16)
                QT = k.sb("QTw", [64, 4, NT], BF16)
                VE = k.sb("VEw", [128, 34, 128], BF16)
                WO = k.sb("WO", [64, 4, NT], BF16)
                E = [k.sb(f"Ew{i}", [128, 512], BF16) for i in range(6)]
                den = [k.sb(f"den{i}", [128, 512], F32) for i in range(2)]
                DVE.op(lambda: nc.vector.memset(VE[:], 1.0), writes=[VE])
                vr = Vtok.rearrange("(c p) f -> p c f", p=128)
                li = 0

                def load_rope(row0, dst_ap, dstbuf):
                    nonlocal li
                    a = A[0]
                    bm = Bm[0]
                    tt = t1[0]
                    li += 1
                    SP.dma(a[:], pT[row0:row0 + 64, :], reads=[k.dbuf("pT", "all")], writes=[a], sbuf=a)
                    SP.dma(bm[0:32, :], pT[row0 + 32:row0 + 64, :], reads=[k.dbuf("pT", "all")], writes=[bm], sbuf=bm)
                    SP.dma(bm[32:64, :], pT[row0:row0 + 32, :], reads=[k.dbuf("pT", "all")], writes=[bm], sbuf=bm)
                    DVE.op(lambda: nc.vector.tensor_copy(out=dst_ap[:, 0:LC], in_=a[:, 0:LC]), reads=[a], writes=[dstbuf])
                    DVE.op(lambda: nc.vector.tensor_tensor(out=tt[:], in0=a[:, LC:NT], in1=cosT[:], op=ALU.mult), reads=[a, cosT], writes=[tt])
                    DVE.op(lambda: nc.vector.tensor_tensor(out=bm[:, LC:NT], in0=bm[:, LC:NT], in1=sinT[:], op=ALU.mult), reads=[bm, sinT], writes=[bm])
                    DVE.op(lambda: nc.vector.tensor_tensor(out=dst_ap[:, LC:NT], in0=tt[:], in1=bm[:, LC:NT], op=ALU.add), reads=[tt, bm], writes=[dstbuf])

                it = 0
                for kv in range(1):
                    load_rope(R_WK, KT[:], KT)
                    for g in range(4):
                        load_rope(R_WQ + g * 64, QT[:, g, :], QT)
                    for (ca, cb) in ((0, 9), (9, 18), (18, 27), (27, 34)):
                        SP.dma(VE[:, ca:cb, 0:64], vr[:, ca:cb, 256:320], reads=[k.dbuf("Vtok", 0)], writes=[VE], sbuf=VE)
                    for n in range(32):
                        chunks = []
                        if n > 0:
                            chunks.append((LC + (n - 1) * 128, 0))
                        chunks.append((LC + n * 128, None))
                        if n < 31:
                            chunks.append((LC + (n + 1) * 128, 1))
                        chunks.append((0, None))
                        chunks.append((128, None))
                        ps_o = PS[6 + it % 2]
                        dn = den[it % 2]
                        it += 1
                        q = QT[:, :, LC + n * 128:LC + (n + 1) * 128]
                        Es = []
                        for ci, (kt0, mk) in enumerate(chunks):
                            ps_s = PS[(it * 5 + ci) % 6]
                            Eb = E[(it * 5 + ci) % 6]
                            mm(ps_s[:, :].rearrange("p (g q) -> p g q", g=4), KT[:, kt0:kt0 + 128], q, True, True, [KT, QT], [ps_s])
                            ACT.op(lambda: nc.scalar.activation(out=Eb[:], in_=ps_s[:], func=AF.Exp, scale=0.125), reads=[ps_s], writes=[Eb])
                            if mk is not None:
                                DVE.op(lambda: nc.vector.tensor_tensor(out=Eb[:], in0=Eb[:], in1=trib[:, mk, :], op=ALU.mult), reads=[Eb, trib], writes=[Eb])
                            Es.append((Eb, kt0))
                        for ci, (Eb, kt0) in enumerate(Es):
                            mm(ps_o[:, :], VE[:, kt0 // 128, :], Eb[:], ci == 0, ci == len(Es) - 1, [VE, Eb], [ps_o])
                        DVE.op(lambda: nc.vector.tensor_tensor(out=dn[64:128, :], in0=ps_o[64:128, :], in1=es_[64:128, :], op=ALU.add), reads=[ps_o, es_], writes=[dn])
                        DVE.op(lambda: nc.vector.reciprocal(out=dn[64:128, :], in_=dn[64:128, :]), reads=[dn], writes=[dn])
                        DVE.op(lambda: nc.vector.tensor_tensor(out=WO[:, :, LC + n * 128:LC + (n + 1) * 128],
                                                               in0=ps_o[0:64, :].rearrange("p (g q) -> p g q", g=4),
                                                               in1=dn[64:128, :].rearrange("p (g q) -> p g q", g=4), op=ALU.mult),
                               reads=[ps_o, dn], writes=[WO])
                    if with_ctx_q:
                        for half in range(2):
                            ps_o = PS[6 + it % 2]
                            dn = den[it % 2]
                            it += 1
                            q = QT[:, 2 * half:2 * half + 2, 0:LC]
                            Es = []
                            for c in range(2):
                                ps_s = PS[c]
                                Eb = E[c]
                                mm(ps_s[:, :].rearrange("p (g q) -> p g q", g=2), KT[:, c * 128:(c + 1) * 128], q, True, True, [KT, QT], [ps_s])
                                ACT.op(lambda: nc.scalar.activation(out=Eb[:], in_=ps_s[:], func=AF.Exp, scale=0.125), reads=[ps_s], writes=[Eb])
                                Es.append(Eb)
                            for c in range(2):
                                mm(ps_o[:, :], VE[:, c, :], Es[c][:], c == 0, c == 1, [VE, Es[c]], [ps_o])
                            for gg in range(2):
                                g = 2 * half + gg
                                DVE.op(lambda: nc.vector.tensor_scalar(out=dn[64:128, gg * 256:(gg + 1) * 256], in0=ps_o[64:128, gg * 256:(gg + 1) * 256],
                                                                       scalar1=es_[64:128, g * 128:g * 128 + 1], scalar2=None, op0=ALU.add),
                                       reads=[ps_o, es_], writes=[dn])
                            DVE.op(lambda: nc.vector.reciprocal(out=dn[64:128, :], in_=dn[64:128, :]), reads=[dn], writes=[dn])
                            DVE.op(lambda: nc.vector.tensor_tensor(out=WO[:, 2 * half:2 * half + 2, 0:LC],
                                                                   in0=ps_o[0:64, :].rearrange("p (g q) -> p g q", g=2),
                                                                   in1=dn[64:128, :].rearrange("p (g q) -> p g q", g=2), op=ALU.mult),
                                   reads=[ps_o, dn], writes=[WO])
                    c_lo = 0 if with_ctx_q else LC
                    dst = brT[2, :, c_lo:NT].rearrange("(g p) t -> p g t", p=64)
                    SP.dma(dst, WO[:, :, c_lo:NT], reads=[WO], writes=[k.dbuf("brT", 2)], sbuf=WO)

        def fourier_phase(with_ctx):
            fr = pT[R_FU:R_FU + 256, :].rearrange("(j p) t -> p j t", p=128)
            fo = brT[1].rearrange("(j p) t -> p j t", p=128)
            with k.phase():
                cs32 = k.sb("cs32", [128, 256], F32)
                csb = k.sb("csb", [128, 256], BF16)
                fu = k.sb("fu", [128, 2, NT], BF16)
                ucs = k.sb("ucs", [128, 34, 2, 256], BF16)
                tb = [[k.sb(f"tb{i}{j}", [128, 32, 256], BF16) for j in range(2)] for i in range(2)]
                tb2 = [k.sb(f"tb2{j}", [128, 2, 256], BF16) for j in range(2)]
                fo_sb = [k.sb(f"fo{i}", [128, 2, 256], BF16) for i in range(2)]
                SP.dma(cs32[:], cs64, writes=[cs32], sbuf=cs32)
                DVE.op(lambda: nc.vector.tensor_copy(out=csb[:], in_=cs32[:]), reads=[cs32], writes=[csb])
                SP.dma(fu[:], fr, reads=[k.dbuf("pT", "all")], writes=[fu], sbuf=fu)
                ei = 0
                for c in range(34):
                    if c < 2 and not with_ctx:
                        continue
                    for half in range(1):
                        ps = PS[c % 4]
                        for jj in range(2):
                            j = jj
                            mm(ps[:, jj * 256:(jj + 1) * 256], fu[:, j, c * 128:(c + 1) * 128], csb[:], True, True, [fu, csb], [ps])
                        src = ps[:, :].rearrange("p (jj s f) -> p s jj f", jj=2, s=2)
                        dstv = ucs[:, c, :, :].rearrange("p s (jj f) -> p s jj f", jj=2)
                        if ei % 2 == 0:
                            ACT.op(lambda: nc.scalar.copy(out=dstv, in_=src), reads=[ps], writes=[ucs])
                        else:
                            DVE.op(lambda: nc.vector.tensor_copy(out=dstv, in_=src), reads=[ps], writes=[ucs])
                        ei += 1
                pi = 0
                for kb in range(16):
                    tC, tS = tb[kb % 2]
                    SP.dma(tC[:], dftC[kb], writes=[tC], sbuf=tC)
                    SP.dma(tS[:], dftS[kb], writes=[tS], sbuf=tS)
                    fb = fo_sb[kb % 2]
                    for j in range(2):
                        ps = PS[4 + pi % 4]
                        pi += 1
                        for c in range(32):
                            mm(ps[:, 0:256], ucs[:, 2 + c, 0, j * 128:(j + 1) * 128], tC[:, c, :], c == 0, False, [ucs, tC], [ps])
                            mm(ps[:, 0:256], ucs[:, 2 + c, 1, j * 128:(j + 1) * 128], tS[:, c, :], False, c == 31, [ucs, tS], [ps])
                        if j % 2 == 0:
                            ACT.op(lambda: nc.scalar.copy(out=fb[:, j, :], in_=ps[:, 0:256]), reads=[ps], writes=[fb])
                        else:
                            DVE.op(lambda: nc.vector.tensor_copy(out=fb[:, j, :], in_=ps[:, 0:256]), reads=[ps], writes=[fb])
                    SP.dma(fo[:, :, LC + kb * 256:LC + (kb + 1) * 256], fb[:], reads=[fb], writes=[k.dbuf("brT", 1)], sbuf=fb)
                if with_ctx:
                    tC, tS = tb2
                    SP.dma(tC[:], dftC2, writes=[tC], sbuf=tC)
                    SP.dma(tS[:], dftS2, writes=[tS], sbuf=tS)
                    fb = fo_sb[0]
                    for j in range(2):
                        ps = PS[4 + pi % 4]
                        pi += 1
                        for c in range(2):
                            mm(ps[:, 0:256], ucs[:, c, 0, j * 128:(j + 1) * 128], tC[:, c, :], c == 0, False, [ucs, tC], [ps])
                            mm(ps[:, 0:256], ucs[:, c, 1, j * 128:(j + 1) * 128], tS[:, c, :], False, c == 1, [ucs, tS], [ps])
                        DVE.op(lambda: nc.vector.tensor_copy(out=fb[:, j, :], in_=ps[:, 0:256]), reads=[ps], writes=[fb])
                    SP.dma(fo[:, :, 0:LC], fb[:], reads=[fb], writes=[k.dbuf("brT", 1)], sbuf=fb)

        def merge_phase(l, tiles):
            bg = brTg.rearrange("(i cc r p) t -> p r i cc t", p=128, cc=2, r=2)
            gr = glT.rearrange("(j p) t -> p j t", p=128)
            hr = hT.rearrange("(k p) t -> p k t", p=128)
            wbr_r = w_br[l].rearrange("i (c p) n -> p i c n", p=128)
            wo_r = w_out[l].rearrange("(k p) n -> p k n", p=128)
            gate = Gt[l][1]
            with k.phase():
                wbr = k.sb("wbr", [128, 3, 4, 1024], BF16)
                wo = k.sb("wo", [128, 8, 1024], BF16)
                bin_ = [k.sb(f"bin{i}", [128, 3, 4, 512], BF16) for i in range(2)]
                binB = [k.sb(f"binB{i}", [128, 3, 4, 512], BF16) for i in range(2)]
                gin = [k.sb(f"gin{i}", [128, 24, 512], BF16) for i in range(2)]
                hin = [k.sb(f"hin{i}", [128, 8, 512], F32) for i in range(2)]
                sg = [k.sb(f"sg{i}", [128, 512], F32) for i in range(3)]
                m32 = [k.sb(f"m32{i}", [128, 512], F32) for i in range(2)]
                tt = [k.sb(f"tt{i}", [128, 512], F32) for i in range(2)]
                mT = [k.sb(f"mT{i}", [128, 8, 512], BF16) for i in range(2)]
                for i in range(3):
                    POOL.dma(wbr[:, i, :, :], wbr_r[:, i, :, :], writes=[wbr], sbuf=wbr)
                POOL.dma(wo[:], wo_r, writes=[wo], sbuf=wo)
                pi = 0
                si = 0
                for ti, (c0, W, cls) in enumerate(tiles):
                    b = ti % 2
                    if cls == 1:
                        for i in range(3):
                            for r in range(2):
                                SP.dma(bin_[b][:, i, 2 * r:2 * r + 2, :W], bg[:, r, i, :, 0:W], reads=[k.dbuf("brTg", 0)], writes=[bin_[b]], sbuf=bin_[b])
                    else:
                        ca = c0
                        cb = c0 + TH
                        for i in range(3):
                            for r in range(2):
                                SP.dma(bin_[b][:, i, 2 * r:2 * r + 2, :W], bg[:, r, i, :, ca:ca + W], reads=[k.dbuf("brTg", 0)], writes=[bin_[b]], sbuf=bin_[b])
                                SP.dma(binB[b][:, i, 2 * r:2 * r + 2, :W], bg[:, r, i, :, cb:cb + W], reads=[k.dbuf("brTg", 0)], writes=[binB[b]], sbuf=binB[b])
                        DVE.op(lambda: nc.vector.tensor_scalar(out=bin_[b][:], in0=bin_[b][:], scalar1=oh[:, 0:1], scalar2=None, op0=ALU.mult),
                               reads=[bin_[b], oh], writes=[bin_[b]])
                        DVE.op(lambda: nc.vector.scalar_tensor_tensor(out=bin_[b][:], in0=binB[b][:], scalar=oh[:, 1:2], in1=bin_[b][:], op0=ALU.mult, op1=ALU.add),
                               reads=[bin_[b], binB[b], oh], writes=[bin_[b]])
                    SP.dma(gin[b][:, :, :W], gr[:, :, c0:c0 + W], reads=[k.dbuf("glT", "all")], writes=[gin[b]], sbuf=gin[b])
                    SP.dma(hin[b][:, :, :W], hr[:, :, c0:c0 + W], reads=[k.dbuf("hT", c0)], writes=[hin[b]], sbuf=hin[b])
                    for n in range(8):
                        mb = m32[n % 2]
                        for i in range(3):
                            ps = PS[pi % 5]
                            pi += 1
                            sgb = sg[si % 3]
                            tb_ = tt[si % 2]
                            si += 1
                            for c in range(4):
                                mm(ps[:, :W], wbr[:, i, c, n * 128:(n + 1) * 128], bin_[b][:, i, c, :W], c == 0, c == 3, [wbr, bin_[b]], [ps])
                            ACT.op(lambda: nc.scalar.activation(out=sgb[:, :W], in_=gin[b][:, i * 8 + n, :W], func=AF.Sigmoid), reads=[gin[b]], writes=[sgb])
                            if i == 0:
                                DVE.op(lambda: nc.vector.tensor_tensor(out=mb[:, :W], in0=sgb[:, :W], in1=ps[:, :W], op=ALU.mult), reads=[sgb, ps], writes=[mb])
                            else:
                                DVE.op(lambda: nc.vector.tensor_tensor(out=tb_[:, :W], in0=sgb[:, :W], in1=ps[:, :W], op=ALU.mult), reads=[sgb, ps], writes=[tb_])
                                if i == 1:
                                    DVE.op(lambda: nc.vector.tensor_tensor(out=mb[:, :W], in0=mb[:, :W], in1=tb_[:, :W], op=ALU.add), reads=[mb, tb_], writes=[mb])
                                else:
                                    DVE.op(lambda: nc.vector.tensor_tensor(out=mT[b][:, n, :W], in0=mb[:, :W], in1=tb_[:, :W], op=ALU.add), reads=[mb, tb_], writes=[mT[b]])
                    for n2 in range(8):
                        ps = PS[5 + n2 % 3]
                        for kk in range(8):
                            mm(ps[:, :W], wo[:, kk, n2 * 128:(n2 + 1) * 128], mT[b][:, kk, :W], kk == 0, kk == 7, [wo, mT[b]], [ps])
                        DVE.op(lambda: nc.vector.scalar_tensor_tensor(out=hin[b][:, n2, :W], in0=ps[:, :W], scalar=gate[:, n2, cls:cls + 1],
                                                                      in1=hin[b][:, n2, :W], op0=ALU.mult, op1=ALU.add),
                               reads=[ps, hin[b], gate], writes=[hin[b]])
                    SP.dma(hr[:, :, c0:c0 + W], hin[b][:, :, :W], reads=[hin[b]], writes=[k.dbuf("hT", c0)], sbuf=hin[b])

        def alias_all(name, keys):
            allb = k.dbuf(name, "all")
            for kk_ in keys:
                fb = k.dram.get((name, kk_))
                if fb is not None:
                    for (s, v, e) in fb.w.values():
                        Eng._reg(allb.w, s, v, e)

        steps = []

        def run():
            for l in range(DEPTH):
                last = l == DEPTH - 1
                tiles_all = TILES
                tiles_lat = TILES[1:]
                hsrc, hname = (h0T, "h0T") if l == 0 else (hT, "hT")
                norm_phase(hsrc, hname, Gs[l][0], lambda kk, cls: Gs[l][0][:, kk, cls:cls + 1], modT[l],
                           lambda kk, cls: shiftv(l, 0)[:, kk, cls:cls + 1], tiles_all, uT, "uT")
                yield f"norm1_{l}"
                w13_phase(ffn_w13[0][l], tiles_all)
                yield f"w13a_{l}"
                w2_phase(ffn_w2[0][l], hsrc, hname, Gt[l][0], tiles_all)
                yield f"ffn1_{l}"
                norm_phase(hT, "hT", Gs[l][1], lambda kk, cls: Gs[l][1][:, kk, cls:cls + 1], modT[l],
                           lambda kk, cls: shiftv(l, 1)[:, kk, cls:cls + 1], tiles_all, uT, "uT", send=uTs)
                all_gather([(uTs[q * 512:(q + 1) * 512, :], uTg[q * 1024:(q + 1) * 1024, :]) for q in range(2)])
                win_gl_phase(l, tiles_lat if last else tiles_all)
                win_own_phase(l)
                yield f"win_{l}"
                na_phase(l, not last)
                yield f"na_{l}"
                fourier_phase(not last)
                yield f"fn_{l}"
                wa_phase(l, not last)
                yield f"wa_{l}"
                b2 = brT.rearrange("i c t -> (i c) t")
                all_gather([(b2[q * 128:(q + 1) * 128, :], brTg[q * 256:(q + 1) * 256, :]) for q in range(6)])
                tl = tiles_lat if last else tiles_all
                merge_phase(l, tl)
                yield f"merge_{l}"
                norm_phase(hT, "hT", Gs[l][2], lambda kk, cls: Gs[l][2][:, kk, cls:cls + 1], modT[l],
                           lambda kk, cls: shiftv(l, 2)[:, kk, cls:cls + 1], tl, uT, "uT")
                w13_phase(ffn_w13[1][l], tl)
                w2_phase(ffn_w2[1][l], hT, "hT", Gt[l][2], tl)
                yield f"ffn2_{l}"
            norm_phase(hT, "hT", gv, lambda kk, cls: gv[:, 6, kk, 0:1], None, None, TILES[1:], outT, "outT", final=True)
            yield "final"

        for name in run():
            if stop_after is not None and name == stop_after:
                break
        k.barrier()
    return nc


def _constants():
    bf = ml_dtypes.bfloat16
    t = np.arange(T)
    row = (t // 64).astype(np.float64)
    col = (t % 64).astype(np.float64)
    inv = 10000.0 ** (-np.arange(16, dtype=np.float64) / 16)
    ang = np.concatenate([row[:, None] * inv, col[:, None] * inv], axis=-1)
    c = np.cos(ang).T
    s = np.sin(ang).T
    ropeC = np.concatenate([c, c], axis=0).astype(np.float32)
    ropeS = np.concatenate([-s, s], axis=0).astype(np.float32)
    cc = np.arange(64)
    a = 2 * np.pi * np.outer(cc, cc) / 64
    C64, S64 = np.cos(a), np.sin(a)
    z = np.zeros((64, 64))
    Cb = np.block([[C64, z], [z, C64]])
    Sb = np.block([[S64, z], [z, S64]])
    cs64 = np.concatenate([Cb, Sb], axis=1).astype(np.float32)

    def pos_tables(N, norm):
        n = np.arange(N)
        m = (np.outer(n, n) % N).astype(np.float64)
        a = 2 * np.pi * m / N
        return (np.cos(a) * norm).astype(np.float32), (-np.sin(a) * norm).astype(np.float32)

    Cn, Sn = pos_tables(T, 1.0 / 512)
    def lay(M):
        return np.ascontiguousarray(M.reshape(32, 128, 16, 256).transpose(2, 1, 0, 3)).astype(bf)
    dftC, dftS = lay(Cn), lay(Sn)
    C2, S2 = pos_tables(LC, 1.0 / 128)
    def lay2(M):
        return np.ascontiguousarray(M.reshape(2, 128, 256).transpose(1, 0, 2)).astype(bf)
    dftC2, dftS2 = lay2(C2), lay2(S2)
    p = np.arange(128)[:, None]
    q = np.arange(128)[None, :]
    m0 = (q <= p).astype(np.float32)
    m1 = (p <= q).astype(np.float32)
    trimask = np.stack([np.tile(m0, (1, 4)), np.tile(m1, (1, 4))], axis=1).astype(np.float32)
    pk = np.arange(128)
    kc = pk % 64
    qc = np.arange(64)
    win_start = np.clip(qc - 8, 0, 48)
    valid = (kc[:, None] >= win_start[None, :]) & (kc[:, None] < win_start[None, :] + 16)
    na_mask = np.broadcast_to(valid[:, None, None, :], (128, 8, 4, 64)).reshape(128, 8, 256).astype(np.float32)
    na_mask = np.ascontiguousarray(na_mask)
    kr = (2 * np.arange(4)[None, :] + (pk // 64)[:, None])
    dr_idx = kr[:, None, :] - np.arange(8)[None, :, None] + 7
    dr_ok = (dr_idx >= 0) & (dr_idx <= 14)
    dr_idx = np.clip(dr_idx, 0, 14)
    dc_idx = np.clip(kc[:, None] - qc[None, :], -15, 15) + 15
    return dict(ropeC=ropeC, ropeS=ropeS, cs64=cs64, dftC=dftC, dftS=dftS, dftC2=dftC2, dftS2=dftS2,
                trimask=trimask, na_mask=na_mask), (dr_idx, dc_idx)


_CONST = None
_PROG = {}


def _prep_inputs(inp):
    global _CONST
    if _CONST is None:
        _CONST = _constants()
    const, (dr_idx, dc_idx) = _CONST
    f32 = np.float32
    x = np.asarray(inp["x"], f32)
    ctx = np.asarray(inp["ctx"], f32)
    c = np.asarray(inp["c"], f32)
    c_ctx = np.asarray(inp["c_ctx"], f32)

    def chunked(v):
        return np.ascontiguousarray(v.reshape(8, 128).T)

    b_ada = np.asarray(inp["b_ada"], f32)
    b_adaT = np.ascontiguousarray(np.repeat(b_ada.reshape(DEPTH, 72, 128).transpose(0, 2, 1)[..., None], 2, axis=-1))
    gl = [inp["g_ffn1"][0], inp["g_mix"][0], inp["g_ffn2"][0], inp["g_ffn1"][1], inp["g_mix"][1], inp["g_ffn2"][1], inp["g_final"]]
    gvec = np.stack([chunked(np.asarray(g, f32)) for g in gl], axis=1)
    gvec = np.ascontiguousarray(np.repeat(gvec[..., None], 2, axis=-1))
    nb = np.asarray(inp["na_bias"], f32)
    G = nb[:, :, dr_idx[:, :, :, None], dc_idx[:, None, None, :]]
    na_biasG = np.ascontiguousarray(G.transpose(0, 2, 3, 1, 4, 5).reshape(DEPTH, 128, 8, 8, 256))
    sk = np.asarray(inp["wa_sink"], f32)
    w_in = np.asarray(inp["w_in"], f32)
    shared = dict(const)
    shared.update(
        w_ada=np.asarray(inp["w_ada"], f32), b_adaT=b_adaT, gvec=gvec,
        ffn1_w13=np.asarray(inp["ffn1_w13"], f32), ffn2_w13=np.asarray(inp["ffn2_w13"], f32),
        ffn1_w2=np.asarray(inp["ffn1_w2"], f32), ffn2_w2=np.asarray(inp["ffn2_w2"], f32),
        w_in_gl=np.ascontiguousarray(w_in[:, :, 2816:5888]),
        w_br=np.asarray(inp["w_br"], f32), w_out=np.asarray(inp["w_out"], f32),
    )
    per_half = []
    for hf in range(2):
        cols = np.concatenate([np.arange(0 + hf * 256, 0 + hf * 256 + 256), np.arange(512 + hf * 256, 512 + hf * 256 + 256),
                               np.arange(1024 + hf * 256, 1024 + hf * 256 + 256), np.arange(1536 + hf * 256, 1536 + hf * 256 + 256),
                               np.arange(2048 + hf * 256, 2048 + hf * 256 + 256), np.arange(2560 + hf * 64, 2560 + hf * 64 + 64),
                               np.arange(2688 + hf * 64, 2688 + hf * 64 + 64)])
        oh = np.zeros((128, 2), f32)
        oh[:, hf] = 1.0
        per_half.append(dict(
            w_in_own=np.ascontiguousarray(w_in[:, :, cols]),
            na_biasG=np.ascontiguousarray(na_biasG[:, :, :, 4 * hf:4 * hf + 4, :]),
            sinkG=np.ascontiguousarray(np.broadcast_to(sk[:, 4 * hf:4 * hf + 4].reshape(DEPTH, 1, 4, 1), (DEPTH, 128, 4, 128)).reshape(DEPTH, 128, 512)),
            oh=oh,
        ))
    maps = []
    for core in range(NCORES):
        b, hf = core // 2, core % 2
        m = dict(shared)
        m.update(per_half[hf])
        m["h0T"] = np.ascontiguousarray(np.concatenate([ctx[b].T, x[b, hf * TH:(hf + 1) * TH].T], axis=1))
        m["ccT"] = np.ascontiguousarray(np.stack([chunked(c[b]), chunked(c_ctx)], axis=-1))
        maps.append(m)
    return maps


def kernel(**inputs):
    maps = _prep_inputs(inputs)
    if "main" not in _PROG:
        _PROG["main"] = build_program()
    nc = _PROG["main"]
    res = run_bass_kernel_spmd(nc, maps, core_ids=list(range(NCORES)))
    out = np.empty((NB, T, D), np.float32)
    for core in range(NCORES):
        b, hf = core // 2, core % 2
        out[b, hf * TH:(hf + 1) * TH, :] = np.asarray(res.results[core]["outT"]).T
    return out
```

```python
import contextlib
import numpy as np
import ml_dtypes
import concourse.bass as bass
import concourse.mybir as mybir
from concourse.bass_utils import run_bass_kernel_spmd

F32 = mybir.dt.float32
BF16 = mybir.dt.bfloat16
AF = mybir.ActivationFunctionType
ALU = mybir.AluOpType

D = 1024
T = 4096
LC = 256
NT = T + LC
TH = T // 2
NTL = LC + TH
NCORES = 8
DFF = 2816
NB = 4
DEPTH = 2
EPS = 1e-6
P_IN = 5888
TILES = [(0, 256, 1)] + [(LC + i * 512, 512, 0) for i in range(4)]
GTILES = [(0, 256, 1)] + [(LC + i * 512, 512, 0) for i in range(8)]
R_NQ, R_NK, R_FU, R_WQ, R_WK = 0, 256, 512, 768, 1024
PT_ROWS = 1088
FM_COLS = [0, 128, 256, 384, 768, 896, 1024, 1152]


class Buf:
    def __init__(self, name, ap=None):
        self.name = name
        self.ap = ap
        self.w = {}
        self.r = {}
        self.dsem = None
        self.dcnt = 0

    def __getitem__(self, idx):
        return self.ap[idx]


class Eng:
    def __init__(self, K, name, eng, sem, is_pe=False):
        self.K = K
        self.name = name
        self.e = eng
        self.sem = sem
        self.cnt = 0
        self.waited = {}
        self.is_pe = is_pe
        self.pending = False

    def _wait(self, sem, val, eng):
        if eng is self and self.is_pe:
            return
        if eng is not None and val > eng.cnt:
            raise RuntimeError(f"pending token of {eng.name} awaited by {self.name}")
        k = id(sem)
        if self.waited.get(k, 0) >= val:
            return
        self.e.wait_ge(sem, val)
        self.waited[k] = val

    def deps(self, reads, writes):
        for b in reads:
            for (s, v, e) in list(b.w.values()):
                self._wait(s, v, e)
        for b in writes:
            for (s, v, e) in list(b.w.values()):
                self._wait(s, v, e)
            for (s, v, e) in list(b.r.values()):
                self._wait(s, v, e)

    @staticmethod
    def _reg(d, sem, val, eng):
        k = id(sem)
        if k not in d or d[k][1] < val:
            d[k] = (sem, val, eng)

    def op(self, ins_fn, reads=(), writes=(), signal=True):
        self.deps(reads, writes)
        ins = ins_fn()
        if signal:
            self.cnt += 1
            ins.then_inc(self.sem, 1)
            val = self.cnt
            self.pending = False
        else:
            val = self.cnt + 1
            self.pending = True
        for b in reads:
            self._reg(b.r, self.sem, val, self)
        for b in writes:
            self._reg(b.w, self.sem, val, self)
        return ins

    def dma(self, out_ap, in_ap, reads=(), writes=(), sbuf=None):
        self.deps(reads, writes)
        if sbuf.dsem is None:
            sbuf.dsem, sbuf.dcnt = self.K.new_sem("d_" + sbuf.name)
        ins = self.e.dma_start(out=out_ap, in_=in_ap)
        sbuf.dcnt += 16
        ins.then_inc(sbuf.dsem, 16)
        for b in reads:
            self._reg(b.r, sbuf.dsem, sbuf.dcnt, None)
        for b in writes:
            self._reg(b.w, sbuf.dsem, sbuf.dcnt, None)
        self.K.live_dma[id(sbuf)] = sbuf
        return ins


class K:
    def __init__(self, nc, es):
        self.nc = nc
        self.es = es
        self.nsem = 0
        self.free_sems = []
        self.live_dma = {}
        self.PE = Eng(self, "pe", nc.tensor, self._sem("s_pe"), is_pe=True)
        self.ACT = Eng(self, "act", nc.scalar, self._sem("s_act"))
        self.DVE = Eng(self, "dve", nc.vector, self._sem("s_dve"))
        self.POOL = Eng(self, "pool", nc.gpsimd, self._sem("s_pool"))
        self.SP = Eng(self, "sp", nc.sync, self._sem("s_sp"))
        self.engs = [self.PE, self.ACT, self.DVE, self.POOL, self.SP]
        self.dram = {}
        self.scopes = []
        self.uid = 0

    def _sem(self, name):
        self.nsem += 1
        return self.es.enter_context(self.nc.semaphore(name))

    def new_sem(self, name):
        if self.free_sems:
            return self.free_sems.pop()
        self.uid += 1
        return self._sem(f"{name}_{self.uid}"), 0

    def sb(self, name, shape, dt, persistent=False):
        self.uid += 1
        es = self.es if persistent else self.scopes[-1]["es"]
        t = es.enter_context(self.nc.sbuf_tensor(f"{name}_{self.uid}", list(shape), dt))
        b = Buf(name, t)
        if not persistent:
            self.scopes[-1]["bufs"].append(b)
        return b

    def dbuf(self, name, i=0):
        k = (name, i)
        if k not in self.dram:
            self.dram[k] = Buf(f"{name}_{i}")
        return self.dram[k]

    def barrier(self, skip=()):
        assert not self.PE.pending, "PE group left open"
        toks = [(e.sem, e.cnt, e) for e in self.engs if e.cnt > 0]
        dm = [(b.dsem, b.dcnt) for b in self.live_dma.values() if b.dcnt > 0 and id(b) not in skip]
        for e in self.engs:
            for (s, v, src) in toks:
                if src is e:
                    if not e.is_pe and e.cnt > 0:
                        e._wait(s, v, None)
                else:
                    e._wait(s, v, None)
            for (s, v) in dm:
                e._wait(s, v, None)

    @contextlib.contextmanager
    def phase(self):
        sc = {"es": contextlib.ExitStack(), "bufs": []}
        self.scopes.append(sc)
        with sc["es"]:
            yield
            outer = set(id(b) for s_ in self.scopes[:-1] for b in s_["bufs"])
            self.barrier(skip=outer)
        self.scopes.pop()
        for b in sc["bufs"]:
            if b.dsem is not None:
                self.live_dma.pop(id(b), None)
                self.free_sems.append((b.dsem, b.dcnt))


def build_program(dbg=(), stop_after=None):
    nc = bass.Bass("TRN2", target_bir_lowering=False)
    es = contextlib.ExitStack()

    def din(name, shape, dt=F32):
        return nc.dram_tensor(name, list(shape), dt, kind="ExternalInput").ap()

    def dscr(name, shape, dt):
        kind = "ExternalOutput" if name in dbg else "Internal"
        return nc.dram_tensor(name, list(shape), dt, kind=kind).ap()

    h0T = din("h0T", [D, NTL])
    oh_in = din("oh", [128, 2])
    ccT = din("ccT", [128, 8, 2])
    w_ada = din("w_ada", [DEPTH, D, 9 * D])
    b_adaT = din("b_adaT", [DEPTH, 128, 72, 2])
    gvec = din("gvec", [128, 7, 8, 2])
    ffn_w13 = [din("ffn1_w13", [DEPTH, D, 2 * DFF]), din("ffn2_w13", [DEPTH, D, 2 * DFF])]
    ffn_w2 = [din("ffn1_w2", [DEPTH, DFF, D]), din("ffn2_w2", [DEPTH, DFF, D])]
    w_in_own = din("w_in_own", [DEPTH, D, 1408])
    w_in_gl = din("w_in_gl", [DEPTH, D, 3072])
    w_br = din("w_br", [DEPTH, 3, 512, D])
    w_out = din("w_out", [DEPTH, D, D])
    na_biasG = din("na_biasG", [DEPTH, 128, 8, 4, 256])
    na_mask = din("na_mask", [128, 8, 256])
    sinkG = din("sinkG", [DEPTH, 128, 512])
    ropeC = din("ropeC", [64, T])
    ropeS = din("ropeS", [64, T])
    cs64 = din("cs64", [128, 256])
    dftC = din("dftC", [16, 128, 32, 256], BF16)
    dftS = din("dftS", [16, 128, 32, 256], BF16)
    dftC2 = din("dftC2", [128, 2, 256], BF16)
    dftS2 = din("dftS2", [128, 2, 256], BF16)
    trimask = din("trimask", [128, 2, 512])
    outT = nc.dram_tensor("outT", [D, TH], F32, kind="ExternalOutput").ap()

    hT = dscr("hT", [D, NTL], F32)
    uT = dscr("uT", [D, NTL], BF16)
    gT = dscr("gT", [DFF, NTL], BF16)
    glT = dscr("glT", [3072, NTL], BF16)
    pT = dscr("pT", [PT_ROWS, NT], BF16)
    Vtok = dscr("Vtok", [NT, 320], BF16)
    brT = dscr("brT", [3, 256, NT], BF16)
    uTs = dscr("uTs", [D, TH], BF16)
    uTg = dscr("uTg", [2 * D, TH], BF16)
    brTg = dscr("brTg", [2 * 768, NT], BF16)

    with es:
        k = K(nc, es)
        PE, ACT, DVE, POOL, SP = k.PE, k.ACT, k.DVE, k.POOL, k.SP
        PS = []
        for i in range(8):
            t = es.enter_context(nc.psum_tensor(f"ps{i}", [128, 512], F32))
            PS.append(Buf(f"ps{i}", t))

        def mm(out, lhsT, rhs, start, stop, reads, writes):
            PE.op(lambda: nc.tensor.matmul(out, lhsT, rhs, start=start, stop=stop),
                  reads=reads, writes=writes, signal=stop)

        ones = k.sb("ones", [128, 128], BF16, persistent=True)
        epsT = k.sb("eps", [128, 1], F32, persistent=True)
        gv = k.sb("gv", [128, 7, 8, 2], F32, persistent=True)
        modT = [k.sb(f"mod{l}", [128, 72, 2], F32, persistent=True) for l in range(DEPTH)]
        Gs = [[k.sb(f"G{l}{s}", [128, 8, 2], F32, persistent=True) for s in range(3)] for l in range(DEPTH)]
        Gt = [[k.sb(f"Gt{l}{s}", [128, 8, 2], F32, persistent=True) for s in range(3)] for l in range(DEPTH)]
        ccdummy = k.sb("ccdummy", [128, 2], F32, persistent=True)
        DVE.op(lambda: nc.vector.memset(ones[:], 1.0), writes=[ones])
        DVE.op(lambda: nc.vector.memset(epsT[:], EPS), writes=[epsT])
        SP.dma(gv[:], gvec, writes=[gv], sbuf=gv)
        oh = k.sb("oh", [128, 2], F32, persistent=True)
        SP.dma(oh[:], oh_in, writes=[oh], sbuf=oh)
        cc_sem = k._sem("s_cc")
        cc_cnt = [0]

        def all_gather(pieces):
            with k.phase():
                k.barrier()
                for (src, dst) in pieces:
                    ins = nc.gpsimd.collective_compute("AllGather", ALU.bypass, replica_groups=[[0, 1], [2, 3], [4, 5], [6, 7]],
                                                       ins=[src], outs=[dst])
                    cc_cnt[0] += 1
                    ins.then_inc(cc_sem, 1)
                    nc.gpsimd.wait_ge(cc_sem, cc_cnt[0])
                POOL.op(lambda: nc.gpsimd.memset(ccdummy[:], 0.0), writes=[ccdummy])

        with k.phase():
            cc = k.sb("cc", [128, 8, 2], F32)
            scb = k.sb("scb", [128, 8, 2], BF16)
            bT = [k.sb(f"bT{l}", [128, 72, 2], F32) for l in range(DEPTH)]
            wsl = [k.sb(f"wada{i}", [128, 8, 1024], BF16) for i in range(2)]
            tmp1 = k.sb("tmp1", [128, 8, 2], F32)
            SP.dma(cc[:], ccT, writes=[cc], sbuf=cc)
            for l in range(DEPTH):
                SP.dma(bT[l][:], b_adaT[l], writes=[bT[l]], sbuf=bT[l])
            ACT.op(lambda: nc.scalar.activation(out=scb[:], in_=cc[:], func=AF.Silu), reads=[cc], writes=[scb])
            it = 0
            for l in range(DEPTH):
                wr = w_ada[l].rearrange("(k p) n -> p k n", p=128)
                ps = PS[l]
                for s9 in range(9):
                    wb = wsl[it % 2]
                    it += 1
                    POOL.dma(wb[:], wr[:, :, s9 * 1024:(s9 + 1) * 1024], writes=[wb], sbuf=wb)
                    for jj in range(8):
                        j = s9 * 8 + jj
                        for kk in range(8):
                            mm(ps[:, 2 * j:2 * j + 2], wb[:, kk, jj * 128:(jj + 1) * 128], scb[:, kk, :],
                               kk == 0, kk == 7, [wb, scb], [ps])
                DVE.op(lambda: nc.vector.tensor_tensor(out=modT[l][:].rearrange("p a b -> p (a b)"), in0=ps[:, 0:144],
                                                       in1=bT[l][:].rearrange("p a b -> p (a b)"), op=ALU.add),
                       reads=[ps, bT[l]], writes=[modT[l]])
                for s in range(3):
                    sc = modT[l][:, (3 * s + 1) * 8:(3 * s + 2) * 8, :]
                    DVE.op(lambda: nc.vector.tensor_scalar(out=tmp1[:], in0=sc, scalar1=1.0, scalar2=None, op0=ALU.add),
                           reads=[modT[l]], writes=[tmp1])
                    DVE.op(lambda: nc.vector.tensor_tensor(out=Gs[l][s][:], in0=tmp1[:], in1=gv[:, l * 3 + s, :, :], op=ALU.mult),
                           reads=[tmp1, gv], writes=[Gs[l][s]])
                    gt = modT[l][:, (3 * s + 2) * 8:(3 * s + 3) * 8, :]
                    fac = 1.0 if s == 1 else 0.5
                    DVE.op(lambda: nc.vector.tensor_scalar(out=Gt[l][s][:], in0=gt, scalar1=fac, scalar2=None, op0=ALU.mult),
                           reads=[modT[l]], writes=[Gt[l][s]])

        def shiftv(l, s):
            return modT[l][:, (3 * s) * 8:(3 * s + 1) * 8, :]

        def norm_phase(hsrc, hsrc_name, Gbuf, Gap, Shbuf, Shap, tiles, dst, dst_name, final=False, send=None, lean=False):
            hr = hsrc.rearrange("(k p) t -> p k t", p=128)
            dr = dst.rearrange("(k p) t -> p k t", p=128)
            with k.phase():
                nb_ = 1 if lean else 2
                hin = [k.sb(f"hin{i}", [128, 8, 512], F32) for i in range(2)]
                sq = [k.sb(f"sq{i}", [128, 8, 512], BF16) for i in range(nb_)] * (3 - nb_)
                tmp = [k.sb(f"tmp{i}", [128, 8, 512], F32) for i in range(nb_)] * (3 - nb_)
                uo = None if final else [k.sb(f"uo{i}", [128, 8, 512], BF16) for i in range(nb_)] * (3 - nb_)
                sd = [k.sb(f"sd{i}", [128, 512], F32) for i in range(nb_)] * (3 - nb_)
                rs = [k.sb(f"rs{i}", [128, 512], F32) for i in range(nb_)] * (3 - nb_)
                def _nload(ti):
                    c0, W, cls = tiles[ti]
                    b = ti % 2
                    SP.dma(hin[b][:, :, :W], hr[:, :, c0:c0 + W], reads=[k.dbuf(hsrc_name, c0)], writes=[hin[b]], sbuf=hin[b])

                _nload(0)
                for ti, (c0, W, cls) in enumerate(tiles):
                    b = ti % 2
                    ps = PS[ti % 2]
                    tix = c0
                    if ti + 1 < len(tiles):
                        _nload(ti + 1)
                    ACT.op(lambda: nc.scalar.activation(out=sq[b][:, :, :W], in_=hin[b][:, :, :W], func=AF.Square),
                           reads=[hin[b]], writes=[sq[b]])
                    for kk in range(8):
                        mm(ps[:, :W], ones[:], sq[b][:, kk, :W], kk == 0, kk == 7, [ones, sq[b]], [ps])
                    ACT.op(lambda: nc.scalar.activation(out=sd[b][:, :W], in_=ps[:, :W], func=AF.Sqrt, bias=epsT[:, 0:1], scale=1.0 / D),
                           reads=[ps, epsT], writes=[sd[b]])
                    DVE.op(lambda: nc.vector.reciprocal(out=rs[b][:, :W], in_=sd[b][:, :W]), reads=[sd[b]], writes=[rs[b]])
                    for kk in range(8):
                        DVE.op(lambda: nc.vector.scalar_tensor_tensor(out=tmp[b][:, kk, :W], in0=hin[b][:, kk, :W],
                                                                      scalar=Gap(kk, cls), in1=rs[b][:, :W],
                                                                      op0=ALU.mult, op1=ALU.mult),
                               reads=[hin[b], rs[b], Gbuf], writes=[tmp[b]])
                    if final:
                        SP.dma(dr[:, :, c0 - LC:c0 - LC + W], tmp[b][:, :, :W], reads=[tmp[b]], writes=[k.dbuf(dst_name, tix)], sbuf=tmp[b])
                    else:
                        for kk in range(8):
                            ACT.op(lambda: nc.scalar.activation(out=uo[b][:, kk, :W], in_=tmp[b][:, kk, :W], func=AF.Identity,
                                                                bias=Shap(kk, cls), scale=1.0),
                                   reads=[tmp[b], Shbuf], writes=[uo[b]])
                        SP.dma(dr[:, :, c0:c0 + W], uo[b][:, :, :W], reads=[uo[b]], writes=[k.dbuf(dst_name, tix)], sbuf=uo[b])
                        if send is not None and cls == 0:
                            sr = send.rearrange("(k p) t -> p k t", p=128)
                            SP.dma(sr[:, :, c0 - LC:c0 - LC + W], uo[b][:, :, :W], reads=[uo[b]], writes=[k.dbuf("uTs", tix)], sbuf=uo[b])

        def w13_alloc_load(w13):
            wr = w13.rearrange("(k p) n -> p k n", p=128)
            wa = [k.sb(f"wa{i}", [128, 8, 1408], BF16) for i in range(2)]
            wb = [k.sb(f"wb{i}", [128, 8, 1408], BF16) for i in range(2)]
            for s in range(2):
                POOL.dma(wa[s][:], wr[:, :, s * 1408:(s + 1) * 1408], writes=[wa[s]], sbuf=wa[s])
                POOL.dma(wb[s][:], wr[:, :, DFF + s * 1408:DFF + (s + 1) * 1408], writes=[wb[s]], sbuf=wb[s])
            return wa, wb

        def w2_load(w2, w):
            wr = w2.rearrange("(j p) n -> p j n", p=128)
            POOL.dma(w[:, 0:11, :], wr[:, 0:11, :], writes=[w], sbuf=w)
            POOL.dma(w[:, 11:22, :], wr[:, 11:22, :], writes=[w], sbuf=w)

        def w13_phase(w13, tiles, pre=None):
            wr = w13.rearrange("(k p) n -> p k n", p=128)
            ur = uT.rearrange("(k p) t -> p k t", p=128)
            gr = gT.rearrange("(j p) t -> p j t", p=128)
            with k.phase():
                wa, wb = pre
                uin = [k.sb(f"uin{i}", [128, 8, 512], BF16) for i in range(2)]
                sg = [k.sb(f"sg{i}", [128, 512], F32) for i in range(2)]
                go = [k.sb(f"go{i}", [128, 11, 512], BF16) for i in range(2)]
                it = 0
                pi = 0
                seq = [(s_, t_) for s_ in range(2) for t_ in tiles]

                def _uload(i):
                    (c0, W, cls) = seq[i][1]
                    SP.dma(uin[i % 2][:, :, :W], ur[:, :, c0:c0 + W], reads=[k.dbuf("uT", c0)], writes=[uin[i % 2]], sbuf=uin[i % 2])

                _uload(0)
                for s in range(2):
                    for (c0, W, cls) in tiles:
                        b = it % 2
                        it += 1
                        if it < len(seq):
                            _uload(it)
                        for j in range(11):
                            pa = PS[(2 * pi) % 8]
                            pb = PS[(2 * pi + 1) % 8]
                            sgb = sg[pi % 2]
                            pi += 1
                            for kk in range(8):
                                mm(pa[:, :W], wa[s][:, kk, j * 128:(j + 1) * 128], uin[b][:, kk, :W], kk == 0, kk == 7, [wa[s], uin[b]], [pa])
                            for kk in range(8):
                                mm(pb[:, :W], wb[s][:, kk, j * 128:(j + 1) * 128], uin[b][:, kk, :W], kk == 0, kk == 7, [wb[s], uin[b]], [pb])
                            ACT.op(lambda: nc.scalar.activation(out=sgb[:, :W], in_=pa[:, :W], func=AF.Silu), reads=[pa], writes=[sgb])
                            DVE.op(lambda: nc.vector.tensor_tensor(out=go[b][:, j, :W], in0=sgb[:, :W], in1=pb[:, :W], op=ALU.mult),
                                   reads=[sgb, pb], writes=[go[b]])
                        SP.dma(gr[:, s * 11:(s + 1) * 11, c0:c0 + W], go[b][:, :, :W], reads=[go[b]], writes=[k.dbuf("gT", (s, c0))], sbuf=go[b])

        def w2_phase(w2, hsrc, hsrc_name, gate, tiles, pre=None):
            wr = w2.rearrange("(j p) n -> p j n", p=128)
            gr = gT.rearrange("(j p) t -> p j t", p=128)
            hr = hsrc.rearrange("(k p) t -> p k t", p=128)
            ho = hT.rearrange("(k p) t -> p k t", p=128)
            with k.phase():
                w = pre
                gin = [k.sb(f"gin{i}", [128, 22, 512], BF16) for i in range(2)]
                hin = [k.sb(f"hin{i}", [128, 8, 512], F32) for i in range(2)]
                pi = 0

                def _wload(ti):
                    c0, W, cls = tiles[ti]
                    b = ti % 2
                    SP.dma(gin[b][:, :, :W], gr[:, :, c0:c0 + W], reads=[k.dbuf("gT", (0, c0)), k.dbuf("gT", (1, c0))], writes=[gin[b]], sbuf=gin[b])
                    SP.dma(hin[b][:, :, :W], hr[:, :, c0:c0 + W], reads=[k.dbuf(hsrc_name, c0)], writes=[hin[b]], sbuf=hin[b])

                _wload(0)
                for ti, (c0, W, cls) in enumerate(tiles):
                    b = ti % 2
                    if ti + 1 < len(tiles):
                        _wload(ti + 1)
                    for n in range(8):
                        ps = PS[pi % 8]
                        pi += 1
                        for j in range(22):
                            mm(ps[:, :W], w[:, j, n * 128:(n + 1) * 128], gin[b][:, j, :W], j == 0, j == 21, [w, gin[b]], [ps])
                        DVE.op(lambda: nc.vector.scalar_tensor_tensor(out=hin[b][:, n, :W], in0=ps[:, :W], scalar=gate[:, n, cls:cls + 1],
                                                                      in1=hin[b][:, n, :W], op0=ALU.mult, op1=ALU.add),
                               reads=[ps, hin[b], gate], writes=[hin[b]])
                    SP.dma(ho[:, :, c0:c0 + W], hin[b][:, :, :W], reads=[hin[b]], writes=[k.dbuf("hT", c0)], sbuf=hin[b])

        def win_gl_phase(l, tiles):
            wr = w_in_gl[l].rearrange("(k p) n -> p k n", p=128)
            ur = uT.rearrange("(k p) t -> p k t", p=128)
            pr = glT.rearrange("(j p) t -> p j t", p=128)
            with k.phase():
                ws = [k.sb(f"ws{i}", [128, 8, 1024], BF16) for i in range(3)]
                uin = [k.sb(f"uin{i}", [128, 8, 512], BF16) for i in range(2)]
                po = [k.sb(f"po{i}", [128, 8, 512], BF16) for i in range(2)]
                for si in range(3):
                    POOL.dma(ws[si][:], wr[:, :, si * 1024:(si + 1) * 1024], writes=[ws[si]], sbuf=ws[si])
                it = 0
                pi = 0
                ei = 0
                seq = [(s_, t_) for s_ in range(3) for t_ in tiles]

                def _gload(i):
                    (c0, W, cls) = seq[i][1]
                    SP.dma(uin[i % 2][:, :, :W], ur[:, :, c0:c0 + W], reads=[k.dbuf("uT", c0)], writes=[uin[i % 2]], sbuf=uin[i % 2])

                _gload(0)
                for si in range(3):
                    wsb = ws[si]
                    for (c0, W, cls) in tiles:
                        b = it % 2
                        it += 1
                        if it < len(seq):
                            _gload(it)
                        for jj in range(8):
                            ps = PS[pi % 8]
                            pi += 1
                            for kk in range(8):
                                mm(ps[:, :W], wsb[:, kk, jj * 128:(jj + 1) * 128], uin[b][:, kk, :W], kk == 0, kk == 7, [wsb, uin[b]], [ps])
                            if ei % 2 == 0:
                                ACT.op(lambda: nc.scalar.copy(out=po[b][:, jj, :W], in_=ps[:, :W]), reads=[ps], writes=[po[b]])
                            else:
                                DVE.op(lambda: nc.vector.tensor_copy(out=po[b][:, jj, :W], in_=ps[:, :W]), reads=[ps], writes=[po[b]])
                            ei += 1
                        SP.dma(pr[:, si * 8:(si + 1) * 8, c0:c0 + W], po[b][:, :, :W], reads=[po[b]], writes=[k.dbuf("glT", (si, c0))], sbuf=po[b])

        def win_own_phase(l):
            wr = w_in_own[l].rearrange("(k p) n -> p k n", p=128)
            ur = uT.rearrange("(k p) t -> p k t", p=128)
            ug = uTg.rearrange("(q r k p) t -> p q r k t", p=128, k=4, r=2)
            pr = pT[0:1024, :].rearrange("(j p) t -> p j t", p=128)
            with k.phase():
                wsb = k.sb("wown", [128, 8, 1408], BF16)
                uin = [k.sb(f"uin{i}", [128, 8, 512], BF16) for i in range(2)]
                po = [k.sb(f"po{i}", [128, 9, 512], BF16) for i in range(2)]
                vo = [k.sb(f"vo{i}", [128, 320], BF16) for i in range(2)]
                POOL.dma(wsb[:], wr, writes=[wsb], sbuf=wsb)
                pi = 0
                ei = 0
                def _oload(ti):
                    c0, W, cls = GTILES[ti]
                    b = ti % 2
                    if cls == 1:
                        SP.dma(uin[b][:, :, :W], ur[:, :, 0:W], reads=[k.dbuf("uT", 0)], writes=[uin[b]], sbuf=uin[b])
                    else:
                        g = (c0 - LC) // 512
                        for q in range(2):
                            SP.dma(uin[b][:, 4 * q:4 * q + 4, :W], ug[:, q, g // 4, :, (g % 4) * 512:(g % 4) * 512 + W], reads=[k.dbuf("uTg", 0)], writes=[uin[b]], sbuf=uin[b])

                _oload(0)
                for ti, (c0, W, cls) in enumerate(GTILES):
                    b = ti % 2
                    if ti + 1 < len(GTILES):
                        _oload(ti + 1)
                    for jj in range(9):
                        ps = PS[pi % 6]
                        pi += 1
                        if jj < 8:
                            cs, M = FM_COLS[jj], 128
                        else:
                            cs, M = 1280, 64
                        for kk in range(8):
                            mm(ps[0:M, :W], wsb[:, kk, cs:cs + M], uin[b][:, kk, :W], kk == 0, kk == 7, [wsb, uin[b]], [ps])
                        if ei % 2 == 0:
                            ACT.op(lambda: nc.scalar.copy(out=po[b][0:M, jj, :W], in_=ps[0:M, :W]), reads=[ps], writes=[po[b]])
                        else:
                            DVE.op(lambda: nc.vector.tensor_copy(out=po[b][0:M, jj, :W], in_=ps[0:M, :W]), reads=[ps], writes=[po[b]])
                        ei += 1
                    SP.dma(pr[:, :, c0:c0 + W], po[b][:, 0:8, :W], reads=[po[b]], writes=[k.dbuf("pT", c0)], sbuf=po[b])
                    SP.dma(pT[1024:1088, c0:c0 + W], po[b][0:64, 8, :W], reads=[po[b]], writes=[k.dbuf("pT", c0)], sbuf=po[b])
                    for tb in range(W // 128):
                        vb = vo[tb % 2]
                        p0 = PS[6 + tb % 2]
                        for kk in range(8):
                            mm(p0[:, 0:256], uin[b][:, kk, tb * 128:(tb + 1) * 128], wsb[:, kk, 512:768], kk == 0, kk == 7, [wsb, uin[b]], [p0])
                        for kk in range(8):
                            mm(p0[:, 256:320], uin[b][:, kk, tb * 128:(tb + 1) * 128], wsb[:, kk, 1344:1408], kk == 0, kk == 7, [wsb, uin[b]], [p0])
                        if tb % 2 == 0:
                            ACT.op(lambda: nc.scalar.copy(out=vb[:, 0:320], in_=p0[:, 0:320]), reads=[p0], writes=[vb])
                        else:
                            DVE.op(lambda: nc.vector.tensor_copy(out=vb[:, 0:320], in_=p0[:, 0:320]), reads=[p0], writes=[vb])
                        t0 = c0 + tb * 128
                        SP.dma(Vtok[t0:t0 + 128, :], vb[:], reads=[vb], writes=[k.dbuf("Vtok", 0)], sbuf=vb)

        def na_phase(l, with_ctx_q):
            with k.phase():
                ebm = k.sb("ebm", [128, 8, 4, 256], BF16)
                msk = k.sb("msk", [128, 8, 256], F32)
                bl = [k.sb(f"bl{i}", [128, 4, 256], F32) for i in range(2)]
                SP.dma(msk[:], na_mask, writes=[msk], sbuf=msk)
                for cls in range(8):
                    b = bl[cls % 2]
                    SP.dma(b[:], na_biasG[l, :, cls, :, :], writes=[b], sbuf=b)
                    ACT.op(lambda: nc.scalar.activation(out=b[:], in_=b[:], func=AF.Exp), reads=[b], writes=[b])
                    for h in range(4):
                        DVE.op(lambda: nc.vector.tensor_tensor(out=ebm[:, cls, h, :], in0=b[:, h, :], in1=msk[:, cls, :], op=ALU.mult),
                               reads=[b, msk], writes=[ebm])
                KT = [k.sb(f"KT{i}", [64, NT], BF16) for i in range(2)]
                QT = [k.sb(f"QT{i}", [64, NT], BF16) for i in range(2)]
                VE = [k.sb(f"VE{i}", [128, 34, 128], BF16) for i in range(2)]
                VO = [k.sb(f"VO{i}", [128, 33, 128], BF16) for i in range(2)]
                AO = [k.sb(f"AO{i}", [64, NT], BF16) for i in range(2)]
                E = [k.sb(f"E{i}", [128, 512], BF16) for i in range(4)]
                rec = [k.sb(f"rec{i}", [128, 256], F32) for i in range(2)]
                for i in range(2):
                    DVE.op(lambda: nc.vector.memset(VE[i][:], 1.0), writes=[VE[i]])
                    DVE.op(lambda: nc.vector.memset(VO[i][:], 1.0), writes=[VO[i]])
                vr = Vtok.rearrange("(c p) f -> p c f", p=128)
                vro = Vtok[64:64 + 33 * 128, :].rearrange("(c p) f -> p c f", p=128)
                items = []
                for h in range(4):
                    for r in range(64):
                        items.append((h, r))
                    if with_ctx_q:
                        items.append((h, -1))
                loaded = set()

                def load_head(h):
                    b = h % 2
                    SP.dma(KT[b][:], pT[R_NK + h * 64:R_NK + (h + 1) * 64, :], reads=[k.dbuf("pT", "all")], writes=[KT[b]], sbuf=KT[b])
                    SP.dma(QT[b][:], pT[R_NQ + h * 64:R_NQ + (h + 1) * 64, :], reads=[k.dbuf("pT", "all")], writes=[QT[b]], sbuf=QT[b])
                    for (ca, cb) in ((0, 9), (9, 18), (18, 27), (27, 34)):
                        SP.dma(VE[b][:, ca:cb, 0:64], vr[:, ca:cb, h * 64:(h + 1) * 64], reads=[k.dbuf("Vtok", 0)], writes=[VE[b]], sbuf=VE[b])
                    for (ca, cb) in ((0, 9), (9, 18), (18, 27), (27, 33)):
                        SP.dma(VO[b][:, ca:cb, 0:64], vro[:, ca:cb, h * 64:(h + 1) * 64], reads=[k.dbuf("Vtok", 0)], writes=[VO[b]], sbuf=VO[b])

                def stage1(idx):
                    h, r = items[idx]
                    b = h % 2
                    if h not in loaded:
                        loaded.add(h)
                        load_head(h)
                    ps_s = PS[idx % 4]
                    Eb = E[idx % 4]
                    if r >= 0:
                        rs_ = min(max(r - 4, 0), 56)
                        cls = r - rs_
                        tok0 = LC + rs_ * 64
                        q = QT[b][:, LC + r * 64:LC + (r + 1) * 64]
                        for c in range(4):
                            mm(ps_s[:, c * 64:(c + 1) * 64], KT[b][:, tok0 + c * 128:tok0 + (c + 1) * 128], q, True, True, [KT[b], QT[b]], [ps_s])
                        for c in range(2):
                            mm(ps_s[:, 256 + c * 64:256 + (c + 1) * 64], KT[b][:, c * 128:(c + 1) * 128], q, True, True, [KT[b], QT[b]], [ps_s])
                        ACT.op(lambda: nc.scalar.activation(out=Eb[:, 0:384], in_=ps_s[:, 0:384], func=AF.Exp, scale=0.125), reads=[ps_s], writes=[Eb])
                        DVE.op(lambda: nc.vector.tensor_tensor(out=Eb[:, 0:256], in0=Eb[:, 0:256], in1=ebm[:, cls, h, :], op=ALU.mult),
                               reads=[Eb, ebm], writes=[Eb])
                    else:
                        for c in range(2):
                            mm(ps_s[:, c * 256:(c + 1) * 256], KT[b][:, c * 128:(c + 1) * 128], QT[b][:, 0:256], True, True, [KT[b], QT[b]], [ps_s])
                        ACT.op(lambda: nc.scalar.activation(out=Eb[:, 0:512], in_=ps_s[:, 0:512], func=AF.Exp, scale=0.125), reads=[ps_s], writes=[Eb])

                def stage2(idx):
                    h, r = items[idx]
                    b = h % 2
                    Eb = E[idx % 4]
                    if r >= 0:
                        gi = (h * 16 + r // 4)
                        ps_o = PS[4 + gi % 3]
                        rc = rec[gi % 2]
                        jo = (r % 4) * 64
                        rs_ = min(max(r - 4, 0), 56)
                        for c in range(6):
                            if c < 4:
                                if rs_ % 2 == 0:
                                    vsrc, vb_ = VE[b][:, 2 + rs_ // 2 + c, :], VE[b]
                                else:
                                    vsrc, vb_ = VO[b][:, 2 + (rs_ - 1) // 2 + c, :], VO[b]
                            else:
                                vsrc, vb_ = VE[b][:, c - 4, :], VE[b]
                            mm(ps_o[:, jo:jo + 64], vsrc, Eb[:, c * 64:(c + 1) * 64], c == 0, c == 5, [vb_, Eb], [ps_o])
                        if r % 4 == 3:
                            DVE.op(lambda: nc.vector.reciprocal(out=rc[64:128, 0:256], in_=ps_o[64:128, 0:256]), reads=[ps_o], writes=[rc])
                            DVE.op(lambda: nc.vector.tensor_tensor(out=AO[b][:, LC + (r - 3) * 64:LC + (r + 1) * 64], in0=ps_o[0:64, 0:256], in1=rc[64:128, 0:256], op=ALU.mult),
                                   reads=[ps_o, rc], writes=[AO[b]])
                    else:
                        ps_o = PS[7]
                        rc = rec[idx % 2]
                        for c in range(2):
                            mm(ps_o[:, 0:256], VE[b][:, c, :], Eb[:, c * 256:(c + 1) * 256], c == 0, c == 1, [VE[b], Eb], [ps_o])
                        DVE.op(lambda: nc.vector.reciprocal(out=rc[64:128, 0:256], in_=ps_o[64:128, 0:256]), reads=[ps_o], writes=[rc])
                        DVE.op(lambda: nc.vector.tensor_tensor(out=AO[b][:, 0:256], in0=ps_o[0:64, 0:256], in1=rc[64:128, 0:256], op=ALU.mult),
                               reads=[ps_o, rc], writes=[AO[b]])
                    last_of_head = (idx + 1 == len(items)) or items[idx + 1][0] != h
                    if last_of_head:
                        if with_ctx_q:
                            SP.dma(brT[0, h * 64:(h + 1) * 64, :], AO[b][:], reads=[AO[b]], writes=[k.dbuf("brT", 0)], sbuf=AO[b])
                        else:
                            SP.dma(brT[0, h * 64:(h + 1) * 64, LC:NT], AO[b][:, LC:NT], reads=[AO[b]], writes=[k.dbuf("brT", 0)], sbuf=AO[b])

                LAG = 2
                for idx in range(len(items) + LAG):
                    if idx < len(items):
                        stage1(idx)
                    if idx - LAG >= 0:
                        stage2(idx - LAG)

        def wa_phase(l, with_ctx_q):
            with k.phase():
                cosT = k.sb("cosT", [64, T], F32)
                sinT = k.sb("sinT", [64, T], F32)
                tri = k.sb("tri", [128, 2, 512], F32)
                trib = k.sb("trib", [128, 2, 512], BF16)
                es_ = k.sb("esink", [128, 512], F32)
                SP.dma(cosT[:], ropeC, writes=[cosT], sbuf=cosT)
                SP.dma(sinT[:], ropeS, writes=[sinT], sbuf=sinT)
                SP.dma(tri[:], trimask, writes=[tri], sbuf=tri)
                SP.dma(es_[:], sinkG[l], writes=[es_], sbuf=es_)
                DVE.op(lambda: nc.vector.tensor_copy(out=trib[:], in_=tri[:]), reads=[tri], writes=[trib])
                ACT.op(lambda: nc.scalar.activation(out=es_[:], in_=es_[:], func=AF.Exp), reads=[es_], writes=[es_])
                A = [k.sb(f"A{i}", [64, NT], BF16) for i in range(1)]
                Bm = [k.sb(f"B{i}", [64, NT], BF16) for i in range(1)]
                t1 = [k.sb(f"t1{i}", [64, T], F32) for i in range(1)]
                KT = k.sb("KTw", [64, NT], BF16)
                QT = k.sb("QTw", [64, 4, NT], BF16)
                VE = k.sb("VEw", [128, 34, 128], BF16)
                WO = k.sb("WO", [64, 4, NT], BF16)
                E = [k.sb(f"Ew{i}", [128, 512], BF16) for i in range(12)]
                den = [k.sb(f"den{i}", [128, 512], F32) for i in range(2)]
                DVE.op(lambda: nc.vector.memset(VE[:], 1.0), writes=[VE])
                vr = Vtok.rearrange("(c p) f -> p c f", p=128)
                li = 0

                def load_rope(row0, dst_ap, dstbuf):
                    nonlocal li
                    a = A[0]
                    bm = Bm[0]
                    tt = t1[0]
                    li += 1
                    SP.dma(a[:], pT[row0:row0 + 64, :], reads=[k.dbuf("pT", "all")], writes=[a], sbuf=a)
                    SP.dma(bm[0:32, :], pT[row0 + 32:row0 + 64, :], reads=[k.dbuf("pT", "all")], writes=[bm], sbuf=bm)
                    SP.dma(bm[32:64, :], pT[row0:row0 + 32, :], reads=[k.dbuf("pT", "all")], writes=[bm], sbuf=bm)
                    DVE.op(lambda: nc.vector.tensor_copy(out=dst_ap[:, 0:LC], in_=a[:, 0:LC]), reads=[a], writes=[dstbuf])
                    DVE.op(lambda: nc.vector.tensor_tensor(out=tt[:], in0=a[:, LC:NT], in1=cosT[:], op=ALU.mult), reads=[a, cosT], writes=[tt])
                    DVE.op(lambda: nc.vector.tensor_tensor(out=bm[:, LC:NT], in0=bm[:, LC:NT], in1=sinT[:], op=ALU.mult), reads=[bm, sinT], writes=[bm])
                    DVE.op(lambda: nc.vector.tensor_tensor(out=dst_ap[:, LC:NT], in0=tt[:], in1=bm[:, LC:NT], op=ALU.add), reads=[tt, bm], writes=[dstbuf])

                load_rope(R_WK, KT[:], KT)
                for g in range(4):
                    load_rope(R_WQ + g * 64, QT[:, g, :], QT)
                for (ca, cb) in ((0, 9), (9, 18), (18, 27), (27, 34)):
                    SP.dma(VE[:, ca:cb, 0:64], vr[:, ca:cb, 256:320], reads=[k.dbuf("Vtok", 0)], writes=[VE], sbuf=VE)
                items = list(range(32)) + ([-1, -2] if with_ctx_q else [])
                ectr = [0]
                pend = {}

                def stage1(idx):
                    n = items[idx]
                    Es = []
                    if n >= 0:
                        chunks = []
                        if n > 0:
                            chunks.append((LC + (n - 1) * 128, 0))
                        chunks.append((LC + n * 128, None))
                        if n < 31:
                            chunks.append((LC + (n + 1) * 128, 1))
                        chunks.append((0, None))
                        chunks.append((128, None))
                        q = QT[:, :, LC + n * 128:LC + (n + 1) * 128]
                        for (kt0, mk) in chunks:
                            e = ectr[0]
                            ectr[0] += 1
                            ps_s = PS[e % 6]
                            Eb = E[e % 12]
                            mm(ps_s[:, :].rearrange("p (g q) -> p g q", g=4), KT[:, kt0:kt0 + 128], q, True, True, [KT, QT], [ps_s])
                            ACT.op(lambda: nc.scalar.activation(out=Eb[:], in_=ps_s[:], func=AF.Exp, scale=0.125), reads=[ps_s], writes=[Eb])
                            if mk is not None:
                                DVE.op(lambda: nc.vector.tensor_tensor(out=Eb[:], in0=Eb[:], in1=trib[:, mk, :], op=ALU.mult), reads=[Eb, trib], writes=[Eb])
                            Es.append((Eb, kt0 // 128))
                    else:
                        half = -n - 1
                        q = QT[:, 2 * half:2 * half + 2, 0:LC]
                        for c in range(2):
                            e = ectr[0]
                            ectr[0] += 1
                            ps_s = PS[e % 6]
                            Eb = E[e % 12]
                            mm(ps_s[:, :].rearrange("p (g q) -> p g q", g=2), KT[:, c * 128:(c + 1) * 128], q, True, True, [KT, QT], [ps_s])
                            ACT.op(lambda: nc.scalar.activation(out=Eb[:], in_=ps_s[:], func=AF.Exp, scale=0.125), reads=[ps_s], writes=[Eb])
                            Es.append((Eb, c))
                    pend[idx] = Es

                def stage2(idx):
                    n = items[idx]
                    Es = pend.pop(idx)
                    ps_o = PS[6 + idx % 2]
                    dn = den[idx % 2]
                    for ci, (Eb, vc) in enumerate(Es):
                        mm(ps_o[:, :], VE[:, vc, :], Eb[:], ci == 0, ci == len(Es) - 1, [VE, Eb], [ps_o])
                    if n >= 0:
                        DVE.op(lambda: nc.vector.tensor_tensor(out=dn[64:128, :], in0=ps_o[64:128, :], in1=es_[64:128, :], op=ALU.add), reads=[ps_o, es_], writes=[dn])
                        DVE.op(lambda: nc.vector.reciprocal(out=dn[64:128, :], in_=dn[64:128, :]), reads=[dn], writes=[dn])
                        DVE.op(lambda: nc.vector.tensor_tensor(out=WO[:, :, LC + n * 128:LC + (n + 1) * 128],
                                                               in0=ps_o[0:64, :].rearrange("p (g q) -> p g q", g=4),
                                                               in1=dn[64:128, :].rearrange("p (g q) -> p g q", g=4), op=ALU.mult),
                               reads=[ps_o, dn], writes=[WO])
                    else:
                        half = -n - 1
                        for gg in range(2):
                            g = 2 * half + gg
                            DVE.op(lambda: nc.vector.tensor_scalar(out=dn[64:128, gg * 256:(gg + 1) * 256], in0=ps_o[64:128, gg * 256:(gg + 1) * 256],
                                                                   scalar1=es_[64:128, g * 128:g * 128 + 1], scalar2=None, op0=ALU.add),
                                   reads=[ps_o, es_], writes=[dn])
                        DVE.op(lambda: nc.vector.reciprocal(out=dn[64:128, :], in_=dn[64:128, :]), reads=[dn], writes=[dn])
                        DVE.op(lambda: nc.vector.tensor_tensor(out=WO[:, 2 * half:2 * half + 2, 0:LC],
                                                               in0=ps_o[0:64, :].rearrange("p (g q) -> p g q", g=2),
                                                               in1=dn[64:128, :].rearrange("p (g q) -> p g q", g=2), op=ALU.mult),
                               reads=[ps_o, dn], writes=[WO])

                LAG = 1
                for idx in range(len(items) + LAG):
                    if idx < len(items):
                        stage1(idx)
                    if idx - LAG >= 0:
                        stage2(idx - LAG)
                c_lo = 0 if with_ctx_q else LC
                dst = brT[2, :, c_lo:NT].rearrange("(g p) t -> p g t", p=64)
                SP.dma(dst, WO[:, :, c_lo:NT], reads=[WO], writes=[k.dbuf("brT", 2)], sbuf=WO)

        def fourier_phase(with_ctx):
            fr = pT[R_FU:R_FU + 256, :].rearrange("(j p) t -> p j t", p=128)
            fo = brT[1].rearrange("(j p) t -> p j t", p=128)
            with k.phase():
                cs32 = k.sb("cs32", [128, 256], F32)
                csb = k.sb("csb", [128, 256], BF16)
                fu = k.sb("fu", [128, 2, NT], BF16)
                ucs = k.sb("ucs", [128, 34, 2, 256], BF16)
                tb = [[k.sb(f"tb{i}{j}", [128, 32, 256], BF16) for j in range(2)] for i in range(2)]
                tb2 = [k.sb(f"tb2{j}", [128, 2, 256], BF16) for j in range(2)]
                fo_sb = [k.sb(f"fo{i}", [128, 2, 256], BF16) for i in range(2)]
                SP.dma(cs32[:], cs64, writes=[cs32], sbuf=cs32)
                DVE.op(lambda: nc.vector.tensor_copy(out=csb[:], in_=cs32[:]), reads=[cs32], writes=[csb])
                SP.dma(fu[:], fr, reads=[k.dbuf("pT", "all")], writes=[fu], sbuf=fu)
                ei = 0
                for c in range(34):
                    if c < 2 and not with_ctx:
                        continue
                    for half in range(1):
                        ps = PS[c % 4]
                        for jj in range(2):
                            j = jj
                            mm(ps[:, jj * 256:(jj + 1) * 256], fu[:, j, c * 128:(c + 1) * 128], csb[:], True, True, [fu, csb], [ps])
                        src = ps[:, :].rearrange("p (jj s f) -> p s jj f", jj=2, s=2)
                        dstv = ucs[:, c, :, :].rearrange("p s (jj f) -> p s jj f", jj=2)
                        if ei % 2 == 0:
                            ACT.op(lambda: nc.scalar.copy(out=dstv, in_=src), reads=[ps], writes=[ucs])
                        else:
                            DVE.op(lambda: nc.vector.tensor_copy(out=dstv, in_=src), reads=[ps], writes=[ucs])
                        ei += 1
                pi = 0
                for kb in range(16):
                    tC, tS = tb[kb % 2]
                    SP.dma(tC[:], dftC[kb], writes=[tC], sbuf=tC)
                    SP.dma(tS[:], dftS[kb], writes=[tS], sbuf=tS)
                    fb = fo_sb[kb % 2]
                    for j in range(2):
                        ps = PS[4 + pi % 4]
                        pi += 1
                        for c in range(32):
                            mm(ps[:, 0:256], ucs[:, 2 + c, 0, j * 128:(j + 1) * 128], tC[:, c, :], c == 0, False, [ucs, tC], [ps])
                            mm(ps[:, 0:256], ucs[:, 2 + c, 1, j * 128:(j + 1) * 128], tS[:, c, :], False, c == 31, [ucs, tS], [ps])
                        if j % 2 == 0:
                            ACT.op(lambda: nc.scalar.copy(out=fb[:, j, :], in_=ps[:, 0:256]), reads=[ps], writes=[fb])
                        else:
                            DVE.op(lambda: nc.vector.tensor_copy(out=fb[:, j, :], in_=ps[:, 0:256]), reads=[ps], writes=[fb])
                    SP.dma(fo[:, :, LC + kb * 256:LC + (kb + 1) * 256], fb[:], reads=[fb], writes=[k.dbuf("brT", 1)], sbuf=fb)
                if with_ctx:
                    tC, tS = tb2
                    SP.dma(tC[:], dftC2, writes=[tC], sbuf=tC)
                    SP.dma(tS[:], dftS2, writes=[tS], sbuf=tS)
                    fb = fo_sb[0]
                    for j in range(2):
                        ps = PS[4 + pi % 4]
                        pi += 1
                        for c in range(2):
                            mm(ps[:, 0:256], ucs[:, c, 0, j * 128:(j + 1) * 128], tC[:, c, :], c == 0, False, [ucs, tC], [ps])
                            mm(ps[:, 0:256], ucs[:, c, 1, j * 128:(j + 1) * 128], tS[:, c, :], False, c == 1, [ucs, tS], [ps])
                        DVE.op(lambda: nc.vector.tensor_copy(out=fb[:, j, :], in_=ps[:, 0:256]), reads=[ps], writes=[fb])
                    SP.dma(fo[:, :, 0:LC], fb[:], reads=[fb], writes=[k.dbuf("brT", 1)], sbuf=fb)

        def merge_phase(l, tiles):
            bg = brTg.rearrange("(i cc r p) t -> p r i cc t", p=128, cc=2, r=2)
            gr = glT.rearrange("(j p) t -> p j t", p=128)
            hr = hT.rearrange("(k p) t -> p k t", p=128)
            wbr_r = w_br[l].rearrange("i (c p) n -> p i c n", p=128)
            wo_r = w_out[l].rearrange("(k p) n -> p k n", p=128)
            gate = Gt[l][1]
            with k.phase():
                wbr = k.sb("wbr", [128, 3, 4, 1024], BF16)
                wo = k.sb("wo", [128, 8, 1024], BF16)
                bin_ = [k.sb(f"bin{i}", [128, 3, 4, 512], BF16) for i in range(2)]
                binB = [k.sb(f"binB{i}", [128, 3, 4, 512], BF16) for i in range(2)]
                gin = [k.sb(f"gin{i}", [128, 24, 512], BF16) for i in range(2)]
                hin = [k.sb(f"hin{i}", [128, 8, 512], F32) for i in range(2)]
                sg = [k.sb(f"sg{i}", [128, 512], F32) for i in range(3)]
                m32 = [k.sb(f"m32{i}", [128, 512], F32) for i in range(2)]
                tt = [k.sb(f"tt{i}", [128, 512], F32) for i in range(2)]
                mT = [k.sb(f"mT{i}", [128, 8, 512], BF16) for i in range(2)]
                for i in range(3):
                    POOL.dma(wbr[:, i, :, :], wbr_r[:, i, :, :], writes=[wbr], sbuf=wbr)
                POOL.dma(wo[:], wo_r, writes=[wo], sbuf=wo)
                pi = 0
                si = 0
                def _mload(ti):
                    c0, W, cls = tiles[ti]
                    b = ti % 2
                    if cls == 1:
                        for i in range(3):
                            for r in range(2):
                                SP.dma(bin_[b][:, i, 2 * r:2 * r + 2, :W], bg[:, r, i, :, 0:W], reads=[k.dbuf("brTg", 0)], writes=[bin_[b]], sbuf=bin_[b])
                    else:
                        ca = c0
                        cb = c0 + TH
                        for i in range(3):
                            for r in range(2):
                                SP.dma(bin_[b][:, i, 2 * r:2 * r + 2, :W], bg[:, r, i, :, ca:ca + W], reads=[k.dbuf("brTg", 0)], writes=[bin_[b]], sbuf=bin_[b])
                                SP.dma(binB[b][:, i, 2 * r:2 * r + 2, :W], bg[:, r, i, :, cb:cb + W], reads=[k.dbuf("brTg", 0)], writes=[binB[b]], sbuf=binB[b])
                    SP.dma(gin[b][:, :, :W], gr[:, :, c0:c0 + W], reads=[k.dbuf("glT", "all")], writes=[gin[b]], sbuf=gin[b])
                    SP.dma(hin[b][:, :, :W], hr[:, :, c0:c0 + W], reads=[k.dbuf("hT", c0)], writes=[hin[b]], sbuf=hin[b])

                _mload(0)
                for ti, (c0, W, cls) in enumerate(tiles):
                    b = ti % 2
                    if ti + 1 < len(tiles):
                        _mload(ti + 1)
                    if cls == 1:
                        pass
                    else:
                        DVE.op(lambda: nc.vector.tensor_scalar(out=bin_[b][:], in0=bin_[b][:], scalar1=oh[:, 0:1], scalar2=None, op0=ALU.mult),
                               reads=[bin_[b], oh], writes=[bin_[b]])
                        DVE.op(lambda: nc.vector.scalar_tensor_tensor(out=bin_[b][:], in0=binB[b][:], scalar=oh[:, 1:2], in1=bin_[b][:], op0=ALU.mult, op1=ALU.add),
                               reads=[bin_[b], binB[b], oh], writes=[bin_[b]])
                    for n in range(8):
                        mb = m32[n % 2]
                        for i in range(3):
                            ps = PS[pi % 5]
                            pi += 1
                            sgb = sg[si % 3]
                            tb_ = tt[si % 2]
                            si += 1
                            for c in range(4):
                                mm(ps[:, :W], wbr[:, i, c, n * 128:(n + 1) * 128], bin_[b][:, i, c, :W], c == 0, c == 3, [wbr, bin_[b]], [ps])
                            ACT.op(lambda: nc.scalar.activation(out=sgb[:, :W], in_=gin[b][:, i * 8 + n, :W], func=AF.Sigmoid), reads=[gin[b]], writes=[sgb])
                            if i == 0:
                                DVE.op(lambda: nc.vector.tensor_tensor(out=mb[:, :W], in0=sgb[:, :W], in1=ps[:, :W], op=ALU.mult), reads=[sgb, ps], writes=[mb])
                            else:
                                DVE.op(lambda: nc.vector.tensor_tensor(out=tb_[:, :W], in0=sgb[:, :W], in1=ps[:, :W], op=ALU.mult), reads=[sgb, ps], writes=[tb_])
                                if i == 1:
                                    DVE.op(lambda: nc.vector.tensor_tensor(out=mb[:, :W], in0=mb[:, :W], in1=tb_[:, :W], op=ALU.add), reads=[mb, tb_], writes=[mb])
                                else:
                                    DVE.op(lambda: nc.vector.tensor_tensor(out=mT[b][:, n, :W], in0=mb[:, :W], in1=tb_[:, :W], op=ALU.add), reads=[mb, tb_], writes=[mT[b]])
                    for n2 in range(8):
                        ps = PS[5 + n2 % 3]
                        for kk in range(8):
                            mm(ps[:, :W], wo[:, kk, n2 * 128:(n2 + 1) * 128], mT[b][:, kk, :W], kk == 0, kk == 7, [wo, mT[b]], [ps])
                        DVE.op(lambda: nc.vector.scalar_tensor_tensor(out=hin[b][:, n2, :W], in0=ps[:, :W], scalar=gate[:, n2, cls:cls + 1],
                                                                      in1=hin[b][:, n2, :W], op0=ALU.mult, op1=ALU.add),
                               reads=[ps, hin[b], gate], writes=[hin[b]])
                    SP.dma(hr[:, :, c0:c0 + W], hin[b][:, :, :W], reads=[hin[b]], writes=[k.dbuf("hT", c0)], sbuf=hin[b])

        def alias_all(name, keys):
            allb = k.dbuf(name, "all")
            for kk_ in keys:
                fb = k.dram.get((name, kk_))
                if fb is not None:
                    for (s, v, e) in fb.w.values():
                        Eng._reg(allb.w, s, v, e)

        steps = []

        def run():
            for l in range(DEPTH):
                last = l == DEPTH - 1
                tiles_all = TILES
                tiles_lat = TILES[1:]
                hsrc, hname = (h0T, "h0T") if l == 0 else (hT, "hT")
                with k.phase():
                    w2b = k.sb("w2", [128, 22, 1024], BF16)
                    with k.phase():
                        pre13 = w13_alloc_load(ffn_w13[0][l])
                        norm_phase(hsrc, hname, Gs[l][0], lambda kk, cls: Gs[l][0][:, kk, cls:cls + 1], modT[l],
                                   lambda kk, cls: shiftv(l, 0)[:, kk, cls:cls + 1], tiles_all, uT, "uT", lean=True)
                        w2_load(ffn_w2[0][l], w2b)
                        w13_phase(ffn_w13[0][l], tiles_all, pre=pre13)
                    w2_phase(ffn_w2[0][l], hsrc, hname, Gt[l][0], tiles_all, pre=w2b)
                yield f"ffn1_{l}"
                norm_phase(hT, "hT", Gs[l][1], lambda kk, cls: Gs[l][1][:, kk, cls:cls + 1], modT[l],
                           lambda kk, cls: shiftv(l, 1)[:, kk, cls:cls + 1], tiles_all, uT, "uT", send=uTs)
                all_gather([(uTs[q * 512:(q + 1) * 512, :], uTg[q * 1024:(q + 1) * 1024, :]) for q in range(2)])
                win_gl_phase(l, tiles_lat if last else tiles_all)
                win_own_phase(l)
                yield f"win_{l}"
                na_phase(l, not last)
                yield f"na_{l}"
                fourier_phase(not last)
                yield f"fn_{l}"
                wa_phase(l, not last)
                yield f"wa_{l}"
                b2 = brT.rearrange("i c t -> (i c) t")
                all_gather([(b2[q * 128:(q + 1) * 128, :], brTg[q * 256:(q + 1) * 256, :]) for q in range(6)])
                tl = tiles_lat if last else tiles_all
                merge_phase(l, tl)
                yield f"merge_{l}"
                with k.phase():
                    w2b = k.sb("w2", [128, 22, 1024], BF16)
                    with k.phase():
                        pre13 = w13_alloc_load(ffn_w13[1][l])
                        norm_phase(hT, "hT", Gs[l][2], lambda kk, cls: Gs[l][2][:, kk, cls:cls + 1], modT[l],
                                   lambda kk, cls: shiftv(l, 2)[:, kk, cls:cls + 1], tl, uT, "uT", lean=True)
                        w2_load(ffn_w2[1][l], w2b)
                        w13_phase(ffn_w13[1][l], tl, pre=pre13)
                    w2_phase(ffn_w2[1][l], hT, "hT", Gt[l][2], tl, pre=w2b)
                yield f"ffn2_{l}"
            norm_phase(hT, "hT", gv, lambda kk, cls: gv[:, 6, kk, 0:1], None, None, TILES[1:], outT, "outT", final=True)
            yield "final"

        for name in run():
            if stop_after is not None and name == stop_after:
                break
        k.barrier()
    return nc


def _constants():
    bf = ml_dtypes.bfloat16
    t = np.arange(T)
    row = (t // 64).astype(np.float64)
    col = (t % 64).astype(np.float64)
    inv = 10000.0 ** (-np.arange(16, dtype=np.float64) / 16)
    ang = np.concatenate([row[:, None] * inv, col[:, None] * inv], axis=-1)
    c = np.cos(ang).T
    s = np.sin(ang).T
    ropeC = np.concatenate([c, c], axis=0).astype(np.float32)
    ropeS = np.concatenate([-s, s], axis=0).astype(np.float32)
    cc = np.arange(64)
    a = 2 * np.pi * np.outer(cc, cc) / 64
    C64, S64 = np.cos(a), np.sin(a)
    z = np.zeros((64, 64))
    Cb = np.block([[C64, z], [z, C64]])
    Sb = np.block([[S64, z], [z, S64]])
    cs64 = np.concatenate([Cb, Sb], axis=1).astype(np.float32)

    def pos_tables(N, norm):
        n = np.arange(N)
        m = (np.outer(n, n) % N).astype(np.float64)
        a = 2 * np.pi * m / N
        return (np.cos(a) * norm).astype(np.float32), (-np.sin(a) * norm).astype(np.float32)

    Cn, Sn = pos_tables(T, 1.0 / 512)
    def lay(M):
        return np.ascontiguousarray(M.reshape(32, 128, 16, 256).transpose(2, 1, 0, 3)).astype(bf)
    dftC, dftS = lay(Cn), lay(Sn)
    C2, S2 = pos_tables(LC, 1.0 / 128)
    def lay2(M):
        return np.ascontiguousarray(M.reshape(2, 128, 256).transpose(1, 0, 2)).astype(bf)
    dftC2, dftS2 = lay2(C2), lay2(S2)
    p = np.arange(128)[:, None]
    q = np.arange(128)[None, :]
    m0 = (q <= p).astype(np.float32)
    m1 = (p <= q).astype(np.float32)
    trimask = np.stack([np.tile(m0, (1, 4)), np.tile(m1, (1, 4))], axis=1).astype(np.float32)
    pk = np.arange(128)
    kc = pk % 64
    qc = np.arange(64)
    win_start = np.clip(qc - 8, 0, 48)
    valid = (kc[:, None] >= win_start[None, :]) & (kc[:, None] < win_start[None, :] + 16)
    na_mask = np.broadcast_to(valid[:, None, None, :], (128, 8, 4, 64)).reshape(128, 8, 256).astype(np.float32)
    na_mask = np.ascontiguousarray(na_mask)
    kr = (2 * np.arange(4)[None, :] + (pk // 64)[:, None])
    dr_idx = kr[:, None, :] - np.arange(8)[None, :, None] + 7
    dr_ok = (dr_idx >= 0) & (dr_idx <= 14)
    dr_idx = np.clip(dr_idx, 0, 14)
    dc_idx = np.clip(kc[:, None] - qc[None, :], -15, 15) + 15
    return dict(ropeC=ropeC, ropeS=ropeS, cs64=cs64, dftC=dftC, dftS=dftS, dftC2=dftC2, dftS2=dftS2,
                trimask=trimask, na_mask=na_mask), (dr_idx, dc_idx)


_CONST = None
_PROG = {}


def _prep_inputs(inp):
    global _CONST
    if _CONST is None:
        _CONST = _constants()
    const, (dr_idx, dc_idx) = _CONST
    f32 = np.float32
    x = np.asarray(inp["x"], f32)
    ctx = np.asarray(inp["ctx"], f32)
    c = np.asarray(inp["c"], f32)
    c_ctx = np.asarray(inp["c_ctx"], f32)

    def chunked(v):
        return np.ascontiguousarray(v.reshape(8, 128).T)

    b_ada = np.asarray(inp["b_ada"], f32)
    b_adaT = np.ascontiguousarray(np.repeat(b_ada.reshape(DEPTH, 72, 128).transpose(0, 2, 1)[..., None], 2, axis=-1))
    gl = [inp["g_ffn1"][0], inp["g_mix"][0], inp["g_ffn2"][0], inp["g_ffn1"][1], inp["g_mix"][1], inp["g_ffn2"][1], inp["g_final"]]
    gvec = np.stack([chunked(np.asarray(g, f32)) for g in gl], axis=1)
    gvec = np.ascontiguousarray(np.repeat(gvec[..., None], 2, axis=-1))
    nb = np.asarray(inp["na_bias"], f32)
    G = nb[:, :, dr_idx[:, :, :, None], dc_idx[:, None, None, :]]
    na_biasG = np.ascontiguousarray(G.transpose(0, 2, 3, 1, 4, 5).reshape(DEPTH, 128, 8, 8, 256))
    sk = np.asarray(inp["wa_sink"], f32)
    w_in = np.asarray(inp["w_in"], f32)
    shared = dict(const)
    shared.update(
        w_ada=np.asarray(inp["w_ada"], f32), b_adaT=b_adaT, gvec=gvec,
        ffn1_w13=np.asarray(inp["ffn1_w13"], f32), ffn2_w13=np.asarray(inp["ffn2_w13"], f32),
        ffn1_w2=np.asarray(inp["ffn1_w2"], f32), ffn2_w2=np.asarray(inp["ffn2_w2"], f32),
        w_in_gl=np.ascontiguousarray(w_in[:, :, 2816:5888]),
        w_br=np.asarray(inp["w_br"], f32), w_out=np.asarray(inp["w_out"], f32),
    )
    per_half = []
    for hf in range(2):
        cols = np.concatenate([np.arange(0 + hf * 256, 0 + hf * 256 + 256), np.arange(512 + hf * 256, 512 + hf * 256 + 256),
                               np.arange(1024 + hf * 256, 1024 + hf * 256 + 256), np.arange(1536 + hf * 256, 1536 + hf * 256 + 256),
                               np.arange(2048 + hf * 256, 2048 + hf * 256 + 256), np.arange(2560 + hf * 64, 2560 + hf * 64 + 64),
                               np.arange(2688 + hf * 64, 2688 + hf * 64 + 64)])
        oh = np.zeros((128, 2), f32)
        oh[:, hf] = 1.0
        per_half.append(dict(
            w_in_own=np.ascontiguousarray(w_in[:, :, cols]),
            na_biasG=np.ascontiguousarray(na_biasG[:, :, :, 4 * hf:4 * hf + 4, :]),
            sinkG=np.ascontiguousarray(np.broadcast_to(sk[:, 4 * hf:4 * hf + 4].reshape(DEPTH, 1, 4, 1), (DEPTH, 128, 4, 128)).reshape(DEPTH, 128, 512)),
            oh=oh,
        ))
    maps = []
    for core in range(NCORES):
        b, hf = core // 2, core % 2
        m = dict(shared)
        m.update(per_half[hf])
        m["h0T"] = np.ascontiguousarray(np.concatenate([ctx[b].T, x[b, hf * TH:(hf + 1) * TH].T], axis=1))
        m["ccT"] = np.ascontiguousarray(np.stack([chunked(c[b]), chunked(c_ctx)], axis=-1))
        maps.append(m)
    return maps


def kernel(**inputs):
    maps = _prep_inputs(inputs)
    if "main" not in _PROG:
        _PROG["main"] = build_program()
    nc = _PROG["main"]
    res = run_bass_kernel_spmd(nc, maps, core_ids=list(range(NCORES)))
    out = np.empty((NB, T, D), np.float32)
    for core in range(NCORES):
        b, hf = core // 2, core % 2
        out[b, hf * TH:(hf + 1) * TH, :] = np.asarray(res.results[core]["outT"]).T
    return out
```

```python
import contextlib
import numpy as np
import ml_dtypes
import concourse.bass as bass
import concourse.mybir as mybir
from concourse.bass_utils import run_bass_kernel_spmd

F32 = mybir.dt.float32
BF16 = mybir.dt.bfloat16
AF = mybir.ActivationFunctionType
ALU = mybir.AluOpType

D = 1024
T = 4096
LC = 256
NT = T + LC
TH = T // 2
NTL = LC + TH
NCORES = 8
DFF = 2816
NB = 4
DEPTH = 2
EPS = 1e-6
P_IN = 5888
TILES = [(0, 256, 1)] + [(LC + i * 512, 512, 0) for i in range(4)]
GTILES = [(0, 256, 1)] + [(LC + i * 512, 512, 0) for i in range(8)]
R_NQ, R_NK, R_FU, R_WQ, R_WK = 0, 256, 512, 768, 1024
PT_ROWS = 1088
FM_COLS = [0, 128, 256, 384, 768, 896, 1024, 1152]


class Buf:
    def __init__(self, name, ap=None):
        self.name = name
        self.ap = ap
        self.w = {}
        self.r = {}
        self.dsem = None
        self.dcnt = 0

    def __getitem__(self, idx):
        return self.ap[idx]


class Eng:
    def __init__(self, K, name, eng, sem, is_pe=False):
        self.K = K
        self.name = name
        self.e = eng
        self.sem = sem
        self.cnt = 0
        self.waited = {}
        self.is_pe = is_pe
        self.pending = False

    def _wait(self, sem, val, eng):
        if eng is self and self.is_pe:
            return
        if eng is not None and val > eng.cnt:
            raise RuntimeError(f"pending token of {eng.name} awaited by {self.name}")
        k = id(sem)
        if self.waited.get(k, 0) >= val:
            return
        self.e.wait_ge(sem, val)
        self.waited[k] = val

    def deps(self, reads, writes):
        for b in reads:
            for (s, v, e) in list(b.w.values()):
                self._wait(s, v, e)
        for b in writes:
            for (s, v, e) in list(b.w.values()):
                self._wait(s, v, e)
            for (s, v, e) in list(b.r.values()):
                self._wait(s, v, e)

    @staticmethod
    def _reg(d, sem, val, eng):
        k = id(sem)
        if k not in d or d[k][1] < val:
            d[k] = (sem, val, eng)

    def op(self, ins_fn, reads=(), writes=(), signal=True):
        self.deps(reads, writes)
        ins = ins_fn()
        if signal:
            self.cnt += 1
            ins.then_inc(self.sem, 1)
            val = self.cnt
            self.pending = False
        else:
            val = self.cnt + 1
            self.pending = True
        for b in reads:
            self._reg(b.r, self.sem, val, self)
        for b in writes:
            self._reg(b.w, self.sem, val, self)
        return ins

    def dma(self, out_ap, in_ap, reads=(), writes=(), sbuf=None):
        self.deps(reads, writes)
        if sbuf.dsem is None:
            sbuf.dsem, sbuf.dcnt = self.K.new_sem("d_" + sbuf.name)
        ins = self.e.dma_start(out=out_ap, in_=in_ap)
        sbuf.dcnt += 16
        ins.then_inc(sbuf.dsem, 16)
        for b in reads:
            self._reg(b.r, sbuf.dsem, sbuf.dcnt, None)
        for b in writes:
            self._reg(b.w, sbuf.dsem, sbuf.dcnt, None)
        self.K.live_dma[id(sbuf)] = sbuf
        return ins


class K:
    def __init__(self, nc, es):
        self.nc = nc
        self.es = es
        self.nsem = 0
        self.free_sems = []
        self.live_dma = {}
        self.PE = Eng(self, "pe", nc.tensor, self._sem("s_pe"), is_pe=True)
        self.ACT = Eng(self, "act", nc.scalar, self._sem("s_act"))
        self.DVE = Eng(self, "dve", nc.vector, self._sem("s_dve"))
        self.POOL = Eng(self, "pool", nc.gpsimd, self._sem("s_pool"))
        self.SP = Eng(self, "sp", nc.sync, self._sem("s_sp"))
        self.engs = [self.PE, self.ACT, self.DVE, self.POOL, self.SP]
        self.dram = {}
        self.scopes = []
        self.uid = 0

    def _sem(self, name):
        self.nsem += 1
        return self.es.enter_context(self.nc.semaphore(name))

    def new_sem(self, name):
        if self.free_sems:
            return self.free_sems.pop()
        self.uid += 1
        return self._sem(f"{name}_{self.uid}"), 0

    def sb(self, name, shape, dt, persistent=False):
        self.uid += 1
        es = self.es if persistent else self.scopes[-1]["es"]
        t = es.enter_context(self.nc.sbuf_tensor(f"{name}_{self.uid}", list(shape), dt))
        b = Buf(name, t)
        if not persistent:
            self.scopes[-1]["bufs"].append(b)
        return b

    def dbuf(self, name, i=0):
        k = (name, i)
        if k not in self.dram:
            self.dram[k] = Buf(f"{name}_{i}")
        return self.dram[k]

    def barrier(self, skip=()):
        assert not self.PE.pending, "PE group left open"
        toks = [(e.sem, e.cnt, e) for e in self.engs if e.cnt > 0]
        dm = [(b.dsem, b.dcnt) for b in self.live_dma.values() if b.dcnt > 0 and id(b) not in skip]
        for e in self.engs:
            for (s, v, src) in toks:
                if src is e:
                    if not e.is_pe and e.cnt > 0:
                        e._wait(s, v, None)
                else:
                    e._wait(s, v, None)
            for (s, v) in dm:
                e._wait(s, v, None)

    @contextlib.contextmanager
    def phase(self):
        sc = {"es": contextlib.ExitStack(), "bufs": []}
        self.scopes.append(sc)
        with sc["es"]:
            yield
            outer = set(id(b) for s_ in self.scopes[:-1] for b in s_["bufs"])
            self.barrier(skip=outer)
        self.scopes.pop()
        for b in sc["bufs"]:
            if b.dsem is not None:
                self.live_dma.pop(id(b), None)
                self.free_sems.append((b.dsem, b.dcnt))


def build_program(dbg=(), stop_after=None):
    nc = bass.Bass("TRN2", target_bir_lowering=False)
    es = contextlib.ExitStack()

    def din(name, shape, dt=F32):
        return nc.dram_tensor(name, list(shape), dt, kind="ExternalInput").ap()

    def dscr(name, shape, dt):
        kind = "ExternalOutput" if name in dbg else "Internal"
        return nc.dram_tensor(name, list(shape), dt, kind=kind).ap()

    h0T = din("h0T", [D, NTL])
    oh_in = din("oh", [128, 2])
    ccT = din("ccT", [128, 8, 2])
    w_ada = din("w_ada", [DEPTH, D, 9 * D])
    b_adaT = din("b_adaT", [DEPTH, 128, 72, 2])
    gvec = din("gvec", [128, 7, 8, 2])
    ffn_w13 = [din("ffn1_w13", [DEPTH, D, 2 * DFF]), din("ffn2_w13", [DEPTH, D, 2 * DFF])]
    ffn_w2 = [din("ffn1_w2", [DEPTH, DFF, D]), din("ffn2_w2", [DEPTH, DFF, D])]
    w_in_own = din("w_in_own", [DEPTH, D, 1408])
    w_in_gl = din("w_in_gl", [DEPTH, D, 3072])
    w_br = din("w_br", [DEPTH, 3, 512, D])
    w_out = din("w_out", [DEPTH, D, D])
    na_biasG = din("na_biasG", [DEPTH, 128, 8, 4, 256])
    na_mask = din("na_mask", [128, 8, 256])
    sinkG = din("sinkG", [DEPTH, 128, 512])
    ropeC = din("ropeC", [64, T])
    ropeS = din("ropeS", [64, T])
    cs64 = din("cs64", [128, 256])
    dftC = din("dftC", [16, 128, 32, 256], BF16)
    dftS = din("dftS", [16, 128, 32, 256], BF16)
    dftC2 = din("dftC2", [128, 2, 256], BF16)
    dftS2 = din("dftS2", [128, 2, 256], BF16)
    trimask = din("trimask", [128, 2, 512])
    outT = nc.dram_tensor("outT", [D, TH], F32, kind="ExternalOutput").ap()

    hT = dscr("hT", [D, NTL], F32)
    uT = dscr("uT", [D, NTL], BF16)
    gT = dscr("gT", [DFF, NTL], BF16)
    glT = dscr("glT", [3072, NTL], BF16)
    pT = dscr("pT", [PT_ROWS, NT], BF16)
    Vtok = dscr("Vtok", [NT, 320], BF16)
    brT = dscr("brT", [3, 256, NT], BF16)
    uTs = dscr("uTs", [D, TH], BF16)
    uTg = dscr("uTg", [2 * D, TH], BF16)
    brTg = dscr("brTg", [2 * 768, NT], BF16)

    with es:
        k = K(nc, es)
        PE, ACT, DVE, POOL, SP = k.PE, k.ACT, k.DVE, k.POOL, k.SP
        PS = []
        for i in range(8):
            t = es.enter_context(nc.psum_tensor(f"ps{i}", [128, 512], F32))
            PS.append(Buf(f"ps{i}", t))

        def mm(out, lhsT, rhs, start, stop, reads, writes):
            PE.op(lambda: nc.tensor.matmul(out, lhsT, rhs, start=start, stop=stop),
                  reads=reads, writes=writes, signal=stop)

        ones = k.sb("ones", [128, 128], BF16, persistent=True)
        epsT = k.sb("eps", [128, 1], F32, persistent=True)
        gv = k.sb("gv", [128, 7, 8, 2], F32, persistent=True)
        modT = [k.sb(f"mod{l}", [128, 72, 2], F32, persistent=True) for l in range(DEPTH)]
        Gs = [[k.sb(f"G{l}{s}", [128, 8, 2], F32, persistent=True) for s in range(3)] for l in range(DEPTH)]
        Gt = [[k.sb(f"Gt{l}{s}", [128, 8, 2], F32, persistent=True) for s in range(3)] for l in range(DEPTH)]
        ccdummy = k.sb("ccdummy", [128, 2], F32, persistent=True)
        DVE.op(lambda: nc.vector.memset(ones[:], 1.0), writes=[ones])
        DVE.op(lambda: nc.vector.memset(epsT[:], EPS), writes=[epsT])
        SP.dma(gv[:], gvec, writes=[gv], sbuf=gv)
        oh = k.sb("oh", [128, 2], F32, persistent=True)
        SP.dma(oh[:], oh_in, writes=[oh], sbuf=oh)
        cc_sem = k._sem("s_cc")
        cc_cnt = [0]

        def all_gather(pieces):
            with k.phase():
                k.barrier(skip=set(id(b_) for s_ in k.scopes[:-1] for b_ in s_["bufs"]))
                for (src, dst) in pieces:
                    ins = nc.gpsimd.collective_compute("AllGather", ALU.bypass, replica_groups=[[0, 1], [2, 3], [4, 5], [6, 7]],
                                                       ins=[src], outs=[dst])
                    cc_cnt[0] += 1
                    ins.then_inc(cc_sem, 1)
                nc.gpsimd.wait_ge(cc_sem, cc_cnt[0])
                POOL.op(lambda: nc.gpsimd.memset(ccdummy[:], 0.0), writes=[ccdummy])

        with k.phase():
            cc = k.sb("cc", [128, 8, 2], F32)
            scb = k.sb("scb", [128, 8, 2], BF16)
            bT = [k.sb(f"bT{l}", [128, 72, 2], F32) for l in range(DEPTH)]
            wsl = [k.sb(f"wada{i}", [128, 8, 1024], BF16) for i in range(2)]
            tmp1 = k.sb("tmp1", [128, 8, 2], F32)
            SP.dma(cc[:], ccT, writes=[cc], sbuf=cc)
            for l in range(DEPTH):
                SP.dma(bT[l][:], b_adaT[l], writes=[bT[l]], sbuf=bT[l])
            ACT.op(lambda: nc.scalar.activation(out=scb[:], in_=cc[:], func=AF.Silu), reads=[cc], writes=[scb])
            it = 0
            for l in range(DEPTH):
                wr = w_ada[l].rearrange("(k p) n -> p k n", p=128)
                ps = PS[l]
                for s9 in range(9):
                    wb = wsl[it % 2]
                    it += 1
                    POOL.dma(wb[:], wr[:, :, s9 * 1024:(s9 + 1) * 1024], writes=[wb], sbuf=wb)
                    for jj in range(8):
                        j = s9 * 8 + jj
                        for kk in range(8):
                            mm(ps[:, 2 * j:2 * j + 2], wb[:, kk, jj * 128:(jj + 1) * 128], scb[:, kk, :],
                               kk == 0, kk == 7, [wb, scb], [ps])
                DVE.op(lambda: nc.vector.tensor_tensor(out=modT[l][:].rearrange("p a b -> p (a b)"), in0=ps[:, 0:144],
                                                       in1=bT[l][:].rearrange("p a b -> p (a b)"), op=ALU.add),
                       reads=[ps, bT[l]], writes=[modT[l]])
                for s in range(3):
                    sc = modT[l][:, (3 * s + 1) * 8:(3 * s + 2) * 8, :]
                    DVE.op(lambda: nc.vector.tensor_scalar(out=tmp1[:], in0=sc, scalar1=1.0, scalar2=None, op0=ALU.add),
                           reads=[modT[l]], writes=[tmp1])
                    DVE.op(lambda: nc.vector.tensor_tensor(out=Gs[l][s][:], in0=tmp1[:], in1=gv[:, l * 3 + s, :, :], op=ALU.mult),
                           reads=[tmp1, gv], writes=[Gs[l][s]])
                    gt = modT[l][:, (3 * s + 2) * 8:(3 * s + 3) * 8, :]
                    fac = 1.0 if s == 1 else 0.5
                    DVE.op(lambda: nc.vector.tensor_scalar(out=Gt[l][s][:], in0=gt, scalar1=fac, scalar2=None, op0=ALU.mult),
                           reads=[modT[l]], writes=[Gt[l][s]])

        def shiftv(l, s):
            return modT[l][:, (3 * s) * 8:(3 * s + 1) * 8, :]

        def norm_phase(hsrc, hsrc_name, Gbuf, Gap, Shbuf, Shap, tiles, dst, dst_name, final=False, send=None, lean=False):
            hr = hsrc.rearrange("(k p) t -> p k t", p=128)
            dr = dst.rearrange("(k p) t -> p k t", p=128)
            with k.phase():
                nb_ = 1 if lean else 2
                hin = [k.sb(f"hin{i}", [128, 8, 512], F32) for i in range(2)]
                sq = [k.sb(f"sq{i}", [128, 8, 512], BF16) for i in range(nb_)] * (3 - nb_)
                tmp = [k.sb(f"tmp{i}", [128, 8, 512], F32) for i in range(nb_)] * (3 - nb_)
                uo = None if final else [k.sb(f"uo{i}", [128, 8, 512], BF16) for i in range(nb_)] * (3 - nb_)
                sd = [k.sb(f"sd{i}", [128, 512], F32) for i in range(nb_)] * (3 - nb_)
                rs = [k.sb(f"rs{i}", [128, 512], F32) for i in range(nb_)] * (3 - nb_)
                def _nload(ti):
                    c0, W, cls = tiles[ti]
                    b = ti % 2
                    SP.dma(hin[b][:, :, :W], hr[:, :, c0:c0 + W], reads=[k.dbuf(hsrc_name, c0)], writes=[hin[b]], sbuf=hin[b])

                _nload(0)
                for ti, (c0, W, cls) in enumerate(tiles):
                    b = ti % 2
                    ps = PS[ti % 2]
                    tix = c0
                    if ti + 1 < len(tiles):
                        _nload(ti + 1)
                    ACT.op(lambda: nc.scalar.activation(out=sq[b][:, :, :W], in_=hin[b][:, :, :W], func=AF.Square),
                           reads=[hin[b]], writes=[sq[b]])
                    for kk in range(8):
                        mm(ps[:, :W], ones[:], sq[b][:, kk, :W], kk == 0, kk == 7, [ones, sq[b]], [ps])
                    ACT.op(lambda: nc.scalar.activation(out=sd[b][:, :W], in_=ps[:, :W], func=AF.Sqrt, bias=epsT[:, 0:1], scale=1.0 / D),
                           reads=[ps, epsT], writes=[sd[b]])
                    DVE.op(lambda: nc.vector.reciprocal(out=rs[b][:, :W], in_=sd[b][:, :W]), reads=[sd[b]], writes=[rs[b]])
                    for kk in range(8):
                        DVE.op(lambda: nc.vector.scalar_tensor_tensor(out=tmp[b][:, kk, :W], in0=hin[b][:, kk, :W],
                                                                      scalar=Gap(kk, cls), in1=rs[b][:, :W],
                                                                      op0=ALU.mult, op1=ALU.mult),
                               reads=[hin[b], rs[b], Gbuf], writes=[tmp[b]])
                    if final:
                        SP.dma(dr[:, :, c0 - LC:c0 - LC + W], tmp[b][:, :, :W], reads=[tmp[b]], writes=[k.dbuf(dst_name, tix)], sbuf=tmp[b])
                    else:
                        for kk in range(8):
                            ACT.op(lambda: nc.scalar.activation(out=uo[b][:, kk, :W], in_=tmp[b][:, kk, :W], func=AF.Identity,
                                                                bias=Shap(kk, cls), scale=1.0),
                                   reads=[tmp[b], Shbuf], writes=[uo[b]])
                        SP.dma(dr[:, :, c0:c0 + W], uo[b][:, :, :W], reads=[uo[b]], writes=[k.dbuf(dst_name, tix)], sbuf=uo[b])
                        if send is not None and cls == 0:
                            sr = send.rearrange("(k p) t -> p k t", p=128)
                            SP.dma(sr[:, :, c0 - LC:c0 - LC + W], uo[b][:, :, :W], reads=[uo[b]], writes=[k.dbuf("uTs", tix)], sbuf=uo[b])

        def w13_alloc_load(w13):
            wr = w13.rearrange("(k p) n -> p k n", p=128)
            wa = [k.sb(f"wa{i}", [128, 8, 1408], BF16) for i in range(2)]
            wb = [k.sb(f"wb{i}", [128, 8, 1408], BF16) for i in range(2)]
            for s in range(2):
                POOL.dma(wa[s][:], wr[:, :, s * 1408:(s + 1) * 1408], writes=[wa[s]], sbuf=wa[s])
                POOL.dma(wb[s][:], wr[:, :, DFF + s * 1408:DFF + (s + 1) * 1408], writes=[wb[s]], sbuf=wb[s])
            return wa, wb

        def w2_load(w2, w):
            wr = w2.rearrange("(j p) n -> p j n", p=128)
            POOL.dma(w[:, 0:11, :], wr[:, 0:11, :], writes=[w], sbuf=w)
            POOL.dma(w[:, 11:22, :], wr[:, 11:22, :], writes=[w], sbuf=w)

        def w13_phase(w13, tiles, pre=None):
            wr = w13.rearrange("(k p) n -> p k n", p=128)
            ur = uT.rearrange("(k p) t -> p k t", p=128)
            gr = gT.rearrange("(j p) t -> p j t", p=128)
            with k.phase():
                wa, wb = pre
                uin = [k.sb(f"uin{i}", [128, 8, 512], BF16) for i in range(2)]
                sg = [k.sb(f"sg{i}", [128, 512], F32) for i in range(2)]
                go = [k.sb(f"go{i}", [128, 11, 512], BF16) for i in range(2)]
                it = 0
                pi = 0
                seq = [(s_, t_) for s_ in range(2) for t_ in tiles]

                def _uload(i):
                    (c0, W, cls) = seq[i][1]
                    SP.dma(uin[i % 2][:, :, :W], ur[:, :, c0:c0 + W], reads=[k.dbuf("uT", c0)], writes=[uin[i % 2]], sbuf=uin[i % 2])

                _uload(0)
                for s in range(2):
                    for (c0, W, cls) in tiles:
                        b = it % 2
                        it += 1
                        if it < len(seq):
                            _uload(it)
                        for j in range(11):
                            pa = PS[(2 * pi) % 8]
                            pb = PS[(2 * pi + 1) % 8]
                            sgb = sg[pi % 2]
                            pi += 1
                            for kk in range(8):
                                mm(pa[:, :W], wa[s][:, kk, j * 128:(j + 1) * 128], uin[b][:, kk, :W], kk == 0, kk == 7, [wa[s], uin[b]], [pa])
                            for kk in range(8):
                                mm(pb[:, :W], wb[s][:, kk, j * 128:(j + 1) * 128], uin[b][:, kk, :W], kk == 0, kk == 7, [wb[s], uin[b]], [pb])
                            ACT.op(lambda: nc.scalar.activation(out=sgb[:, :W], in_=pa[:, :W], func=AF.Silu), reads=[pa], writes=[sgb])
                            DVE.op(lambda: nc.vector.tensor_tensor(out=go[b][:, j, :W], in0=sgb[:, :W], in1=pb[:, :W], op=ALU.mult),
                                   reads=[sgb, pb], writes=[go[b]])
                        SP.dma(gr[:, s * 11:(s + 1) * 11, c0:c0 + W], go[b][:, :, :W], reads=[go[b]], writes=[k.dbuf("gT", (s, c0))], sbuf=go[b])

        def w2_phase(w2, hsrc, hsrc_name, gate, tiles, pre=None, post=None):
            wr = w2.rearrange("(j p) n -> p j n", p=128)
            gr = gT.rearrange("(j p) t -> p j t", p=128)
            hr = hsrc.rearrange("(k p) t -> p k t", p=128)
            ho = hT.rearrange("(k p) t -> p k t", p=128)
            with k.phase():
                w = pre
                gin = [k.sb(f"gin{i}", [128, 22, 512], BF16) for i in range(2)]
                hin = [k.sb(f"hin{i}", [128, 8, 512], F32) for i in range(2)]
                pi = 0
                if post is not None:
                    n_sq = k.sb("n_sq", [128, 8, 512], BF16)
                    n_tmp = k.sb("n_tmp", [128, 8, 512], F32)
                    n_uo = None if post["final"] else k.sb("n_uo", [128, 8, 512], BF16)
                    n_sd = k.sb("n_sd", [128, 512], F32)
                    n_rs = k.sb("n_rs", [128, 512], F32)
                    n_dr = post["dst"].rearrange("(k p) t -> p k t", p=128)
                    n_tiles = set(t_[0] for t_ in post["tiles"])

                def _post_norm(b, c0, W, cls):
                    hb = hin[b]
                    ps = PS[pi % 8]
                    ACT.op(lambda: nc.scalar.activation(out=n_sq[:, :, :W], in_=hb[:, :, :W], func=AF.Square), reads=[hb], writes=[n_sq])
                    for kk in range(8):
                        mm(ps[:, :W], ones[:], n_sq[:, kk, :W], kk == 0, kk == 7, [ones, n_sq], [ps])
                    ACT.op(lambda: nc.scalar.activation(out=n_sd[:, :W], in_=ps[:, :W], func=AF.Sqrt, bias=epsT[:, 0:1], scale=1.0 / D),
                           reads=[ps, epsT], writes=[n_sd])
                    DVE.op(lambda: nc.vector.reciprocal(out=n_rs[:, :W], in_=n_sd[:, :W]), reads=[n_sd], writes=[n_rs])
                    for kk in range(8):
                        DVE.op(lambda: nc.vector.scalar_tensor_tensor(out=n_tmp[:, kk, :W], in0=hb[:, kk, :W], scalar=post["Gap"](kk, cls), in1=n_rs[:, :W],
                                                                      op0=ALU.mult, op1=ALU.mult),
                               reads=[hb, n_rs, post["Gbuf"]], writes=[n_tmp])
                    if post["final"]:
                        SP.dma(n_dr[:, :, c0 - LC:c0 - LC + W], n_tmp[:, :, :W], reads=[n_tmp], writes=[k.dbuf("outT", c0)], sbuf=n_tmp)
                    else:
                        for kk in range(8):
                            ACT.op(lambda: nc.scalar.activation(out=n_uo[:, kk, :W], in_=n_tmp[:, kk, :W], func=AF.Identity, bias=post["Shap"](kk, cls), scale=1.0),
                                   reads=[n_tmp, post["Shbuf"]], writes=[n_uo])
                        SP.dma(n_dr[:, :, c0:c0 + W], n_uo[:, :, :W], reads=[n_uo], writes=[k.dbuf("uT", c0)], sbuf=n_uo)
                        if post["send"] is not None and cls == 0:
                            sr = post["send"].rearrange("(k p) t -> p k t", p=128)
                            SP.dma(sr[:, :, c0 - LC:c0 - LC + W], n_uo[:, :, :W], reads=[n_uo], writes=[k.dbuf("uTs", c0)], sbuf=n_uo)

                def _wload(ti):
                    c0, W, cls = tiles[ti]
                    b = ti % 2
                    SP.dma(gin[b][:, :, :W], gr[:, :, c0:c0 + W], reads=[k.dbuf("gT", (0, c0)), k.dbuf("gT", (1, c0))], writes=[gin[b]], sbuf=gin[b])
                    SP.dma(hin[b][:, :, :W], hr[:, :, c0:c0 + W], reads=[k.dbuf(hsrc_name, c0)], writes=[hin[b]], sbuf=hin[b])

                _wload(0)
                for ti, (c0, W, cls) in enumerate(tiles):
                    b = ti % 2
                    if ti + 1 < len(tiles):
                        _wload(ti + 1)
                    for n in range(8):
                        ps = PS[pi % 8]
                        pi += 1
                        for j in range(22):
                            mm(ps[:, :W], w[:, j, n * 128:(n + 1) * 128], gin[b][:, j, :W], j == 0, j == 21, [w, gin[b]], [ps])
                        DVE.op(lambda: nc.vector.scalar_tensor_tensor(out=hin[b][:, n, :W], in0=ps[:, :W], scalar=gate[:, n, cls:cls + 1],
                                                                      in1=hin[b][:, n, :W], op0=ALU.mult, op1=ALU.add),
                               reads=[ps, hin[b], gate], writes=[hin[b]])
                    SP.dma(ho[:, :, c0:c0 + W], hin[b][:, :, :W], reads=[hin[b]], writes=[k.dbuf("hT", c0)], sbuf=hin[b])
                    if post is not None and c0 in n_tiles:
                        _post_norm(b, c0, W, cls)
                        pi += 1

        def win_alloc_load(l):
            wr = w_in_gl[l].rearrange("(k p) n -> p k n", p=128)
            ws = [k.sb(f"ws{i}", [128, 8, 1024], BF16) for i in range(3)]
            for si in range(3):
                POOL.dma(ws[si][:], wr[:, :, si * 1024:(si + 1) * 1024], writes=[ws[si]], sbuf=ws[si])
            wsb = k.sb("wown", [128, 8, 1408], BF16)
            POOL.dma(wsb[:], w_in_own[l].rearrange("(k p) n -> p k n", p=128), writes=[wsb], sbuf=wsb)
            return ws, wsb

        def win_gl_phase(l, tiles, pre=None):
            wr = w_in_gl[l].rearrange("(k p) n -> p k n", p=128)
            ur = uT.rearrange("(k p) t -> p k t", p=128)
            pr = glT.rearrange("(j p) t -> p j t", p=128)
            with k.phase():
                ws = pre
                uin = [k.sb(f"uin{i}", [128, 8, 512], BF16) for i in range(2)]
                po = [k.sb(f"po{i}", [128, 8, 512], BF16) for i in range(2)]
                it = 0
                pi = 0
                ei = 0
                seq = [(s_, t_) for s_ in range(3) for t_ in tiles]

                def _gload(i):
                    (c0, W, cls) = seq[i][1]
                    SP.dma(uin[i % 2][:, :, :W], ur[:, :, c0:c0 + W], reads=[k.dbuf("uT", c0)], writes=[uin[i % 2]], sbuf=uin[i % 2])

                _gload(0)
                for si in range(3):
                    wsb = ws[si]
                    for (c0, W, cls) in tiles:
                        b = it % 2
                        it += 1
                        if it < len(seq):
                            _gload(it)
                        for jj in range(8):
                            ps = PS[pi % 8]
                            pi += 1
                            for kk in range(8):
                                mm(ps[:, :W], wsb[:, kk, jj * 128:(jj + 1) * 128], uin[b][:, kk, :W], kk == 0, kk == 7, [wsb, uin[b]], [ps])
                            if ei % 2 == 0:
                                ACT.op(lambda: nc.scalar.copy(out=po[b][:, jj, :W], in_=ps[:, :W]), reads=[ps], writes=[po[b]])
                            else:
                                DVE.op(lambda: nc.vector.tensor_copy(out=po[b][:, jj, :W], in_=ps[:, :W]), reads=[ps], writes=[po[b]])
                            ei += 1
                        SP.dma(pr[:, si * 8:(si + 1) * 8, c0:c0 + W], po[b][:, :, :W], reads=[po[b]], writes=[k.dbuf("glT", (si, c0))], sbuf=po[b])

        def win_own_phase(l, pre=None):
            wr = w_in_own[l].rearrange("(k p) n -> p k n", p=128)
            ur = uT.rearrange("(k p) t -> p k t", p=128)
            ug = uTg.rearrange("(q r k p) t -> p q r k t", p=128, k=4, r=2)
            pr = pT[0:1024, :].rearrange("(j p) t -> p j t", p=128)
            with k.phase():
                wsb = pre
                uin = [k.sb(f"uin{i}", [128, 8, 512], BF16) for i in range(2)]
                po = [k.sb(f"po{i}", [128, 9, 512], BF16) for i in range(2)]
                vo = [k.sb(f"vo{i}", [128, 320], BF16) for i in range(2)]
                pi = 0
                ei = 0
                def _oload(ti):
                    c0, W, cls = GTILES[ti]
                    b = ti % 2
                    if cls == 1:
                        SP.dma(uin[b][:, :, :W], ur[:, :, 0:W], reads=[k.dbuf("uT", 0)], writes=[uin[b]], sbuf=uin[b])
                    else:
                        g = (c0 - LC) // 512
                        for q in range(2):
                            SP.dma(uin[b][:, 4 * q:4 * q + 4, :W], ug[:, q, g // 4, :, (g % 4) * 512:(g % 4) * 512 + W], reads=[k.dbuf("uTg", 0)], writes=[uin[b]], sbuf=uin[b])

                _oload(0)
                for ti, (c0, W, cls) in enumerate(GTILES):
                    b = ti % 2
                    if ti + 1 < len(GTILES):
                        _oload(ti + 1)
                    for jj in range(9):
                        ps = PS[pi % 6]
                        pi += 1
                        if jj < 8:
                            cs, M = FM_COLS[jj], 128
                        else:
                            cs, M = 1280, 64
                        for kk in range(8):
                            mm(ps[0:M, :W], wsb[:, kk, cs:cs + M], uin[b][:, kk, :W], kk == 0, kk == 7, [wsb, uin[b]], [ps])
                        if ei % 2 == 0:
                            ACT.op(lambda: nc.scalar.copy(out=po[b][0:M, jj, :W], in_=ps[0:M, :W]), reads=[ps], writes=[po[b]])
                        else:
                            DVE.op(lambda: nc.vector.tensor_copy(out=po[b][0:M, jj, :W], in_=ps[0:M, :W]), reads=[ps], writes=[po[b]])
                        ei += 1
                    SP.dma(pr[:, :, c0:c0 + W], po[b][:, 0:8, :W], reads=[po[b]], writes=[k.dbuf("pT", c0)], sbuf=po[b])
                    SP.dma(pT[1024:1088, c0:c0 + W], po[b][0:64, 8, :W], reads=[po[b]], writes=[k.dbuf("pT", c0)], sbuf=po[b])
                    for tb in range(W // 128):
                        vb = vo[tb % 2]
                        p0 = PS[6 + tb % 2]
                        for kk in range(8):
                            mm(p0[:, 0:256], uin[b][:, kk, tb * 128:(tb + 1) * 128], wsb[:, kk, 512:768], kk == 0, kk == 7, [wsb, uin[b]], [p0])
                        for kk in range(8):
                            mm(p0[:, 256:320], uin[b][:, kk, tb * 128:(tb + 1) * 128], wsb[:, kk, 1344:1408], kk == 0, kk == 7, [wsb, uin[b]], [p0])
                        if tb % 2 == 0:
                            ACT.op(lambda: nc.scalar.copy(out=vb[:, 0:320], in_=p0[:, 0:320]), reads=[p0], writes=[vb])
                        else:
                            DVE.op(lambda: nc.vector.tensor_copy(out=vb[:, 0:320], in_=p0[:, 0:320]), reads=[p0], writes=[vb])
                        t0 = c0 + tb * 128
                        SP.dma(Vtok[t0:t0 + 128, :], vb[:], reads=[vb], writes=[k.dbuf("Vtok", 0)], sbuf=vb)

        def na_phase(l, with_ctx_q):
            with k.phase():
                ebm = k.sb("ebm", [128, 8, 4, 256], BF16)
                msk = k.sb("msk", [128, 8, 256], F32)
                bl = [k.sb(f"bl{i}", [128, 4, 256], F32) for i in range(2)]
                SP.dma(msk[:], na_mask, writes=[msk], sbuf=msk)
                for cls in range(8):
                    b = bl[cls % 2]
                    SP.dma(b[:], na_biasG[l, :, cls, :, :], writes=[b], sbuf=b)
                    ACT.op(lambda: nc.scalar.activation(out=b[:], in_=b[:], func=AF.Exp), reads=[b], writes=[b])
                    for h in range(4):
                        DVE.op(lambda: nc.vector.tensor_tensor(out=ebm[:, cls, h, :], in0=b[:, h, :], in1=msk[:, cls, :], op=ALU.mult),
                               reads=[b, msk], writes=[ebm])
                KT = [k.sb(f"KT{i}", [64, NT], BF16) for i in range(2)]
                QT = [k.sb(f"QT{i}", [64, NT], BF16) for i in range(2)]
                VE = [k.sb(f"VE{i}", [128, 34, 128], BF16) for i in range(2)]
                VO = [k.sb(f"VO{i}", [128, 33, 128], BF16) for i in range(2)]
                AO = [k.sb(f"AO{i}", [64, NT], BF16) for i in range(2)]
                E = [k.sb(f"E{i}", [128, 512], BF16) for i in range(4)]
                rec = [k.sb(f"rec{i}", [128, 256], F32) for i in range(2)]
                for i in range(2):
                    DVE.op(lambda: nc.vector.memset(VE[i][:], 1.0), writes=[VE[i]])
                    DVE.op(lambda: nc.vector.memset(VO[i][:], 1.0), writes=[VO[i]])
                vr = Vtok.rearrange("(c p) f -> p c f", p=128)
                vro = Vtok[64:64 + 33 * 128, :].rearrange("(c p) f -> p c f", p=128)
                items = []
                for h in range(4):
                    for r in range(64):
                        items.append((h, r))
                    if with_ctx_q:
                        items.append((h, -1))
                loaded = set()

                def load_head(h):
                    b = h % 2
                    SP.dma(KT[b][:], pT[R_NK + h * 64:R_NK + (h + 1) * 64, :], reads=[k.dbuf("pT", "all")], writes=[KT[b]], sbuf=KT[b])
                    SP.dma(QT[b][:], pT[R_NQ + h * 64:R_NQ + (h + 1) * 64, :], reads=[k.dbuf("pT", "all")], writes=[QT[b]], sbuf=QT[b])
                    for (ca, cb) in ((0, 9), (9, 18), (18, 27), (27, 34)):
                        SP.dma(VE[b][:, ca:cb, 0:64], vr[:, ca:cb, h * 64:(h + 1) * 64], reads=[k.dbuf("Vtok", 0)], writes=[VE[b]], sbuf=VE[b])
                    for (ca, cb) in ((0, 9), (9, 18), (18, 27), (27, 33)):
                        SP.dma(VO[b][:, ca:cb, 0:64], vro[:, ca:cb, h * 64:(h + 1) * 64], reads=[k.dbuf("Vtok", 0)], writes=[VO[b]], sbuf=VO[b])

                def stage1(idx):
                    h, r = items[idx]
                    b = h % 2
                    if h not in loaded:
                        loaded.add(h)
                        load_head(h)
                    ps_s = PS[idx % 4]
                    Eb = E[idx % 4]
                    if r >= 0:
                        rs_ = min(max(r - 4, 0), 56)
                        cls = r - rs_
                        tok0 = LC + rs_ * 64
                        q = QT[b][:, LC + r * 64:LC + (r + 1) * 64]
                        for c in range(4):
                            mm(ps_s[:, c * 64:(c + 1) * 64], KT[b][:, tok0 + c * 128:tok0 + (c + 1) * 128], q, True, True, [KT[b], QT[b]], [ps_s])
                        for c in range(2):
                            mm(ps_s[:, 256 + c * 64:256 + (c + 1) * 64], KT[b][:, c * 128:(c + 1) * 128], q, True, True, [KT[b], QT[b]], [ps_s])
                        ACT.op(lambda: nc.scalar.activation(out=Eb[:, 0:384], in_=ps_s[:, 0:384], func=AF.Exp, scale=0.125), reads=[ps_s], writes=[Eb])
                        DVE.op(lambda: nc.vector.tensor_tensor(out=Eb[:, 0:256], in0=Eb[:, 0:256], in1=ebm[:, cls, h, :], op=ALU.mult),
                               reads=[Eb, ebm], writes=[Eb])
                    else:
                        for c in range(2):
                            mm(ps_s[:, c * 256:(c + 1) * 256], KT[b][:, c * 128:(c + 1) * 128], QT[b][:, 0:256], True, True, [KT[b], QT[b]], [ps_s])
                        ACT.op(lambda: nc.scalar.activation(out=Eb[:, 0:512], in_=ps_s[:, 0:512], func=AF.Exp, scale=0.125), reads=[ps_s], writes=[Eb])

                def stage2(idx):
                    h, r = items[idx]
                    b = h % 2
                    Eb = E[idx % 4]
                    if r >= 0:
                        gi = (h * 16 + r // 4)
                        ps_o = PS[4 + gi % 3]
                        rc = rec[gi % 2]
                        jo = (r % 4) * 64
                        rs_ = min(max(r - 4, 0), 56)
                        for c in range(6):
                            if c < 4:
                                if rs_ % 2 == 0:
                                    vsrc, vb_ = VE[b][:, 2 + rs_ // 2 + c, :], VE[b]
                                else:
                                    vsrc, vb_ = VO[b][:, 2 + (rs_ - 1) // 2 + c, :], VO[b]
                            else:
                                vsrc, vb_ = VE[b][:, c - 4, :], VE[b]
                            mm(ps_o[:, jo:jo + 64], vsrc, Eb[:, c * 64:(c + 1) * 64], c == 0, c == 5, [vb_, Eb], [ps_o])
                        if r % 4 == 3:
                            DVE.op(lambda: nc.vector.reciprocal(out=rc[64:128, 0:256], in_=ps_o[64:128, 0:256]), reads=[ps_o], writes=[rc])
                            DVE.op(lambda: nc.vector.tensor_tensor(out=AO[b][:, LC + (r - 3) * 64:LC + (r + 1) * 64], in0=ps_o[0:64, 0:256], in1=rc[64:128, 0:256], op=ALU.mult),
                                   reads=[ps_o, rc], writes=[AO[b]])
                    else:
                        ps_o = PS[7]
                        rc = rec[idx % 2]
                        for c in range(2):
                            mm(ps_o[:, 0:256], VE[b][:, c, :], Eb[:, c * 256:(c + 1) * 256], c == 0, c == 1, [VE[b], Eb], [ps_o])
                        DVE.op(lambda: nc.vector.reciprocal(out=rc[64:128, 0:256], in_=ps_o[64:128, 0:256]), reads=[ps_o], writes=[rc])
                        DVE.op(lambda: nc.vector.tensor_tensor(out=AO[b][:, 0:256], in0=ps_o[0:64, 0:256], in1=rc[64:128, 0:256], op=ALU.mult),
                               reads=[ps_o, rc], writes=[AO[b]])
                    last_of_head = (idx + 1 == len(items)) or items[idx + 1][0] != h
                    if last_of_head:
                        if with_ctx_q:
                            SP.dma(brT[0, h * 64:(h + 1) * 64, :], AO[b][:], reads=[AO[b]], writes=[k.dbuf("brT", 0)], sbuf=AO[b])
                        else:
                            SP.dma(brT[0, h * 64:(h + 1) * 64, LC:NT], AO[b][:, LC:NT], reads=[AO[b]], writes=[k.dbuf("brT", 0)], sbuf=AO[b])

                LAG = 2
                for idx in range(len(items) + LAG):
                    if idx < len(items):
                        stage1(idx)
                    if idx - LAG >= 0:
                        stage2(idx - LAG)

        def wa_phase(l, with_ctx_q):
            with k.phase():
                cosT = k.sb("cosT", [64, T], F32)
                sinT = k.sb("sinT", [64, T], F32)
                tri = k.sb("tri", [128, 2, 512], F32)
                trib = k.sb("trib", [128, 2, 512], BF16)
                es_ = k.sb("esink", [128, 512], F32)
                SP.dma(cosT[:], ropeC, writes=[cosT], sbuf=cosT)
                SP.dma(sinT[:], ropeS, writes=[sinT], sbuf=sinT)
                SP.dma(tri[:], trimask, writes=[tri], sbuf=tri)
                SP.dma(es_[:], sinkG[l], writes=[es_], sbuf=es_)
                DVE.op(lambda: nc.vector.tensor_copy(out=trib[:], in_=tri[:]), reads=[tri], writes=[trib])
                ACT.op(lambda: nc.scalar.activation(out=es_[:], in_=es_[:], func=AF.Exp), reads=[es_], writes=[es_])
                A = [k.sb(f"A{i}", [64, NT], BF16) for i in range(1)]
                Bm = [k.sb(f"B{i}", [64, NT], BF16) for i in range(1)]
                t1 = [k.sb(f"t1{i}", [64, T], F32) for i in range(1)]
                KT = k.sb("KTw", [64, NT], BF16)
                QT = k.sb("QTw", [64, 4, NT], BF16)
                VE = k.sb("VEw", [128, 34, 128], BF16)
                WO = k.sb("WO", [64, 4, NT], BF16)
                E = [k.sb(f"Ew{i}", [128, 512], BF16) for i in range(12)]
                den = [k.sb(f"den{i}", [128, 512], F32) for i in range(2)]
                DVE.op(lambda: nc.vector.memset(VE[:], 1.0), writes=[VE])
                vr = Vtok.rearrange("(c p) f -> p c f", p=128)
                li = 0

                def load_rope(row0, dst_ap, dstbuf):
                    nonlocal li
                    a = A[0]
                    bm = Bm[0]
                    tt = t1[0]
                    li += 1
                    SP.dma(a[:], pT[row0:row0 + 64, :], reads=[k.dbuf("pT", "all")], writes=[a], sbuf=a)
                    SP.dma(bm[0:32, :], pT[row0 + 32:row0 + 64, :], reads=[k.dbuf("pT", "all")], writes=[bm], sbuf=bm)
                    SP.dma(bm[32:64, :], pT[row0:row0 + 32, :], reads=[k.dbuf("pT", "all")], writes=[bm], sbuf=bm)
                    DVE.op(lambda: nc.vector.tensor_copy(out=dst_ap[:, 0:LC], in_=a[:, 0:LC]), reads=[a], writes=[dstbuf])
                    DVE.op(lambda: nc.vector.tensor_tensor(out=tt[:], in0=a[:, LC:NT], in1=cosT[:], op=ALU.mult), reads=[a, cosT], writes=[tt])
                    DVE.op(lambda: nc.vector.tensor_tensor(out=bm[:, LC:NT], in0=bm[:, LC:NT], in1=sinT[:], op=ALU.mult), reads=[bm, sinT], writes=[bm])
                    DVE.op(lambda: nc.vector.tensor_tensor(out=dst_ap[:, LC:NT], in0=tt[:], in1=bm[:, LC:NT], op=ALU.add), reads=[tt, bm], writes=[dstbuf])

                load_rope(R_WK, KT[:], KT)
                for g in range(4):
                    load_rope(R_WQ + g * 64, QT[:, g, :], QT)
                for (ca, cb) in ((0, 9), (9, 18), (18, 27), (27, 34)):
                    SP.dma(VE[:, ca:cb, 0:64], vr[:, ca:cb, 256:320], reads=[k.dbuf("Vtok", 0)], writes=[VE], sbuf=VE)
                items = list(range(32)) + ([-1, -2] if with_ctx_q else [])
                ectr = [0]
                pend = {}

                def stage1(idx):
                    n = items[idx]
                    Es = []
                    if n >= 0:
                        chunks = []
                        if n > 0:
                            chunks.append((LC + (n - 1) * 128, 0))
                        chunks.append((LC + n * 128, None))
                        if n < 31:
                            chunks.append((LC + (n + 1) * 128, 1))
                        chunks.append((0, None))
                        chunks.append((128, None))
                        q = QT[:, :, LC + n * 128:LC + (n + 1) * 128]
                        for (kt0, mk) in chunks:
                            e = ectr[0]
                            ectr[0] += 1
                            ps_s = PS[e % 6]
                            Eb = E[e % 12]
                            mm(ps_s[:, :].rearrange("p (g q) -> p g q", g=4), KT[:, kt0:kt0 + 128], q, True, True, [KT, QT], [ps_s])
                            ACT.op(lambda: nc.scalar.activation(out=Eb[:], in_=ps_s[:], func=AF.Exp, scale=0.125), reads=[ps_s], writes=[Eb])
                            if mk is not None:
                                DVE.op(lambda: nc.vector.tensor_tensor(out=Eb[:], in0=Eb[:], in1=trib[:, mk, :], op=ALU.mult), reads=[Eb, trib], writes=[Eb])
                            Es.append((Eb, kt0 // 128))
                    else:
                        half = -n - 1
                        q = QT[:, 2 * half:2 * half + 2, 0:LC]
                        for c in range(2):
                            e = ectr[0]
                            ectr[0] += 1
                            ps_s = PS[e % 6]
                            Eb = E[e % 12]
                            mm(ps_s[:, :].rearrange("p (g q) -> p g q", g=2), KT[:, c * 128:(c + 1) * 128], q, True, True, [KT, QT], [ps_s])
                            ACT.op(lambda: nc.scalar.activation(out=Eb[:], in_=ps_s[:], func=AF.Exp, scale=0.125), reads=[ps_s], writes=[Eb])
                            Es.append((Eb, c))
                    pend[idx] = Es

                def stage2(idx):
                    n = items[idx]
                    Es = pend.pop(idx)
                    ps_o = PS[6 + idx % 2]
                    dn = den[idx % 2]
                    for ci, (Eb, vc) in enumerate(Es):
                        mm(ps_o[:, :], VE[:, vc, :], Eb[:], ci == 0, ci == len(Es) - 1, [VE, Eb], [ps_o])
                    if n >= 0:
                        DVE.op(lambda: nc.vector.tensor_tensor(out=dn[64:128, :], in0=ps_o[64:128, :], in1=es_[64:128, :], op=ALU.add), reads=[ps_o, es_], writes=[dn])
                        DVE.op(lambda: nc.vector.reciprocal(out=dn[64:128, :], in_=dn[64:128, :]), reads=[dn], writes=[dn])
                        DVE.op(lambda: nc.vector.tensor_tensor(out=WO[:, :, LC + n * 128:LC + (n + 1) * 128],
                                                               in0=ps_o[0:64, :].rearrange("p (g q) -> p g q", g=4),
                                                               in1=dn[64:128, :].rearrange("p (g q) -> p g q", g=4), op=ALU.mult),
                               reads=[ps_o, dn], writes=[WO])
                    else:
                        half = -n - 1
                        for gg in range(2):
                            g = 2 * half + gg
                            DVE.op(lambda: nc.vector.tensor_scalar(out=dn[64:128, gg * 256:(gg + 1) * 256], in0=ps_o[64:128, gg * 256:(gg + 1) * 256],
                                                                   scalar1=es_[64:128, g * 128:g * 128 + 1], scalar2=None, op0=ALU.add),
                                   reads=[ps_o, es_], writes=[dn])
                        DVE.op(lambda: nc.vector.reciprocal(out=dn[64:128, :], in_=dn[64:128, :]), reads=[dn], writes=[dn])
                        DVE.op(lambda: nc.vector.tensor_tensor(out=WO[:, 2 * half:2 * half + 2, 0:LC],
                                                               in0=ps_o[0:64, :].rearrange("p (g q) -> p g q", g=2),
                                                               in1=dn[64:128, :].rearrange("p (g q) -> p g q", g=2), op=ALU.mult),
                               reads=[ps_o, dn], writes=[WO])

                LAG = 1
                for idx in range(len(items) + LAG):
                    if idx < len(items):
                        stage1(idx)
                    if idx - LAG >= 0:
                        stage2(idx - LAG)
                c_lo = 0 if with_ctx_q else LC
                dst = brT[2, :, c_lo:NT].rearrange("(g p) t -> p g t", p=64)
                SP.dma(dst, WO[:, :, c_lo:NT], reads=[WO], writes=[k.dbuf("brT", 2)], sbuf=WO)

        def fourier_phase(with_ctx):
            fr = pT[R_FU:R_FU + 256, :].rearrange("(j p) t -> p j t", p=128)
            fo = brT[1].rearrange("(j p) t -> p j t", p=128)
            with k.phase():
                cs32 = k.sb("cs32", [128, 256], F32)
                csb = k.sb("csb", [128, 256], BF16)
                fu = k.sb("fu", [128, 2, NT], BF16)
                ucs = k.sb("ucs", [128, 34, 2, 256], BF16)
                tb = [[k.sb(f"tb{i}{j}", [128, 32, 256], BF16) for j in range(2)] for i in range(2)]
                tb2 = [k.sb(f"tb2{j}", [128, 2, 256], BF16) for j in range(2)]
                fo_sb = [k.sb(f"fo{i}", [128, 2, 256], BF16) for i in range(2)]
                SP.dma(cs32[:], cs64, writes=[cs32], sbuf=cs32)
                DVE.op(lambda: nc.vector.tensor_copy(out=csb[:], in_=cs32[:]), reads=[cs32], writes=[csb])
                SP.dma(fu[:], fr, reads=[k.dbuf("pT", "all")], writes=[fu], sbuf=fu)
                ei = 0
                for c in range(34):
                    if c < 2 and not with_ctx:
                        continue
                    for half in range(1):
                        ps = PS[c % 4]
                        for jj in range(2):
                            j = jj
                            mm(ps[:, jj * 256:(jj + 1) * 256], fu[:, j, c * 128:(c + 1) * 128], csb[:], True, True, [fu, csb], [ps])
                        src = ps[:, :].rearrange("p (jj s f) -> p s jj f", jj=2, s=2)
                        dstv = ucs[:, c, :, :].rearrange("p s (jj f) -> p s jj f", jj=2)
                        if ei % 2 == 0:
                            ACT.op(lambda: nc.scalar.copy(out=dstv, in_=src), reads=[ps], writes=[ucs])
                        else:
                            DVE.op(lambda: nc.vector.tensor_copy(out=dstv, in_=src), reads=[ps], writes=[ucs])
                        ei += 1
                pi = 0
                for kb in range(16):
                    tC, tS = tb[kb % 2]
                    SP.dma(tC[:], dftC[kb], writes=[tC], sbuf=tC)
                    SP.dma(tS[:], dftS[kb], writes=[tS], sbuf=tS)
                    fb = fo_sb[kb % 2]
                    for j in range(2):
                        ps = PS[4 + pi % 4]
                        pi += 1
                        for c in range(32):
                            mm(ps[:, 0:256], ucs[:, 2 + c, 0, j * 128:(j + 1) * 128], tC[:, c, :], c == 0, False, [ucs, tC], [ps])
                            mm(ps[:, 0:256], ucs[:, 2 + c, 1, j * 128:(j + 1) * 128], tS[:, c, :], False, c == 31, [ucs, tS], [ps])
                        if j % 2 == 0:
                            ACT.op(lambda: nc.scalar.copy(out=fb[:, j, :], in_=ps[:, 0:256]), reads=[ps], writes=[fb])
                        else:
                            DVE.op(lambda: nc.vector.tensor_copy(out=fb[:, j, :], in_=ps[:, 0:256]), reads=[ps], writes=[fb])
                    SP.dma(fo[:, :, LC + kb * 256:LC + (kb + 1) * 256], fb[:], reads=[fb], writes=[k.dbuf("brT", 1)], sbuf=fb)
                if with_ctx:
                    tC, tS = tb2
                    SP.dma(tC[:], dftC2, writes=[tC], sbuf=tC)
                    SP.dma(tS[:], dftS2, writes=[tS], sbuf=tS)
                    fb = fo_sb[0]
                    for j in range(2):
                        ps = PS[4 + pi % 4]
                        pi += 1
                        for c in range(2):
                            mm(ps[:, 0:256], ucs[:, c, 0, j * 128:(j + 1) * 128], tC[:, c, :], c == 0, False, [ucs, tC], [ps])
                            mm(ps[:, 0:256], ucs[:, c, 1, j * 128:(j + 1) * 128], tS[:, c, :], False, c == 1, [ucs, tS], [ps])
                        DVE.op(lambda: nc.vector.tensor_copy(out=fb[:, j, :], in_=ps[:, 0:256]), reads=[ps], writes=[fb])
                    SP.dma(fo[:, :, 0:LC], fb[:], reads=[fb], writes=[k.dbuf("brT", 1)], sbuf=fb)

        def merge_alloc_load(l):
            wbr_r = w_br[l].rearrange("i (c p) n -> p i c n", p=128)
            wo_r = w_out[l].rearrange("(k p) n -> p k n", p=128)
            wbr = k.sb("wbr", [128, 3, 4, 1024], BF16)
            wo = k.sb("wo", [128, 8, 1024], BF16)
            for i in range(3):
                POOL.dma(wbr[:, i, :, :], wbr_r[:, i, :, :], writes=[wbr], sbuf=wbr)
            POOL.dma(wo[:], wo_r, writes=[wo], sbuf=wo)
            return wbr, wo

        def merge_phase(l, tiles, pre=None):
            bg = brTg.rearrange("(i cc r p) t -> p r i cc t", p=128, cc=2, r=2)
            gr = glT.rearrange("(j p) t -> p j t", p=128)
            hr = hT.rearrange("(k p) t -> p k t", p=128)
            wbr_r = w_br[l].rearrange("i (c p) n -> p i c n", p=128)
            wo_r = w_out[l].rearrange("(k p) n -> p k n", p=128)
            gate = Gt[l][1]
            with k.phase():
                wbr, wo = pre
                bin_ = [k.sb(f"bin{i}", [128, 3, 4, 512], BF16) for i in range(2)]
                binB = [k.sb(f"binB{i}", [128, 3, 4, 512], BF16) for i in range(2)]
                gin = [k.sb(f"gin{i}", [128, 24, 512], BF16) for i in range(2)]
                hin = [k.sb(f"hin{i}", [128, 8, 512], F32) for i in range(2)]
                sg = [k.sb(f"sg{i}", [128, 512], F32) for i in range(3)]
                m32 = [k.sb(f"m32{i}", [128, 512], F32) for i in range(2)]
                tt = [k.sb(f"tt{i}", [128, 512], F32) for i in range(2)]
                mT = [k.sb(f"mT{i}", [128, 8, 512], BF16) for i in range(2)]
                pi = 0
                si = 0
                def _mload(ti):
                    c0, W, cls = tiles[ti]
                    b = ti % 2
                    if cls == 1:
                        for i in range(3):
                            for r in range(2):
                                SP.dma(bin_[b][:, i, 2 * r:2 * r + 2, :W], bg[:, r, i, :, 0:W], reads=[k.dbuf("brTg", 0)], writes=[bin_[b]], sbuf=bin_[b])
                    else:
                        ca = c0
                        cb = c0 + TH
                        for i in range(3):
                            for r in range(2):
                                SP.dma(bin_[b][:, i, 2 * r:2 * r + 2, :W], bg[:, r, i, :, ca:ca + W], reads=[k.dbuf("brTg", 0)], writes=[bin_[b]], sbuf=bin_[b])
                                SP.dma(binB[b][:, i, 2 * r:2 * r + 2, :W], bg[:, r, i, :, cb:cb + W], reads=[k.dbuf("brTg", 0)], writes=[binB[b]], sbuf=binB[b])
                    SP.dma(gin[b][:, :, :W], gr[:, :, c0:c0 + W], reads=[k.dbuf("glT", "all")], writes=[gin[b]], sbuf=gin[b])
                    SP.dma(hin[b][:, :, :W], hr[:, :, c0:c0 + W], reads=[k.dbuf("hT", c0)], writes=[hin[b]], sbuf=hin[b])

                _mload(0)
                for ti, (c0, W, cls) in enumerate(tiles):
                    b = ti % 2
                    if ti + 1 < len(tiles):
                        _mload(ti + 1)
                    if cls == 1:
                        pass
                    else:
                        DVE.op(lambda: nc.vector.tensor_scalar(out=bin_[b][:], in0=bin_[b][:], scalar1=oh[:, 0:1], scalar2=None, op0=ALU.mult),
                               reads=[bin_[b], oh], writes=[bin_[b]])
                        DVE.op(lambda: nc.vector.scalar_tensor_tensor(out=bin_[b][:], in0=binB[b][:], scalar=oh[:, 1:2], in1=bin_[b][:], op0=ALU.mult, op1=ALU.add),
                               reads=[bin_[b], binB[b], oh], writes=[bin_[b]])
                    for n in range(8):
                        mb = m32[n % 2]
                        for i in range(3):
                            ps = PS[pi % 5]
                            pi += 1
                            sgb = sg[si % 3]
                            tb_ = tt[si % 2]
                            si += 1
                            for c in range(4):
                                mm(ps[:, :W], wbr[:, i, c, n * 128:(n + 1) * 128], bin_[b][:, i, c, :W], c == 0, c == 3, [wbr, bin_[b]], [ps])
                            ACT.op(lambda: nc.scalar.activation(out=sgb[:, :W], in_=gin[b][:, i * 8 + n, :W], func=AF.Sigmoid), reads=[gin[b]], writes=[sgb])
                            if i == 0:
                                DVE.op(lambda: nc.vector.tensor_tensor(out=mb[:, :W], in0=sgb[:, :W], in1=ps[:, :W], op=ALU.mult), reads=[sgb, ps], writes=[mb])
                            else:
                                DVE.op(lambda: nc.vector.tensor_tensor(out=tb_[:, :W], in0=sgb[:, :W], in1=ps[:, :W], op=ALU.mult), reads=[sgb, ps], writes=[tb_])
                                if i == 1:
                                    DVE.op(lambda: nc.vector.tensor_tensor(out=mb[:, :W], in0=mb[:, :W], in1=tb_[:, :W], op=ALU.add), reads=[mb, tb_], writes=[mb])
                                else:
                                    DVE.op(lambda: nc.vector.tensor_tensor(out=mT[b][:, n, :W], in0=mb[:, :W], in1=tb_[:, :W], op=ALU.add), reads=[mb, tb_], writes=[mT[b]])
                    for n2 in range(8):
                        ps = PS[5 + n2 % 3]
                        for kk in range(8):
                            mm(ps[:, :W], wo[:, kk, n2 * 128:(n2 + 1) * 128], mT[b][:, kk, :W], kk == 0, kk == 7, [wo, mT[b]], [ps])
                        DVE.op(lambda: nc.vector.scalar_tensor_tensor(out=hin[b][:, n2, :W], in0=ps[:, :W], scalar=gate[:, n2, cls:cls + 1],
                                                                      in1=hin[b][:, n2, :W], op0=ALU.mult, op1=ALU.add),
                               reads=[ps, hin[b], gate], writes=[hin[b]])
                    SP.dma(hr[:, :, c0:c0 + W], hin[b][:, :, :W], reads=[hin[b]], writes=[k.dbuf("hT", c0)], sbuf=hin[b])

        def alias_all(name, keys):
            allb = k.dbuf(name, "all")
            for kk_ in keys:
                fb = k.dram.get((name, kk_))
                if fb is not None:
                    for (s, v, e) in fb.w.values():
                        Eng._reg(allb.w, s, v, e)

        steps = []

        def run():
            for l in range(DEPTH):
                last = l == DEPTH - 1
                tiles_all = TILES
                tiles_lat = TILES[1:]
                hsrc, hname = (h0T, "h0T") if l == 0 else (hT, "hT")
                with k.phase():
                    w2b = k.sb("w2", [128, 22, 1024], BF16)
                    with k.phase():
                        pre13 = w13_alloc_load(ffn_w13[0][l])
                        if l == 0:
                            norm_phase(hsrc, hname, Gs[l][0], lambda kk, cls: Gs[l][0][:, kk, cls:cls + 1], modT[l],
                                       lambda kk, cls: shiftv(l, 0)[:, kk, cls:cls + 1], tiles_all, uT, "uT", lean=True)
                        w2_load(ffn_w2[0][l], w2b)
                        w13_phase(ffn_w13[0][l], tiles_all, pre=pre13)
                    post2 = dict(Gbuf=Gs[l][1], Gap=lambda kk, cls: Gs[l][1][:, kk, cls:cls + 1], Shbuf=modT[l],
                                 Shap=lambda kk, cls: shiftv(l, 1)[:, kk, cls:cls + 1], tiles=tiles_all, dst=uT, final=False, send=uTs)
                    w2_phase(ffn_w2[0][l], hsrc, hname, Gt[l][0], tiles_all, pre=w2b, post=post2)
                yield f"ffn1_{l}"
                with k.phase():
                    pre_gl, pre_own = win_alloc_load(l)
                    all_gather([(uTs[q * 512:(q + 1) * 512, :], uTg[q * 1024:(q + 1) * 1024, :]) for q in range(2)])
                    win_gl_phase(l, tiles_lat if last else tiles_all, pre=pre_gl)
                    win_own_phase(l, pre=pre_own)
                yield f"win_{l}"
                na_phase(l, not last)
                yield f"na_{l}"
                fourier_phase(not last)
                yield f"fn_{l}"
                wa_phase(l, not last)
                yield f"wa_{l}"
                b2 = brT.rearrange("i c t -> (i c) t")
                mscope = k.phase()
                mscope.__enter__()
                pre_m = merge_alloc_load(l)
                all_gather([(b2[q * 128:(q + 1) * 128, :], brTg[q * 256:(q + 1) * 256, :]) for q in range(6)])
                tl = tiles_lat if last else tiles_all
                merge_phase(l, tl, pre=pre_m)
                mscope.__exit__(None, None, None)
                yield f"merge_{l}"
                with k.phase():
                    w2b = k.sb("w2", [128, 22, 1024], BF16)
                    with k.phase():
                        pre13 = w13_alloc_load(ffn_w13[1][l])
                        norm_phase(hT, "hT", Gs[l][2], lambda kk, cls: Gs[l][2][:, kk, cls:cls + 1], modT[l],
                                   lambda kk, cls: shiftv(l, 2)[:, kk, cls:cls + 1], tl, uT, "uT", lean=True)
                        w2_load(ffn_w2[1][l], w2b)
                        w13_phase(ffn_w13[1][l], tl, pre=pre13)
                    if last:
                        postn = dict(Gbuf=gv, Gap=lambda kk, cls: gv[:, 6, kk, 0:1], Shbuf=None, Shap=None, tiles=TILES[1:], dst=outT, final=True, send=None)
                    else:
                        postn = dict(Gbuf=Gs[l + 1][0], Gap=lambda kk, cls: Gs[l + 1][0][:, kk, cls:cls + 1], Shbuf=modT[l + 1],
                                     Shap=lambda kk, cls: shiftv(l + 1, 0)[:, kk, cls:cls + 1], tiles=tiles_all, dst=uT, final=False, send=None)
                    w2_phase(ffn_w2[1][l], hT, "hT", Gt[l][2], tl, pre=w2b, post=postn)
                yield f"ffn2_{l}"
            yield "final"

        for name in run():
            if stop_after is not None and name == stop_after:
                break
        k.barrier()
    return nc


def _constants():
    bf = ml_dtypes.bfloat16
    t = np.arange(T)
    row = (t // 64).astype(np.float64)
    col = (t % 64).astype(np.float64)
    inv = 10000.0 ** (-np.arange(16, dtype=np.float64) / 16)
    ang = np.concatenate([row[:, None] * inv, col[:, None] * inv], axis=-1)
    c = np.cos(ang).T
    s = np.sin(ang).T
    ropeC = np.concatenate([c, c], axis=0).astype(np.float32)
    ropeS = np.concatenate([-s, s], axis=0).astype(np.float32)
    cc = np.arange(64)
    a = 2 * np.pi * np.outer(cc, cc) / 64
    C64, S64 = np.cos(a), np.sin(a)
    z = np.zeros((64, 64))
    Cb = np.block([[C64, z], [z, C64]])
    Sb = np.block([[S64, z], [z, S64]])
    cs64 = np.concatenate([Cb, Sb], axis=1).astype(np.float32)

    def pos_tables(N, norm):
        n = np.arange(N)
        m = (np.outer(n, n) % N).astype(np.float64)
        a = 2 * np.pi * m / N
        return (np.cos(a) * norm).astype(np.float32), (-np.sin(a) * norm).astype(np.float32)

    Cn, Sn = pos_tables(T, 1.0 / 512)
    def lay(M):
        return np.ascontiguousarray(M.reshape(32, 128, 16, 256).transpose(2, 1, 0, 3)).astype(bf)
    dftC, dftS = lay(Cn), lay(Sn)
    C2, S2 = pos_tables(LC, 1.0 / 128)
    def lay2(M):
        return np.ascontiguousarray(M.reshape(2, 128, 256).transpose(1, 0, 2)).astype(bf)
    dftC2, dftS2 = lay2(C2), lay2(S2)
    p = np.arange(128)[:, None]
    q = np.arange(128)[None, :]
    m0 = (q <= p).astype(np.float32)
    m1 = (p <= q).astype(np.float32)
    trimask = np.stack([np.tile(m0, (1, 4)), np.tile(m1, (1, 4))], axis=1).astype(np.float32)
    pk = np.arange(128)
    kc = pk % 64
    qc = np.arange(64)
    win_start = np.clip(qc - 8, 0, 48)
    valid = (kc[:, None] >= win_start[None, :]) & (kc[:, None] < win_start[None, :] + 16)
    na_mask = np.broadcast_to(valid[:, None, None, :], (128, 8, 4, 64)).reshape(128, 8, 256).astype(np.float32)
    na_mask = np.ascontiguousarray(na_mask)
    kr = (2 * np.arange(4)[None, :] + (pk // 64)[:, None])
    dr_idx = kr[:, None, :] - np.arange(8)[None, :, None] + 7
    dr_ok = (dr_idx >= 0) & (dr_idx <= 14)
    dr_idx = np.clip(dr_idx, 0, 14)
    dc_idx = np.clip(kc[:, None] - qc[None, :], -15, 15) + 15
    return dict(ropeC=ropeC, ropeS=ropeS, cs64=cs64, dftC=dftC, dftS=dftS, dftC2=dftC2, dftS2=dftS2,
                trimask=trimask, na_mask=na_mask), (dr_idx, dc_idx)


_CONST = None
_PROG = {}


def _prep_inputs(inp):
    global _CONST
    if _CONST is None:
        _CONST = _constants()
    const, (dr_idx, dc_idx) = _CONST
    f32 = np.float32
    x = np.asarray(inp["x"], f32)
    ctx = np.asarray(inp["ctx"], f32)
    c = np.asarray(inp["c"], f32)
    c_ctx = np.asarray(inp["c_ctx"], f32)

    def chunked(v):
        return np.ascontiguousarray(v.reshape(8, 128).T)

    b_ada = np.asarray(inp["b_ada"], f32)
    b_adaT = np.ascontiguousarray(np.repeat(b_ada.reshape(DEPTH, 72, 128).transpose(0, 2, 1)[..., None], 2, axis=-1))
    gl = [inp["g_ffn1"][0], inp["g_mix"][0], inp["g_ffn2"][0], inp["g_ffn1"][1], inp["g_mix"][1], inp["g_ffn2"][1], inp["g_final"]]
    gvec = np.stack([chunked(np.asarray(g, f32)) for g in gl], axis=1)
    gvec = np.ascontiguousarray(np.repeat(gvec[..., None], 2, axis=-1))
    nb = np.asarray(inp["na_bias"], f32)
    G = nb[:, :, dr_idx[:, :, :, None], dc_idx[:, None, None, :]]
    na_biasG = np.ascontiguousarray(G.transpose(0, 2, 3, 1, 4, 5).reshape(DEPTH, 128, 8, 8, 256))
    sk = np.asarray(inp["wa_sink"], f32)
    w_in = np.asarray(inp["w_in"], f32)
    shared = dict(const)
    shared.update(
        w_ada=np.asarray(inp["w_ada"], f32), b_adaT=b_adaT, gvec=gvec,
        ffn1_w13=np.asarray(inp["ffn1_w13"], f32), ffn2_w13=np.asarray(inp["ffn2_w13"], f32),
        ffn1_w2=np.asarray(inp["ffn1_w2"], f32), ffn2_w2=np.asarray(inp["ffn2_w2"], f32),
        w_in_gl=np.ascontiguousarray(w_in[:, :, 2816:5888]),
        w_br=np.asarray(inp["w_br"], f32), w_out=np.asarray(inp["w_out"], f32),
    )
    per_half = []
    for hf in range(2):
        cols = np.concatenate([np.arange(0 + hf * 256, 0 + hf * 256 + 256), np.arange(512 + hf * 256, 512 + hf * 256 + 256),
                               np.arange(1024 + hf * 256, 1024 + hf * 256 + 256), np.arange(1536 + hf * 256, 1536 + hf * 256 + 256),
                               np.arange(2048 + hf * 256, 2048 + hf * 256 + 256), np.arange(2560 + hf * 64, 2560 + hf * 64 + 64),
                               np.arange(2688 + hf * 64, 2688 + hf * 64 + 64)])
        oh = np.zeros((128, 2), f32)
        oh[:, hf] = 1.0
        per_half.append(dict(
            w_in_own=np.ascontiguousarray(w_in[:, :, cols]),
            na_biasG=np.ascontiguousarray(na_biasG[:, :, :, 4 * hf:4 * hf + 4, :]),
            sinkG=np.ascontiguousarray(np.broadcast_to(sk[:, 4 * hf:4 * hf + 4].reshape(DEPTH, 1, 4, 1), (DEPTH, 128, 4, 128)).reshape(DEPTH, 128, 512)),
            oh=oh,
        ))
    maps = []
    for core in range(NCORES):
        b, hf = core // 2, core % 2
        m = dict(shared)
        m.update(per_half[hf])
        m["h0T"] = np.ascontiguousarray(np.concatenate([ctx[b].T, x[b, hf * TH:(hf + 1) * TH].T], axis=1))
        m["ccT"] = np.ascontiguousarray(np.stack([chunked(c[b]), chunked(c_ctx)], axis=-1))
        maps.append(m)
    return maps


def kernel(**inputs):
    maps = _prep_inputs(inputs)
    if "main" not in _PROG:
        _PROG["main"] = build_program()
    nc = _PROG["main"]
    res = run_bass_kernel_spmd(nc, maps, core_ids=list(range(NCORES)))
    out = np.empty((NB, T, D), np.float32)
    for core in range(NCORES):
        b, hf = core // 2, core % 2
        out[b, hf * TH:(hf + 1) * TH, :] = np.asarray(res.results[core]["outT"]).T
    return out
```
